# Optimizing a Trainium2 kernel written in Bass

```python
import jax, jax.numpy as jnp
from jax import lax
import numpy as np

D_MODEL = 1024
BATCH = 2
SEQ = 16384
DEPTH = 2

MIX_WIDTH = D_MODEL
POOL_WIDTH = D_MODEL // 4
POOL_WINDOWS = (2, 4, 8, 16)
POOL_GROUP = POOL_WIDTH // len(POOL_WINDOWS)
LRU_WIDTH = D_MODEL // 4
LRU_HEADS = 4
LRU_HEAD_DIM = LRU_WIDTH // LRU_HEADS
LRU_C = 8.0
CONV_WIDTH = 4
HEAD_DIM = 64
ATTN_WIDTH = MIX_WIDTH - POOL_WIDTH - LRU_WIDTH
N_HEADS = ATTN_WIDTH // HEAD_DIM
N_KV_HEADS = 2
GQA_GROUP = N_HEADS // N_KV_HEADS
KV_WIDTH = N_KV_HEADS * HEAD_DIM
N_BRANCH = 3
CMP_LEN = 32
CMP_STRIDE = 16
SEL_BLOCK = 64
SEL_TOPK = 16
WINDOW = 512
Q_BLOCK = 128
ROPE_THETA = 10000.0
D_FF = 2816
NORM_EPS = 1e-6
NEG_INF = -1e30
BIG_SCORE = 1e9
IN_SIZES = (POOL_WIDTH, LRU_WIDTH, LRU_WIDTH, ATTN_WIDTH,
            KV_WIDTH, KV_WIDTH, KV_WIDTH, KV_WIDTH, KV_WIDTH, KV_WIDTH,
            N_BRANCH * N_HEADS)
IN_COLS = sum(IN_SIZES)

kernel_name = "hymba_pool_rglru_nsa_macaron"


def rms_norm(x, g):
    x32 = x.astype(jnp.float32)
    y = x32 * lax.rsqrt(jnp.mean(x32 * x32, axis=-1, keepdims=True) + NORM_EPS)
    return (y * g.astype(jnp.float32)).astype(x.dtype)


def swiglu(x, w_gate, w_up, w_down):
    return (jax.nn.silu(x @ w_gate) * (x @ w_up)) @ w_down


def rope_tables(positions):
    inv = ROPE_THETA ** (-jnp.arange(0, HEAD_DIM, 2, dtype=jnp.float32) / HEAD_DIM)
    ang = positions.astype(jnp.float32)[..., None] * inv
    return jnp.cos(ang), jnp.sin(ang)


def apply_rope(x, cos, sin):
    x32 = x.astype(jnp.float32)
    x1, x2 = jnp.split(x32, 2, axis=-1)
    c = cos[:, :, None, :]
    s = sin[:, :, None, :]
    return jnp.concatenate([x1 * c - x2 * s, x2 * c + x1 * s], axis=-1).astype(x.dtype)


def pool_mixer(xp, pool_w, pool_scale):
    B, S, _ = xp.shape
    xg = xp.astype(jnp.float32).reshape(B, S, len(POOL_WINDOWS), POOL_GROUP)
    c = jnp.concatenate([jnp.zeros((B, 1) + xg.shape[2:], jnp.float32),
                         jnp.cumsum(xg, axis=1)], axis=1)
    t = jnp.arange(S)
    means = []
    for g, w in enumerate(POOL_WINDOWS):
        cg = c[:, :, g]
        lo = jnp.concatenate([jnp.zeros((B, w - 1, POOL_GROUP), jnp.float32),
                              cg[:, :S + 1 - w]], axis=1)
        cnt = jnp.minimum(t + 1, w).astype(jnp.float32)
        means.append((cg[:, 1:] - lo) / cnt[None, :, None])
    pooled = jnp.stack(means, axis=2)
    y = jnp.einsum('bsgi,gij->bsgj', (pooled - xg).astype(xp.dtype), pool_w)
    return y.reshape(B, S, POOL_WIDTH) * pool_scale


def rglru_mixer(xl, gate, conv_w, conv_b, w_r, b_r, w_i, b_i, lam):
    B, S, C = xl.shape
    xc = lax.conv_general_dilated(xl, conv_w[:, None, :], window_strides=(1,),
                                  padding=[(CONV_WIDTH - 1, 0)],
                                  dimension_numbers=('NWC', 'WIO', 'NWC'),
                                  feature_group_count=C) + conv_b
    xh = xc.reshape(B, S, LRU_HEADS, LRU_HEAD_DIM)
    r = jax.nn.sigmoid(jnp.einsum('bshi,hij->bshj', xh, w_r).reshape(B, S, C) + b_r)
    i = jax.nn.sigmoid(jnp.einsum('bshi,hij->bshj', xh, w_i).reshape(B, S, C) + b_i)
    log_a = -LRU_C * r.astype(jnp.float32) * jax.nn.softplus(-lam.astype(jnp.float32))
    a = jnp.exp(log_a)
    mult = jnp.sqrt(-jnp.expm1(2.0 * log_a))
    b = mult * (i * xc).astype(jnp.float32)

    def combine(left, right):
        a1, b1 = left
        a2, b2 = right
        return a1 * a2, a2 * b1 + b2

    _, h = lax.associative_scan(combine, (a, b), axis=1)
    return h.astype(xl.dtype) * jax.nn.gelu(gate)


def masked_softmax(s, mask, scale):
    s = jnp.where(mask, s.astype(jnp.float32) * scale, NEG_INF)
    p = jax.nn.softmax(s, axis=-1)
    return jnp.where(mask, p, 0.0)


def nsa_mixer(q, kc, vc, ks, vs, kw, vw, g, cos, sin, cmp_w_k, cmp_w_v, cmp_pe):
    B, S, _ = q.shape
    dt = q.dtype
    kv_heads = lambda t: t.reshape(B, S, N_KV_HEADS, HEAD_DIM)
    q = apply_rope(q.reshape(B, S, N_HEADS, HEAD_DIM), cos, sin)
    kc = apply_rope(kv_heads(kc), cos, sin)
    ks = apply_rope(kv_heads(ks), cos, sin)
    kw = apply_rope(kv_heads(kw), cos, sin)
    vc, vs, vw = kv_heads(vc), kv_heads(vs), kv_heads(vw)
    gates = jax.nn.sigmoid(g.astype(jnp.float32)).astype(dt).reshape(
        B, S, N_KV_HEADS, GQA_GROUP, N_BRANCH)
    qg = q.reshape(B, S, N_KV_HEADS, GQA_GROUP, HEAD_DIM)
    scale = HEAD_DIM ** -0.5

    n_cmp = (S - CMP_LEN) // CMP_STRIDE + 1
    cidx = jnp.arange(n_cmp)[:, None] * CMP_STRIDE + jnp.arange(CMP_LEN)[None, :]
    pe = cmp_pe[None, None, :, None, :]
    k_cmp = jnp.einsum('bnlhd,lde->bnhe', kc[:, cidx] + pe, cmp_w_k)
    v_cmp = jnp.einsum('bnlhd,lde->bnhe', vc[:, cidx] + pe, cmp_w_v)
    cmp_start = jnp.arange(n_cmp) * CMP_STRIDE
    cmp_end = cmp_start + CMP_LEN - 1

    n_sel = S // SEL_BLOCK
    n_top = min(SEL_TOPK, n_sel)
    sel_start = jnp.arange(n_sel) * SEL_BLOCK
    overlap = ((cmp_start[:, None] < sel_start[None, :] + SEL_BLOCK) &
               (cmp_start[:, None] + CMP_LEN > sel_start[None, :])).astype(jnp.float32)
    ks_blk = ks.reshape(B, n_sel, SEL_BLOCK, N_KV_HEADS, HEAD_DIM).transpose(0, 3, 1, 2, 4)
    vs_blk = vs.reshape(B, n_sel, SEL_BLOCK, N_KV_HEADS, HEAD_DIM).transpose(0, 3, 1, 2, 4)
    gather = jax.vmap(jax.vmap(lambda kb, ix: kb[ix]))

    pad = jnp.zeros((B, WINDOW, N_KV_HEADS, HEAD_DIM), dt)
    kw_pad = jnp.concatenate([pad, kw], axis=1)
    vw_pad = jnp.concatenate([pad, vw], axis=1)
    j_sel = jnp.arange(n_sel)

    def query_block(qb):
        start = qb * Q_BLOCK
        tq = start + jnp.arange(Q_BLOCK)
        qblk = lax.dynamic_slice_in_dim(qg, start, Q_BLOCK, axis=1)
        gblk = lax.dynamic_slice_in_dim(gates, start, Q_BLOCK, axis=1)

        s_c = jnp.einsum('bqhgd,bnhd->bhgqn', qblk, k_cmp)
        p_c = masked_softmax(s_c, cmp_end[None, :] <= tq[:, None], scale)
        o_c = jnp.einsum('bhgqn,bnhd->bqhgd', p_c.astype(dt), v_cmp)

        imp = jnp.einsum('bhqn,nj->bhqj', p_c.sum(axis=2), overlap)
        cur = tq // SEL_BLOCK
        valid = j_sel[None, :] <= cur[:, None]
        forced = ((j_sel[None, :] == 0) | (j_sel[None, :] == cur[:, None]) |
                  (j_sel[None, :] == cur[:, None] - 1))
        score = jnp.where(valid, jnp.where(forced, BIG_SCORE, imp), -BIG_SCORE)
        _, idx = lax.top_k(score, n_top)
        k_sel = gather(ks_blk, idx).reshape(B, N_KV_HEADS, Q_BLOCK, n_top * SEL_BLOCK, HEAD_DIM)
        v_sel = gather(vs_blk, idx).reshape(B, N_KV_HEADS, Q_BLOCK, n_top * SEL_BLOCK, HEAD_DIM)
        kpos = (idx[..., None] * SEL_BLOCK + jnp.arange(SEL_BLOCK)).reshape(
            B, N_KV_HEADS, Q_BLOCK, n_top * SEL_BLOCK)
        mask_s = (kpos <= tq[None, None, :, None])[:, :, None]
        s_s = jnp.einsum('bqhgd,bhqkd->bhgqk', qblk, k_sel)
        p_s = masked_softmax(s_s, mask_s, scale)
        o_s = jnp.einsum('bhgqk,bhqkd->bqhgd', p_s.astype(dt), v_sel)

        kwb = lax.dynamic_slice_in_dim(kw_pad, start, Q_BLOCK + WINDOW, axis=1)
        vwb = lax.dynamic_slice_in_dim(vw_pad, start, Q_BLOCK + WINDOW, axis=1)
        kp = start - WINDOW + jnp.arange(Q_BLOCK + WINDOW)
        diff = tq[:, None] - kp[None, :]
        mask_w = (diff >= 0) & (diff < WINDOW) & (kp[None, :] >= 0)
        s_w = jnp.einsum('bqhgd,bkhd->bhgqk', qblk, kwb)
        p_w = masked_softmax(s_w, mask_w, scale)
        o_w = jnp.einsum('bhgqk,bkhd->bqhgd', p_w.astype(dt), vwb)

        o = gblk[..., 0:1] * o_c + gblk[..., 1:2] * o_s + gblk[..., 2:3] * o_w
        return o.reshape(B, Q_BLOCK, ATTN_WIDTH)

    out = lax.map(query_block, jnp.arange(S // Q_BLOCK))
    return out.transpose(1, 0, 2, 3).reshape(B, S, ATTN_WIDTH)


def token_mixer(h, cos, sin, w_in, w_out, pool_w, pool_scale, conv_w, conv_b,
                lru_w_r, lru_b_r, lru_w_i, lru_b_i, lru_lambda, cmp_w_k, cmp_w_v, cmp_pe):
    proj = h @ w_in
    bounds = []
    acc = 0
    for size in IN_SIZES[:-1]:
        acc += size
        bounds.append(acc)
    xp, xl, gl, q, kc, vc, ks, vs, kw, vw, g = jnp.split(proj, bounds, axis=-1)
    y_pool = pool_mixer(xp, pool_w, pool_scale)
    y_lru = rglru_mixer(xl, gl, conv_w, conv_b, lru_w_r, lru_b_r, lru_w_i, lru_b_i, lru_lambda)
    y_attn = nsa_mixer(q, kc, vc, ks, vs, kw, vw, g, cos, sin, cmp_w_k, cmp_w_v, cmp_pe)
    return jnp.concatenate([y_pool, y_lru, y_attn], axis=-1) @ w_out


def setup_inputs(seed: int = 0) -> dict:
    key = jax.random.key(seed)
    ks = jax.random.split(key, 32)
    f32 = jnp.float32
    nrm = lambda k, shape, s: jax.random.normal(k, shape, f32) * s
    gain = lambda k: 1.0 + nrm(k, (DEPTH, D_MODEL), 0.05)
    x = jax.random.normal(ks[0], (BATCH, SEQ, D_MODEL), f32)
    offset = jax.random.randint(ks[1], (BATCH, 1), 0, 1024, dtype=jnp.int32)
    positions = (offset + jnp.arange(SEQ, dtype=jnp.int32)[None, :]).astype(jnp.int32)
    u = jax.random.uniform(ks[2], (DEPTH, LRU_WIDTH), f32, minval=0.9, maxval=0.999)
    return {
        "x": x,
        "positions": positions,
        "ffn1_pre_g": gain(ks[3]),
        "ffn1_post_g": gain(ks[4]),
        "ffn1_w_gate": nrm(ks[5], (DEPTH, D_MODEL, D_FF), D_MODEL ** -0.5),
        "ffn1_w_up": nrm(ks[6], (DEPTH, D_MODEL, D_FF), D_MODEL ** -0.5),
        "ffn1_w_down": nrm(ks[7], (DEPTH, D_FF, D_MODEL), D_FF ** -0.5),
        "mix_pre_g": gain(ks[8]),
        "mix_post_g": gain(ks[9]),
        "w_in": nrm(ks[10], (DEPTH, D_MODEL, IN_COLS), D_MODEL ** -0.5),
        "w_out": nrm(ks[11], (DEPTH, MIX_WIDTH, D_MODEL), MIX_WIDTH ** -0.5),
        "pool_w": nrm(ks[12], (DEPTH, len(POOL_WINDOWS), POOL_GROUP, POOL_GROUP), POOL_GROUP ** -0.5),
        "pool_scale": 1.0 + nrm(ks[13], (DEPTH, POOL_WIDTH), 0.1),
        "conv_w": nrm(ks[14], (DEPTH, CONV_WIDTH, LRU_WIDTH), CONV_WIDTH ** -0.5),
        "conv_b": nrm(ks[15], (DEPTH, LRU_WIDTH), 0.01),
        "lru_w_r": nrm(ks[16], (DEPTH, LRU_HEADS, LRU_HEAD_DIM, LRU_HEAD_DIM), LRU_HEAD_DIM ** -0.5),
        "lru_b_r": nrm(ks[17], (DEPTH, LRU_WIDTH), 0.01),
        "lru_w_i": nrm(ks[18], (DEPTH, LRU_HEADS, LRU_HEAD_DIM, LRU_HEAD_DIM), LRU_HEAD_DIM ** -0.5),
        "lru_b_i": nrm(ks[19], (DEPTH, LRU_WIDTH), 0.01),
        "lru_lambda": jnp.log(u) - jnp.log1p(-u),
        "cmp_w_k": nrm(ks[20], (DEPTH, CMP_LEN, HEAD_DIM, HEAD_DIM), (CMP_LEN * HEAD_DIM) ** -0.5),
        "cmp_w_v": nrm(ks[21], (DEPTH, CMP_LEN, HEAD_DIM, HEAD_DIM), (CMP_LEN * HEAD_DIM) ** -0.5),
        "cmp_pe": nrm(ks[22], (DEPTH, CMP_LEN, HEAD_DIM), 0.02),
        "ffn2_pre_g": gain(ks[23]),
        "ffn2_post_g": gain(ks[24]),
        "ffn2_w_gate": nrm(ks[25], (DEPTH, D_MODEL, D_FF), D_MODEL ** -0.5),
        "ffn2_w_up": nrm(ks[26], (DEPTH, D_MODEL, D_FF), D_MODEL ** -0.5),
        "ffn2_w_down": nrm(ks[27], (DEPTH, D_FF, D_MODEL), D_FF ** -0.5),
    }


def reference(x, positions, ffn1_pre_g, ffn1_post_g, ffn1_w_gate, ffn1_w_up, ffn1_w_down,
              mix_pre_g, mix_post_g, w_in, w_out, pool_w, pool_scale, conv_w, conv_b,
              lru_w_r, lru_b_r, lru_w_i, lru_b_i, lru_lambda, cmp_w_k, cmp_w_v, cmp_pe,
              ffn2_pre_g, ffn2_post_g, ffn2_w_gate, ffn2_w_up, ffn2_w_down):
    cos, sin = rope_tables(positions)
    h = x
    for l in range(DEPTH):
        f1 = swiglu(rms_norm(h, ffn1_pre_g[l]), ffn1_w_gate[l], ffn1_w_up[l], ffn1_w_down[l])
        h = h + 0.5 * rms_norm(f1, ffn1_post_g[l])
        m = token_mixer(rms_norm(h, mix_pre_g[l]), cos, sin, w_in[l], w_out[l],
                        pool_w[l], pool_scale[l], conv_w[l], conv_b[l],
                        lru_w_r[l], lru_b_r[l], lru_w_i[l], lru_b_i[l], lru_lambda[l],
                        cmp_w_k[l], cmp_w_v[l], cmp_pe[l])
        h = h + rms_norm(m, mix_post_g[l])
        f2 = swiglu(rms_norm(h, ffn2_pre_g[l]), ffn2_w_gate[l], ffn2_w_up[l], ffn2_w_down[l])
        h = h + 0.5 * rms_norm(f2, ffn2_post_g[l])
    return h
```

```python
import numpy as np
import concourse.bass as bass
import concourse.mybir as mybir

F32 = mybir.dt.float32
BF16 = mybir.dt.bfloat16
I32 = mybir.dt.int32
ALU = mybir.AluOpType
AF = mybir.ActivationFunctionType
AX = mybir.AxisListType


class Tok:
    __slots__ = ("name", "last_w", "readers", "sem", "dma_cnt", "fslot")

    def __init__(self, name):
        self.name = name
        self.last_w = None
        self.readers = []
        self.sem = None
        self.dma_cnt = 0


class Op:
    __slots__ = ("eng", "fn", "deps", "signal", "idx", "is_dma", "dtok", "dval", "dslot")

    def __init__(self, eng, fn, is_dma=False):
        self.eng = eng
        self.fn = fn
        self.deps = []
        self.signal = False
        self.idx = None
        self.is_dma = is_dma
        self.dtok = None
        self.dval = None


ENGS = ("pe", "act", "dve", "pool", "sp")


class Prog:
    def __init__(self, nc):
        self.nc = nc
        self.ops = []
        self.nslots = 0
        self.slot_cnt = {}
        self.free_slots = []
        self.active = []

    def tok(self, name):
        return Tok(name)

    def toks(self, name, n):
        return [Tok(f"{name}{i}") for i in range(n)]

    def _track(self, op, reads, writes):
        deps = []
        for t in reads:
            if t.last_w is not None:
                deps.append(t.last_w)
        for t in writes:
            if t.last_w is not None:
                deps.append(t.last_w)
            deps.extend(t.readers)
        seen = set()
        for d in deps:
            if d is op or id(d) in seen:
                continue
            seen.add(id(d))
            if d.eng == "pe" and op.eng == "pe" and not d.is_dma and not op.is_dma:
                continue
            op.deps.append(d)
            if not d.is_dma:
                d.signal = True
        for t in writes:
            t.last_w = op
            t.readers = []
        for t in reads:
            if t not in writes:
                t.readers.append(op)

    def add(self, eng, fn, reads=(), writes=()):
        op = Op(eng, fn)
        self._track(op, list(reads), list(writes))
        self.ops.append(op)
        return op

    def dma(self, eng, out, in_, reads, write, **kw):
        op = Op(eng, lambda e: e.dma_start(out=out, in_=in_, **kw), is_dma=True)
        self._track(op, list(reads), [write])
        if write.sem is None:
            if self.free_slots:
                write.sem = self.free_slots.pop()
            else:
                write.sem = self.nslots
                self.nslots += 1
                self.slot_cnt[write.sem] = 0
            self.active.append(write)
        self.slot_cnt[write.sem] += 16
        op.dtok = write
        op.dslot = write.sem
        op.dval = self.slot_cnt[write.sem]
        write.dma_cnt = op.dval
        self.ops.append(op)
        return op

    def barrier(self):
        last = {}
        for op in self.ops:
            if op.fn is None:
                continue
            if op.is_dma:
                last[("d", op.dslot)] = op
            else:
                last[("e", op.eng)] = op
        deps = list(last.values())
        for t in self.active:
            self.free_slots.append(t.sem)
            t.fslot = t.sem
            t.sem = None
        self.active = []
        for e in ENGS:
            if e == 'pool':
                continue
            b = Op(e, None)
            for d in deps:
                b.deps.append(d)
                if not d.is_dma:
                    d.signal = True
            self.ops.append(b)

    def emit(self, final_toks=()):
        nc = self.nc
        from contextlib import ExitStack
        with ExitStack() as es:
            esem = {e: es.enter_context(nc.semaphore(f"s_{e}")) for e in ENGS}
            dsem = [es.enter_context(nc.semaphore(f"d_{i}")) for i in range(self.nslots)]
            cnt = {e: 0 for e in ENGS}
            waited = {e: {} for e in ENGS}
            streams = {e: [] for e in ENGS}
            nwaits = 0
            for op in self.ops:
                waits = []
                for d in op.deps:
                    if d.is_dma:
                        sem, val = dsem[d.dslot], d.dval
                    else:
                        sem, val = esem[d.eng], d.idx
                    key = id(sem)
                    if waited[op.eng].get(key, 0) >= val:
                        continue
                    waited[op.eng][key] = val
                    waits.append((sem, val))
                nwaits += len(waits)
                inc = None
                if op.is_dma:
                    inc = (dsem[op.dslot], 16)
                elif op.signal:
                    cnt[op.eng] += 1
                    op.idx = cnt[op.eng]
                    inc = (esem[op.eng], 1)
                streams[op.eng].append((waits, op.fn, inc))
            fin = []
            for t in final_toks:
                fin.append((dsem[t.sem if t.sem is not None else t.fslot], t.dma_cnt))
            self.stats = dict(n_ops=len(self.ops), n_waits=nwaits,
                              per_eng={e: len(streams[e]) for e in ENGS})

            def run(eng_obj, items, final=None):
                for waits, fn, inc in items:
                    for sem, val in waits:
                        eng_obj.wait_ge(sem, val)
                    if fn is None:
                        continue
                    ins = fn(eng_obj)
                    if inc is not None:
                        ins.then_inc(inc[0], inc[1])
                if final:
                    for sem, val in final:
                        eng_obj.wait_ge(sem, val)

            with nc.Block() as block:
                @block.tensor
                def _(e):
                    run(e, streams["pe"])

                @block.scalar
                def _(e):
                    run(e, streams["act"])

                @block.vector
                def _(e):
                    run(e, streams["dve"])

                @block.gpsimd
                def _(e):
                    run(e, streams["pool"])

                @block.sync
                def _(e):
                    run(e, streams["sp"], fin)


import math
import numpy as np
from contextlib import ExitStack

D = 1024
DFF = 2816
NF = DFF // 128
ST = 512
EPS = 1e-6
IN_COLS = 2072
PI = math.pi


class Ctx:
    def __init__(self, nc, P, es):
        self.nc, self.P, self.es = nc, P, es
        self.psb = []
        for i in range(8):
            t = es.enter_context(nc.psum_tensor(f"ps{i}", [128, 512], F32))
            self.psb.append((t, P.tok(f"ps{i}")))
        self.n = 0

    def sb(self, es, name, shape, dt):
        self.n += 1
        return es.enter_context(self.nc.sbuf_tensor(f"{name}_{self.n}", shape, dt))


def load_consts(cx, es, ident_d):
    P = cx.P
    cx.ident = cx.sb(es, "ident", [128, 128], BF16)
    cx.t_ident = P.tok("ident")
    P.dma("sp", cx.ident[:], ident_d, [], cx.t_ident)


def load_cast(cx, dst, t_dst, src, stg, t_stg, q, eng):
    P = cx.P
    P.dma(q, stg, src, [], t_stg)
    if eng == "act":
        P.add("act", lambda e: e.activation(out=dst, in_=stg, func=AF.Copy), [t_stg], [t_dst])
    else:
        P.add("dve", lambda e: e.tensor_copy(out=dst, in_=stg), [t_stg], [t_dst])


class NormT:
    def __init__(self, cx, es, g_dram, tag):
        P = cx.P
        self.cx = cx
        self.gcol = cx.sb(es, tag + "gcol", [128, 8], F32)
        self.t_gcol = P.tok("gcol")
        P.dma("sp", self.gcol[:], g_dram.rearrange("(k p) -> p k", p=128), [], self.t_gcol,
              allow_slow_non_contiguous=True)
        self.xs = [cx.sb(es, tag + "xs", [128, D], BF16) for _ in range(4)]
        self.t_xs = P.toks("xs", 4)
        self.junk = cx.sb(es, tag + "junk", [128, D], BF16)
        self.ss = cx.sb(es, tag + "ss", [128, 8], F32)
        self.rstd = cx.sb(es, tag + "rstd", [128, 8], F32)
        self.xnT = cx.sb(es, tag + "xnT", [128, 8, ST], BF16)
        self.t_junk, self.t_ss, self.t_rstd, self.t_xnT = P.tok("junk"), P.tok("ss"), P.tok("rstd"), P.tok("xnT")

    def run(self, xb, txb):
        cx, P = self.cx, self.cx.P
        ss, rstd, junk, xs, xnT = self.ss, self.rstd, self.junk, self.xs, self.xnT
        P.add("dve", lambda e: e.memset(ss[:], 0.0), [], [self.t_ss])
        for j in range(4):
            P.add("act", lambda e, j=j: e.activation(out=junk[:], in_=xb[:, j, :], func=AF.Square,
                                                     accum_out=ss[:, j:j + 1]), [txb], [self.t_junk, self.t_ss])
        P.add("dve", lambda e: e.tensor_scalar(out=rstd[:, 0:4], in0=ss[:, 0:4], scalar1=1.0 / D, scalar2=EPS,
                                               op0=ALU.mult, op1=ALU.add), [self.t_ss], [self.t_rstd])
        P.add("act", lambda e: e.activation(out=rstd[:, 4:8], in_=rstd[:, 0:4], func=AF.Sqrt),
              [self.t_rstd], [self.t_rstd])
        P.add("dve", lambda e: e.reciprocal(out=rstd[:, 0:4], in_=rstd[:, 4:8]), [self.t_rstd], [self.t_rstd])
        for j in range(4):
            P.add("dve", lambda e, j=j: e.tensor_scalar(out=xs[j][:], in0=xb[:, j, :], scalar1=rstd[:, j:j + 1],
                                                        scalar2=None, op0=ALU.mult),
                  [txb, self.t_rstd], [self.t_xs[j]])
        for k in range(8):
            pt, tpt = cx.psb[6 + (k % 2)]
            for j in range(4):
                P.add("pe", lambda e, j=j, k=k, pt=pt: e.matmul(pt[:, j * 128:(j + 1) * 128],
                                                               lhsT=xs[j][:, k * 128:(k + 1) * 128],
                                                               rhs=cx.ident[:], start=True, stop=True),
                      [self.t_xs[j], cx.t_ident], [tpt])
            P.add("act", lambda e, k=k, pt=pt: e.activation(out=xnT[:, k, :], in_=pt[:], func=AF.Copy,
                                                            scale=self.gcol[:, k:k + 1]),
                  [tpt, self.t_gcol], [self.t_xnT])
        return xnT, self.t_xnT


class Tail:
    def __init__(self, cx, es, g_dram, scale, tag):
        P = cx.P
        self.cx = cx
        self.gpb = cx.sb(es, tag + "gpb", [128, D], F32)
        self.t_gpb = P.tok("gpb")
        P.dma("sp", self.gpb[:], g_dram.partition_broadcast(128), [], self.t_gpb)
        if scale != 1.0:
            P.add("dve", lambda e: e.tensor_scalar(out=self.gpb[:], in0=self.gpb[:], scalar1=scale, scalar2=None,
                                                   op0=ALU.mult), [self.t_gpb], [self.t_gpb])
        self.y = [cx.sb(es, tag + "y", [128, D], F32) for _ in range(2)]
        self.t_y = P.toks("y", 2)
        self.ot = [cx.sb(es, tag + "ot", [128, D], F32) for _ in range(2)]
        self.t_ot = P.toks("ot", 2)
        self.junk = cx.sb(es, tag + "junk2", [128, D], BF16)
        self.ss2 = cx.sb(es, tag + "ss2", [128, 4], F32)
        self.rstd2 = cx.sb(es, tag + "rstd2", [128, 2], F32)
        self.t_junk, self.t_ss2, self.t_rstd2 = P.tok("junk2"), P.tok("ss2"), P.tok("rstd2")
        self.cnt = 0

    def run(self, lhs_fn, nk, w_fn, lhs_toks, t_w, resid, t_resid, out_ap, t_out):
        cx, P = self.cx, self.cx.P
        i = self.cnt
        self.cnt += 1
        yb, tyb = self.y[i % 2], self.t_y[i % 2]
        ob, tob = self.ot[i % 2], self.t_ot[i % 2]
        ss2, rstd2, junk, gpb = self.ss2, self.rstd2, self.junk, self.gpb
        for hf in range(2):
            py, tpy = cx.psb[4 + hf]
            for k in range(nk):
                P.add("pe", lambda e, k=k, hf=hf, py=py: e.matmul(py[:], lhsT=lhs_fn(k), rhs=w_fn(k, hf),
                                                                 start=(k == 0), stop=(k == nk - 1)),
                      list(lhs_toks) + [t_w], [tpy])
            P.add("dve", lambda e, py=py, hf=hf: e.tensor_copy(out=yb[:, hf * 512:(hf + 1) * 512], in_=py[:]),
                  [tpy], [tyb])
        P.add("dve", lambda e: e.memset(ss2[:], 0.0), [], [self.t_ss2])
        P.add("act", lambda e: e.activation(out=junk[:], in_=yb[:], func=AF.Square, accum_out=ss2[:, 0:1]),
              [tyb], [self.t_junk, self.t_ss2])
        P.add("dve", lambda e: e.tensor_scalar(out=ss2[:, 1:2], in0=ss2[:, 0:1], scalar1=1.0 / D, scalar2=EPS,
                                               op0=ALU.mult, op1=ALU.add), [self.t_ss2], [self.t_ss2])
        P.add("act", lambda e: e.activation(out=ss2[:, 2:3], in_=ss2[:, 1:2], func=AF.Sqrt),
              [self.t_ss2], [self.t_ss2])
        P.add("dve", lambda e: e.reciprocal(out=rstd2[:, 0:1], in_=ss2[:, 2:3]), [self.t_ss2], [self.t_rstd2])
        P.add("dve", lambda e: e.scalar_tensor_tensor(out=ob[:], in0=yb[:], scalar=rstd2[:, 0:1], in1=gpb[:],
                                                      op0=ALU.mult, op1=ALU.mult),
              [tyb, self.t_rstd2, self.t_gpb], [tob])
        P.add("dve", lambda e: e.tensor_tensor(out=ob[:], in0=ob[:], in1=resid, op=ALU.add),
              [tob, t_resid], [tob])
        P.dma("sp", out_ap, ob[:], [tob], t_out)


def emit_ffn(cx, h_in, t_hin, h_out, wg, wu, wd, gpre, gpost, NT, tag):
    P, nc = cx.P, cx.nc
    nst = NT // ST
    t_hout = P.tok("hout")
    with ExitStack() as es:
        sb = lambda name, shape, dt: cx.sb(es, tag + name, shape, dt)
        nt = NormT(cx, es, gpre, tag)
        tl = Tail(cx, es, gpost, 0.5, tag)
        wd_sb = sb("wd", [128, NF, D], BF16)
        t_wd = P.tok("wd")
        wgu = [sb("wgu", [128, 2, 8, 256], BF16) for _ in range(2)]
        t_wgu = P.toks("wgu", 2)
        stg = [[sb("stg", [128, 8, 256], F32) for _ in range(2)] for _ in range(2)]
        t_stg = [[P.tok("stg") for _ in range(2)] for _ in range(2)]
        xt = [sb("xt", [128, 4, D], F32) for _ in range(2)]
        t_xt = P.toks("xt", 2)
        act = sb("act", [128, NF, ST], BF16)
        t_act = P.toks("act", NF)
        sil = [sb("sil", [128, ST], F32) for _ in range(2)]
        t_sil = P.toks("sil", 2)
        wd_v = wd.rearrange("(f p) d -> p f d", p=128)
        for ii, f0 in enumerate(range(0, NF, 2)):
            sg, tsg = stg[ii % 2][(ii // 2) % 2], t_stg[ii % 2][(ii // 2) % 2]
            sgv = sg[:].rearrange("p k c -> p (k c)").rearrange("p (f d) -> p f d", f=2)
            load_cast(cx, wd_sb[:, f0:f0 + 2, :], t_wd, wd_v[:, f0:f0 + 2, :], sgv, tsg,
                      "sp" if ii % 2 == 0 else "act", "act" if ii % 2 == 0 else "dve")
        wg_v = wg.rearrange("(k p) c -> p k c", p=128)
        wu_v = wu.rearrange("(k p) c -> p k c", p=128)
        h_in_v = h_in.rearrange("(s j p) d -> s p j d", p=128, j=4)
        h_out_v = h_out.rearrange("(s j p) d -> s j p d", p=128, j=4)

        def load_x(s):
            P.dma("sp", xt[s % 2][:], h_in_v[s], [t_hin], t_xt[s % 2])

        def load_w(g, slot):
            load_cast(cx, wgu[slot][:, 0], t_wgu[slot], wg_v[:, :, g * 256:(g + 1) * 256], stg[slot][0][:],
                      t_stg[slot][0], "sp", "act")
            load_cast(cx, wgu[slot][:, 1], t_wgu[slot], wu_v[:, :, g * 256:(g + 1) * 256], stg[slot][1][:],
                      t_stg[slot][1], "act", "dve")

        gi = 0
        load_x(0)
        load_w(0, 0)
        for s in range(nst):
            xb, txb = xt[s % 2], t_xt[s % 2]
            if s + 1 < nst:
                load_x(s + 1)
            xnT, t_xnT = nt.run(xb, txb)
            for g in range(NF // 2):
                slot = gi % 2
                gi += 1
                if g + 1 < NF // 2:
                    load_w(g + 1, gi % 2)
                elif s + 1 < nst:
                    load_w(0, gi % 2)
                for c in range(2):
                    f = g * 2 + c
                    pg, tpg = cx.psb[f % 2]
                    pu, tpu = cx.psb[2 + f % 2]
                    for k in range(8):
                        P.add("pe", lambda e, k=k, c=c, pg=pg, slot=slot: e.matmul(
                            pg[:], lhsT=wgu[slot][:, 0, k, c * 128:(c + 1) * 128], rhs=xnT[:, k, :],
                            start=(k == 0), stop=(k == 7)), [t_wgu[slot], t_xnT], [tpg])
                    for k in range(8):
                        P.add("pe", lambda e, k=k, c=c, pu=pu, slot=slot: e.matmul(
                            pu[:], lhsT=wgu[slot][:, 1, k, c * 128:(c + 1) * 128], rhs=xnT[:, k, :],
                            start=(k == 0), stop=(k == 7)), [t_wgu[slot], t_xnT], [tpu])
                    sl, tsl = sil[f % 2], t_sil[f % 2]
                    P.add("act", lambda e, pg=pg, sl=sl: e.activation(out=sl[:], in_=pg[:], func=AF.Silu),
                          [tpg], [tsl])
                    P.add("dve", lambda e, pu=pu, sl=sl, f=f: e.tensor_tensor(out=act[:, f, :], in0=sl[:],
                                                                             in1=pu[:], op=ALU.mult),
                          [tsl, tpu], [t_act[f]])
            for j in range(4):
                tl.run(lambda k, j=j: act[:, k, j * 128:(j + 1) * 128], NF,
                       lambda k, hf: wd_sb[:, k, hf * 512:(hf + 1) * 512], t_act, t_wd,
                       xb[:, j, :], txb, h_out_v[s, j], t_hout)
        P.barrier()
    return t_hout


def emit_proj(cx, h_in, t_hin, w_in, gpre, pos, invtab_d, o_f32, o_qkv, o_gate, NT, tag):
    P, nc = cx.P, cx.nc
    nst = NT // ST
    t_o = [P.tok("of32"), P.tok("oqkv"), P.tok("ogate")]
    with ExitStack() as es:
        sb = lambda name, shape, dt: cx.sb(es, tag + name, shape, dt)
        nt = NormT(cx, es, gpre, tag)
        win = sb("win", [128, 8, IN_COLS], BF16)
        t_win = P.tok("win")
        stg = [sb("stg", [128, IN_COLS], F32) for _ in range(2)]
        t_stg = P.toks("stg", 2)
        w_v = w_in.rearrange("(k p) c -> p k c", p=128)
        for k in range(8):
            load_cast(cx, win[:, k, :], t_win, w_v[:, k, :], stg[k % 2][:], t_stg[k % 2],
                      "sp" if k % 2 == 0 else "act", "act" if k % 2 == 0 else "dve")
        xt = [sb("xt", [128, 4, D], F32) for _ in range(2)]
        t_xt = P.toks("xt", 2)
        invtab = sb("invtab", [128, 32], F32)
        t_inv = P.tok("inv")
        P.dma("sp", invtab[:], invtab_d, [], t_inv)
        posi = sb("posi", [128, NT // 128], I32)
        posf = sb("posf", [128, NT // 128], F32)
        t_pos = P.tok("pos")
        P.dma("sp", posi[:], pos.rearrange("(t p) -> p t", p=128), [], t_pos, allow_slow_non_contiguous=True)
        P.add("dve", lambda e: e.tensor_copy(out=posf[:], in_=posi[:]), [t_pos], [t_pos])
        pr = [sb("pr", [128, IN_COLS], F32) for _ in range(2)]
        t_pr = P.toks("pr", 2)
        pof = [sb("pof", [128, 768], F32) for _ in range(2)]
        poq = [sb("poq", [128, 1280], BF16) for _ in range(2)]
        pog = [sb("pog", [128, 24], F32) for _ in range(2)]
        t_pof, t_poq, t_pog = P.toks("pof", 2), P.toks("poq", 2), P.toks("pog", 2)
        tr = {n: sb(n, [128, 32], F32) for n in ("ang", "kf", "r", "fl", "sin", "cos", "ang2")}
        ki = sb("ki", [128, 32], I32)
        t_trig = P.tok("trig")
        tmp = [sb("tmp", [128, 10, 32], F32) for _ in range(4)]
        t_tmp = P.tok("tmp")
        h_in_v = h_in.rearrange("(s j p) d -> s p j d", p=128, j=4)
        of_v = o_f32.rearrange("(t p) c -> t p c", p=128)
        oq_v = o_qkv.rearrange("(t p) c -> t p c", p=128)
        og_v = o_gate.rearrange("(t p) c -> t p c", p=128)
        cgroups = [(0, 512), (512, 512), (1024, 512), (1536, 512), (2048, 24)]

        def load_x(s):
            P.dma("sp", xt[s % 2][:], h_in_v[s], [t_hin], t_xt[s % 2])

        def trig(dst, src, tt):
            ang, kf, r, fl = src, tr["kf"], tr["r"], tr["fl"]
            P.add("dve", lambda e: e.tensor_scalar(out=kf[:], in0=ang[:], scalar1=1.0 / (2 * PI), scalar2=None,
                                                   op0=ALU.mult), [tt], [tt])
            P.add("dve", lambda e: e.tensor_copy(out=ki[:], in_=kf[:]), [tt], [tt])
            P.add("dve", lambda e: e.tensor_copy(out=kf[:], in_=ki[:]), [tt], [tt])
            P.add("dve", lambda e: e.scalar_tensor_tensor(out=r[:], in0=kf[:], scalar=-2 * PI, in1=ang[:],
                                                          op0=ALU.mult, op1=ALU.add), [tt], [tt])
            P.add("dve", lambda e: e.tensor_scalar(out=fl[:], in0=r[:], scalar1=PI, scalar2=None, op0=ALU.is_gt),
                  [tt], [tt])
            P.add("dve", lambda e: e.scalar_tensor_tensor(out=kf[:], in0=fl[:], scalar=-2 * PI, in1=r[:],
                                                          op0=ALU.mult, op1=ALU.add), [tt], [tt])
            P.add("dve", lambda e: e.tensor_scalar(out=fl[:], in0=kf[:], scalar1=-PI, scalar2=None, op0=ALU.is_lt),
                  [tt], [tt])
            P.add("dve", lambda e: e.scalar_tensor_tensor(out=r[:], in0=fl[:], scalar=2 * PI, in1=kf[:],
                                                          op0=ALU.mult, op1=ALU.add), [tt], [tt])
            P.add("act", lambda e: e.activation(out=dst[:], in_=r[:], func=AF.Sin), [tt], [tt])

        load_x(0)
        ti = 0
        for s in range(nst):
            xb, txb = xt[s % 2], t_xt[s % 2]
            if s + 1 < nst:
                load_x(s + 1)
            xnT, t_xnT = nt.run(xb, txb)
            for j in range(4):
                tix = s * 4 + j
                prb, tprb = pr[tix % 2], t_pr[tix % 2]
                for gi_, (c0, cw) in enumerate(cgroups):
                    pp, tpp = cx.psb[gi_ % 4]
                    for k in range(8):
                        P.add("pe", lambda e, k=k, j=j, c0=c0, cw=cw, pp=pp: e.matmul(
                            pp[:, 0:cw], lhsT=xnT[:, k, j * 128:(j + 1) * 128], rhs=win[:, k, c0:c0 + cw],
                            start=(k == 0), stop=(k == 7)), [t_xnT, t_win], [tpp])
                    if gi_ % 2 == 0:
                        P.add("act", lambda e, pp=pp, c0=c0, cw=cw, prb=prb: e.activation(
                            out=prb[:, c0:c0 + cw], in_=pp[:, 0:cw], func=AF.Copy), [tpp], [tprb])
                    else:
                        P.add("dve", lambda e, pp=pp, c0=c0, cw=cw, prb=prb: e.tensor_copy(
                            out=prb[:, c0:c0 + cw], in_=pp[:, 0:cw]), [tpp], [tprb])
                ang = tr["ang"]
                P.add("dve", lambda e, tix=tix: e.tensor_scalar(out=ang[:], in0=invtab[:],
                                                                scalar1=posf[:, tix:tix + 1], scalar2=None,
                                                                op0=ALU.mult), [t_inv, t_pos], [t_trig])
                trig(tr["sin"], ang, t_trig)
                P.add("dve", lambda e: e.tensor_scalar(out=tr["ang2"][:], in0=ang[:], scalar1=PI / 2, scalar2=None,
                                                       op0=ALU.add), [t_trig], [t_trig])
                trig(tr["cos"], tr["ang2"], t_trig)
                fo, tfo = pof[tix % 2], t_pof[tix % 2]
                qo, tqo = poq[tix % 2], t_poq[tix % 2]
                go, tgo = pog[tix % 2], t_pog[tix % 2]
                P.add("act", lambda e, fo=fo, prb=prb: e.activation(out=fo[:], in_=prb[:, 0:768], func=AF.Copy),
                      [tprb], [tfo])
                P.add("act", lambda e, go=go, prb=prb: e.activation(out=go[:], in_=prb[:, 2048:2072],
                                                                    func=AF.Sigmoid), [tprb], [tgo])
                for c0 in (1408, 1664, 1920):
                    P.add("dve", lambda e, qo=qo, prb=prb, c0=c0: e.tensor_copy(
                        out=qo[:, c0 - 768:c0 - 768 + 128], in_=prb[:, c0:c0 + 128]), [tprb], [tqo])
                for (c0, nh) in ((768, 10), (1536, 2), (1792, 2)):
                    xv = prb[:, c0:c0 + nh * 64].rearrange("p (h t f) -> p h t f", t=2, f=32)
                    ov = qo[:, c0 - 768:c0 - 768 + nh * 64].rearrange("p (h t f) -> p h t f", t=2, f=32)
                    x1, x2 = xv[:, :, 0, :], xv[:, :, 1, :]
                    cb = tr["cos"][:].unsqueeze(1).broadcast_to([128, nh, 32])
                    sb_ = tr["sin"][:].unsqueeze(1).broadcast_to([128, nh, 32])
                    a, b, c, d = (t[:, 0:nh, :] for t in tmp)
                    P.add("dve", lambda e, a=a, x1=x1, cb=cb: e.tensor_tensor(out=a, in0=x1, in1=cb, op=ALU.mult),
                          [tprb, t_trig], [t_tmp])
                    P.add("dve", lambda e, b=b, x2=x2, sb_=sb_: e.tensor_tensor(out=b, in0=x2, in1=sb_, op=ALU.mult),
                          [tprb, t_trig], [t_tmp])
                    P.add("dve", lambda e, a=a, b=b, ov=ov: e.tensor_tensor(out=ov[:, :, 0, :], in0=a, in1=b,
                                                                           op=ALU.subtract), [t_tmp], [tqo])
                    P.add("dve", lambda e, c=c, x2=x2, cb=cb: e.tensor_tensor(out=c, in0=x2, in1=cb, op=ALU.mult),
                          [tprb, t_trig], [t_tmp])
                    P.add("dve", lambda e, d=d, x1=x1, sb_=sb_: e.tensor_tensor(out=d, in0=x1, in1=sb_, op=ALU.mult),
                          [tprb, t_trig], [t_tmp])
                    P.add("dve", lambda e, c=c, d=d, ov=ov: e.tensor_tensor(out=ov[:, :, 1, :], in0=c, in1=d,
                                                                           op=ALU.add), [t_tmp], [tqo])
                P.dma("sp", of_v[tix], fo[:], [tfo], t_o[0])
                P.dma("act", oq_v[tix], qo[:], [tqo], t_o[1])
                P.dma("sp", og_v[tix], go[:], [tgo], t_o[2])
        P.barrier()
    return t_o


POOL_WINDOWS = (2, 4, 8, 16)
SCALE = 0.125
BIG = 1.0e9


def _bd_build(cx, es, P, name, w_d, idx0, idx1):
    f = cx.sb(es, name + "f", [128, 128], F32)
    b = cx.sb(es, name + "b", [128, 128], BF16)
    t = P.tok(name)
    P.add("dve", lambda e: e.memset(f[:], 0.0), [], [t])
    P.dma("sp", f[0:64, 0:64], w_d[idx0], [], t)
    P.dma("sp", f[64:128, 64:128], w_d[idx1], [], t)
    P.add("dve", lambda e: e.tensor_copy(out=b[:], in_=f[:]), [t], [t])
    return b, t


def emit_pool(cx, xpT_d, poolfix_d, pool_w, pool_scale, yT_d, t_yT, TC):
    P, nc = cx.P, cx.nc
    L = 16 + TC
    with ExitStack() as es:
        sb = lambda n, s, d: cx.sb(es, "pl" + n, s, d)
        xp = sb("xp", [128, 2, L], F32)
        t_xp = P.tok("xp")
        P.dma("sp", xp[:], xpT_d, [], t_xp)
        fix = sb("fix", [128, 2, 16], F32)
        t_fix = P.tok("fix")
        P.dma("sp", fix[:], poolfix_d, [], t_fix)
        psc = sb("psc", [128, 2], F32)
        t_psc = P.tok("psc")
        P.dma("sp", psc[:], pool_scale.rearrange("(c p) -> p c", p=128), [], t_psc, allow_slow_non_contiguous=True)
        wA = sb("wA", [128, L], F32)
        wB = sb("wB", [128, L], F32)
        t_wA, t_wB = P.tok("wA"), P.tok("wB")
        dfb = sb("dfb", [128, TC], BF16)
        t_dfb = P.tok("dfb")
        yo = [sb("yo", [128, 512], BF16) for _ in range(2)]
        t_yo = P.toks("yo", 2)
        for cc in range(2):
            bd, t_bd = _bd_build(cx, es, P, f"plbd{cc}", pool_w, 2 * cc, 2 * cc + 1)
            for hh in range(2):
                w = POOL_WINDOWS[2 * cc + hh]
                rows = slice(hh * 64, hh * 64 + 64)
                src, tsrc = xp[rows, cc, :], t_xp
                bufs = [(wA, t_wA), (wB, t_wB)]
                lo, sh, bi = 0, 1, 0
                while sh < w:
                    dst, tdst = bufs[bi]
                    lo2 = lo + sh
                    P.add("dve", lambda e, dst=dst, src=src, lo2=lo2, sh=sh, rows=rows: e.tensor_tensor(
                        out=dst[rows, lo2:L], in0=src[:, lo2:L], in1=src[:, lo2 - sh:L - sh], op=ALU.add),
                        [tsrc], [tdst])
                    src, tsrc = dst[rows, :], tdst
                    lo, sh, bi = lo2, sh * 2, 1 - bi
                dst, tdst = bufs[bi]
                P.add("dve", lambda e, dst=dst, src=src, rows=rows, w=w: e.tensor_scalar(
                    out=dst[rows, 16:L], in0=src[:, 16:L], scalar1=1.0 / w, scalar2=None, op0=ALU.mult),
                    [tsrc], [tdst])
                P.add("dve", lambda e, dst=dst, rows=rows, cc=cc: e.tensor_tensor(
                    out=dst[rows, 16:32], in0=dst[rows, 16:32], in1=fix[rows, cc, :], op=ALU.mult),
                    [tdst, t_fix], [tdst])
                P.add("dve", lambda e, dst=dst, rows=rows, cc=cc: e.tensor_tensor(
                    out=dfb[rows, :], in0=dst[rows, 16:L], in1=xp[rows, cc, 16:L], op=ALU.subtract),
                    [tdst, t_xp], [t_dfb])
            for ch in range(TC // 512):
                pp, tpp = cx.psb[ch % 2]
                P.add("pe", lambda e, pp=pp, ch=ch, bd=bd: e.matmul(pp[:], lhsT=bd[:], rhs=dfb[:, ch * 512:(ch + 1) * 512],
                                                                   start=True, stop=True), [t_bd, t_dfb], [tpp])
                yb, tyb = yo[ch % 2], t_yo[ch % 2]
                P.add("act", lambda e, pp=pp, yb=yb, cc=cc: e.activation(out=yb[:], in_=pp[:], func=AF.Copy,
                                                                        scale=psc[:, cc:cc + 1]), [tpp, t_psc], [tyb])
                P.dma("sp", yT_d[:, cc, ch * 512:(ch + 1) * 512], yb[:], [tyb], t_yT)
        P.barrier()


def emit_lru(cx, xlT_d, glT_d, lruflag_d, conv_w, conv_b, w_r, b_r, w_i, b_i, lam, yT_d, t_yT, S, TC):
    P, nc = cx.P, cx.nc
    NLC = S // 512
    own0 = NLC - TC // 512
    with ExitStack() as es:
        sb = lambda n, s, d: cx.sb(es, "lr" + n, s, d)
        flag = sb("flag", [128, NLC], F32)
        t_flag = P.tok("flag")
        P.dma("sp", flag[:], lruflag_d, [], t_flag)
        one = sb("one", [128, 1], F32)
        t_one = P.tok("one")
        P.add("dve", lambda e: e.memset(one[:], 1.0), [], [t_one])
        for cc in range(2):
            cw = sb("cw", [128, 4], F32)
            cst = sb("cst", [128, 8], F32)
            t_c = P.tok("lrc")
            P.dma("sp", cw[:], conv_w[:, cc * 128:(cc + 1) * 128].rearrange("k p -> p k"), [], t_c,
                  allow_slow_non_contiguous=True)
            for ci, v in enumerate((conv_b, b_r, b_i, lam)):
                P.dma("sp", cst[:, ci:ci + 1], v[cc * 128:(cc + 1) * 128].rearrange("(p o) -> p o", o=1), [], t_c,
                      allow_slow_non_contiguous=True)
            P.add("act", lambda e, cst=cst: e.activation(out=cst[:, 4:5], in_=cst[:, 3:4], func=AF.Exp, scale=-1.0),
                  [t_c], [t_c])
            P.add("dve", lambda e, cst=cst: e.tensor_scalar(out=cst[:, 4:5], in0=cst[:, 4:5], scalar1=1.0,
                                                            scalar2=None, op0=ALU.add), [t_c], [t_c])
            P.add("act", lambda e, cst=cst: e.activation(out=cst[:, 5:6], in_=cst[:, 4:5], func=AF.Ln), [t_c], [t_c])
            P.add("dve", lambda e, cst=cst: e.tensor_scalar(out=cst[:, 6:7], in0=cst[:, 5:6], scalar1=-8.0,
                                                            scalar2=None, op0=ALU.mult), [t_c], [t_c])
            P.add("dve", lambda e, cst=cst: e.tensor_scalar(out=cst[:, 7:8], in0=cst[:, 5:6], scalar1=-16.0,
                                                            scalar2=None, op0=ALU.mult), [t_c], [t_c])
            bdr, t_bdr = _bd_build(cx, es, P, f"bdr{cc}", w_r, 2 * cc, 2 * cc + 1)
            bdi, t_bdi = _bd_build(cx, es, P, f"bdi{cc}", w_i, 2 * cc, 2 * cc + 1)
            hst = sb("hst", [128, 2], F32)
            t_hst = P.tok("hst")
            P.add("dve", lambda e, hst=hst: e.memset(hst[:], 0.0), [], [t_hst])
            X = [sb("X", [128, 516], F32) for _ in range(2)]
            t_X = P.toks("X", 2)
            G = [sb("G", [128, 512], F32) for _ in range(2)]
            t_G = P.toks("G", 2)
            w = {n: sb(n, [128, 512], F32) for n in ("xc", "r", "i", "a", "a2", "m", "t", "b", "h", "g2", "u", "sg", "ge")}
            tw = {n: P.tok(n) for n in w}
            xcb = sb("xcb", [128, 512], BF16)
            t_xcb = P.tok("xcb")
            yo = [sb("yo", [128, 512], BF16) for _ in range(2)]
            t_yo = P.toks("yo", 2)
            for ch in range(NLC):
                Xb, tX = X[ch % 2], t_X[ch % 2]
                P.dma("sp", Xb[:], xlT_d[:, cc, ch * 512:ch * 512 + 516], [], tX)
                xc = w["xc"]
                P.add("dve", lambda e, Xb=Xb, cw=cw, cst=cst: e.tensor_scalar(
                    out=xc[:], in0=Xb[:, 4:516], scalar1=cw[:, 3:4], scalar2=cst[:, 0:1], op0=ALU.mult, op1=ALU.add),
                    [tX, t_c], [tw["xc"]])
                for k in range(3):
                    P.add("dve", lambda e, Xb=Xb, cw=cw, k=k: e.scalar_tensor_tensor(
                        out=xc[:], in0=Xb[:, 1 + k:513 + k], scalar=cw[:, k:k + 1], in1=xc[:], op0=ALU.mult,
                        op1=ALU.add), [tX, t_c, tw["xc"]], [tw["xc"]])
                P.add("act", lambda e: e.activation(out=xcb[:], in_=xc[:], func=AF.Copy), [tw["xc"]], [t_xcb])
                pr, tpr = cx.psb[0]
                pi_, tpi = cx.psb[1]
                P.add("pe", lambda e, bdr=bdr, pr=pr: e.matmul(pr[:], lhsT=bdr[:], rhs=xcb[:], start=True, stop=True),
                      [t_bdr, t_xcb], [tpr])
                P.add("pe", lambda e, bdi=bdi, pi_=pi_: e.matmul(pi_[:], lhsT=bdi[:], rhs=xcb[:], start=True, stop=True),
                      [t_bdi, t_xcb], [tpi])
                P.add("act", lambda e, pr=pr, cst=cst: e.activation(out=w["r"][:], in_=pr[:], func=AF.Sigmoid,
                                                                    bias=cst[:, 1:2]), [tpr, t_c], [tw["r"]])
                P.add("act", lambda e, pi_=pi_, cst=cst: e.activation(out=w["i"][:], in_=pi_[:], func=AF.Sigmoid,
                                                                      bias=cst[:, 2:3]), [tpi, t_c], [tw["i"]])
                P.add("act", lambda e, cst=cst: e.activation(out=w["a"][:], in_=w["r"][:], func=AF.Exp,
                                                             scale=cst[:, 6:7]), [tw["r"], t_c], [tw["a"]])
                P.add("act", lambda e, cst=cst: e.activation(out=w["a2"][:], in_=w["r"][:], func=AF.Exp,
                                                             scale=cst[:, 7:8]), [tw["r"], t_c], [tw["a2"]])
                P.add("act", lambda e: e.activation(out=w["m"][:], in_=w["a2"][:], func=AF.Sqrt, scale=-1.0,
                                                    bias=one[:, 0:1]), [tw["a2"], t_one], [tw["m"]])
                P.add("dve", lambda e: e.tensor_tensor(out=w["t"][:], in0=w["i"][:], in1=xc[:], op=ALU.mult),
                      [tw["i"], tw["xc"]], [tw["t"]])
                P.add("dve", lambda e, ch=ch: e.scalar_tensor_tensor(out=w["b"][:], in0=w["t"][:],
                                                                    scalar=flag[:, ch:ch + 1], in1=w["m"][:],
                                                                    op0=ALU.mult, op1=ALU.mult),
                      [tw["t"], tw["m"], t_flag], [tw["b"]])
                P.add("dve", lambda e, hst=hst: e.tensor_tensor_scan(out=w["h"][:], data0=w["a"][:], data1=w["b"][:],
                                                                    initial=hst[:, 0:1], op0=ALU.mult, op1=ALU.add),
                      [tw["a"], tw["b"], t_hst], [tw["h"]])
                P.add("dve", lambda e, hst=hst: e.tensor_copy(out=hst[:, 0:1], in_=w["h"][:, 511:512]),
                      [tw["h"]], [t_hst])
                if ch >= own0:
                    oc = ch - own0
                    Gb, tG = G[oc % 2], t_G[oc % 2]
                    P.dma("act", Gb[:], glT_d[:, cc, oc * 512:(oc + 1) * 512], [], tG)
                    P.add("dve", lambda e, Gb=Gb: e.tensor_tensor(out=w["g2"][:], in0=Gb[:], in1=Gb[:], op=ALU.mult),
                          [tG], [tw["g2"]])
                    P.add("dve", lambda e: e.tensor_scalar(out=w["u"][:], in0=w["g2"][:], scalar1=0.044715,
                                                           scalar2=1.0, op0=ALU.mult, op1=ALU.add),
                          [tw["g2"]], [tw["u"]])
                    P.add("dve", lambda e, Gb=Gb: e.tensor_tensor(out=w["g2"][:], in0=w["u"][:], in1=Gb[:],
                                                                  op=ALU.mult), [tw["u"], tG], [tw["g2"]])
                    P.add("act", lambda e: e.activation(out=w["sg"][:], in_=w["g2"][:], func=AF.Sigmoid,
                                                        scale=1.5957691216057308), [tw["g2"]], [tw["sg"]])
                    P.add("dve", lambda e, Gb=Gb: e.tensor_tensor(out=w["ge"][:], in0=w["sg"][:], in1=Gb[:],
                                                                  op=ALU.mult), [tw["sg"], tG], [tw["ge"]])
                    yb, tyb = yo[oc % 2], t_yo[oc % 2]
                    P.add("dve", lambda e, yb=yb: e.tensor_tensor(out=yb[:], in0=w["h"][:], in1=w["ge"][:],
                                                                  op=ALU.mult), [tw["h"], tw["ge"]], [tyb])
                    P.dma("sp", yT_d[:, 2 + cc, oc * 512:(oc + 1) * 512], yb[:], [tyb], t_yT)
        P.barrier()


def emit_attn(cx, a, yT_d, t_yT, S, TC):
    P, nc = cx.P, cx.nc
    TCT, NCH, NSEL, NCM = TC // 128, S // 128, S // 64, S // 16
    NCC = max(NCM // 128, 1)
    JH = min(NSEL, 128)
    NJH = max(NSEL // 128, 1)
    NB = max(NCM // 512, 1)
    NBW = min(NCM, 512)
    with ExitStack() as es:
        sb = lambda n, s, d: cx.sb(es, "at" + n, s, d)
        T = P.tok
        QT = sb("QT", [128, 4, TC], BF16); t_QT = T("QT")
        P.dma("sp", QT[:], a["QT"], [], t_QT)
        KsT = sb("KsT", [128, S], BF16); t_KsT = T("KsT")
        P.dma("act", KsT[:], a["KsT"], [], t_KsT)
        VsE = sb("VsE", [128, NCH, 2, 65], BF16); t_VsE = T("VsE")
        P.add("dve", lambda e: e.memset(VsE[:], 1.0), [], [t_VsE])
        P.dma("sp", VsE[:, :, :, 0:64], a["Vs"].rearrange("p c (h d) -> p c h d", h=2), [], t_VsE)
        KwT = sb("KwT", [128, 512 + TC], BF16); t_KwT = T("KwT")
        P.dma("act", KwT[:], a["KwT"], [], t_KwT)
        VwE = sb("VwE", [128, 4 + TCT, 2, 65], BF16); t_VwE = T("VwE")
        P.add("dve", lambda e: e.memset(VwE[:], 1.0), [], [t_VwE])
        P.dma("sp", VwE[:, :, :, 0:64], a["Vw"].rearrange("p c (h d) -> p c h d", h=2), [], t_VwE)
        Ebig = sb("Ebig", [128, 8192], BF16); t_cst = T("cst")
        P.dma("act", Ebig[:], a["Ebig"], [], t_cst)
        Akq = sb("Akq", [128, 128], F32); Ac = sb("Ac", [128, 128], F32); identf = sb("idf", [128, 128], F32)
        iota16 = sb("iota16", [128, NCM], F32)
        for dst, src in ((Akq, "Akq"), (Ac, "Ac"), (identf, "identf"), (iota16, "iota16")):
            P.dma("sp", dst[:], a[src], [], t_cst)
        gate = sb("gate", [128, TCT, 24], F32)
        P.dma("sp", gate[:], a["gate"], [], t_cst)
        NTHR = a["thr"].shape[1]
        thr = sb("thr", [128, NTHR], F32)
        P.dma("sp", thr[:], a["thr"], [], t_cst)
        thrq = sb("thrq", [128, TCT], F32)
        P.dma("sp", thrq[:], a["thrq"], [], t_cst)
        o_c, o_d, o_w = 0, TCT * NCC, TCT * NCC + TCT * 4
        KcmpT = sb("KcmpT", [128, NCM], BF16); t_Kcmp = T("Kcmp")
        VcE = sb("VcE", [128, NCC, 2, 65], BF16); t_VcE = T("VcE")
        P.add("dve", lambda e: e.memset(VcE[:], 1.0), [], [t_VcE])
        with ExitStack() as es2:
            sb2 = lambda n, s, d: cx.sb(es2, "cm" + n, s, d)
            src = sb2("src", [128, S + 16], BF16); t_src = T("src")
            wst = sb2("wst", [128, 32, 64], F32); t_wst = T("wst")
            BD = sb2("BD", [128, 32, 128], BF16); t_BD = T("BD")
            pes = sb2("pes", [128, 32], F32); PEt = sb2("PEt", [128, 32], BF16); t_pe = T("pe")
            cpe = sb2("cpe", [128, 2], F32); t_cpe = T("cpe")
            VcT = sb2("VcT", [128, NCM], BF16); t_VcT = T("VcT")
            for hh in range(2):
                P.dma("sp", pes[hh * 64:(hh + 1) * 64, :], a["cmp_pe"].rearrange("l d -> d l"), [], t_pe,
                      allow_slow_non_contiguous=True)
            P.add("dve", lambda e: e.tensor_copy(out=PEt[:], in_=pes[:]), [t_pe], [t_pe])
            for which, (srcname, wname) in enumerate((("KcT", "cmp_w_k"), ("VcT", "cmp_w_v"))):
                P.add("dve", lambda e: e.memset(src[:, S:S + 16], 0.0), [], [t_src])
                P.dma("sp", src[:, 0:S], a[srcname], [], t_src)
                for hh in range(2):
                    P.dma("act", wst[hh * 64:(hh + 1) * 64, :, :], a[wname].rearrange("l d e -> d l e"), [], t_wst)
                P.add("dve", lambda e: e.memset(BD[:], 0.0), [], [t_BD])
                for hh in range(2):
                    P.add("dve", lambda e, hh=hh: e.tensor_copy(out=BD[hh * 64:(hh + 1) * 64, :, hh * 64:(hh + 1) * 64],
                                                                in_=wst[hh * 64:(hh + 1) * 64, :, :]), [t_wst], [t_BD])
                pc, tpc = cx.psb[2]
                for l in range(32):
                    P.add("pe", lambda e, l=l, pc=pc: e.matmul(pc[:, 0:1], lhsT=BD[:, l, :], rhs=PEt[:, l:l + 1],
                                                               start=(l == 0), stop=(l == 31)), [t_BD, t_pe], [tpc])
                P.add("dve", lambda e, pc=pc, which=which: e.tensor_copy(out=cpe[:, which:which + 1], in_=pc[:, 0:1]),
                      [tpc], [t_cpe])
                sv = src[:].rearrange("p (n s) -> p n s", s=16)
                dstT = KcmpT if which == 0 else VcT
                tdst = t_Kcmp if which == 0 else t_VcT
                for nb in range(NB):
                    pk, tpk = cx.psb[nb % 2]
                    n0 = nb * NBW
                    for l in range(32):
                        rhs = sv[:, n0:n0 + NBW, l] if l < 16 else sv[:, n0 + 1:n0 + 1 + NBW, l - 16]
                        P.add("pe", lambda e, l=l, pk=pk, rhs=rhs: e.matmul(pk[:, 0:NBW], lhsT=BD[:, l, :], rhs=rhs,
                                                                          start=(l == 0), stop=(l == 31)),
                              [t_BD, t_src], [tpk])
                    P.add("act", lambda e, pk=pk, n0=n0, dstT=dstT, which=which: e.activation(
                        out=dstT[:, n0:n0 + NBW], in_=pk[:, 0:NBW], func=AF.Identity, bias=cpe[:, which:which + 1]),
                        [tpk, t_cpe], [tdst])
            for c in range(NCC):
                pv, tpv = cx.psb[3]
                P.add("pe", lambda e, c=c, pv=pv: e.matmul(pv[:, 0:128], lhsT=VcT[:, c * 128:(c + 1) * 128],
                                                           rhs=cx.ident[:], start=True, stop=True),
                      [t_VcT, cx.t_ident], [tpv])
                P.add("dve", lambda e, c=c, pv=pv: e.tensor_copy(
                    out=VcE[:, c, :, 0:64], in_=pv[:, 0:128].rearrange("p (h d) -> p h d", h=2)), [tpv], [t_VcE])
            P.barrier()
        cand = [sb("cand", [128, NSEL], F32) for _ in range(2)]; t_cand = P.toks("cand", 2)
        forc = [sb("forc", [128, NSEL], F32) for _ in range(2)]; t_forc = P.toks("forc", 2)
        Etm = sb("Etm", [128, NCM], F32); t_Etm = T("Etm")
        Pm_tm = sb("Pmtm", [128, NCM], F32); t_Pmtm = T("Pmtm")
        st = sb("st", [128, 16], F32); t_st = T("st")
        t4 = sb("t4", [128, NSEL], F32); t_t4 = T("t4")
        imp = sb("imp", [128, NSEL], F32); t_imp = T("imp")
        score = sb("score", [128, NSEL], F32); sc2 = sb("sc2", [128, NSEL], F32); t_sc = T("score")
        m8 = sb("m8", [128, 16], F32); t_m8 = T("m8")
        sel = sb("sel", [128, NSEL], BF16); t_sel = T("sel")
        selT = sb("selT", [128, NJH, 128], BF16); t_selT = T("selT")
        P.add("dve", lambda e: e.memset(selT[:], 0.0), [], [t_selT])
        E = [sb("E", [128, 4, 128], BF16) for _ in range(2)]; t_E = P.toks("E", 2)
        Pm = [sb("Pm", [128, 4, 128], BF16) for _ in range(2)]; t_Pm = P.toks("Pm", 2)
        OTs = sb("OTs", [65, 512], F32); t_OTs = T("OTs")
        w4 = sb("w4", [128, 8], F32); t_w4 = T("w4")
        tmpo = sb("tmpo", [128, 4, 64], F32); t_tmpo = T("tmpo")
        oacc = sb("oacc", [128, 512], F32); t_oacc = T("oacc")
        ob = sb("ob", [128, 512], BF16); t_ob = T("ob")
        oT = [sb("oT", [128, 128], BF16) for _ in range(2)]; t_oT = P.toks("oT", 2)
        step = [0]

        def attn_step(h, lhsK, tK, qv, vE, tV, maskfn, OT, tOT, first, last):
            i = step[0]; step[0] += 1
            ps, tps = cx.psb[i % 2]
            Eb, tE = E[i % 2], t_E[i % 2]
            P.add("pe", lambda e, ps=ps: e.matmul(ps[:], lhsT=lhsK, rhs=qv, start=True, stop=True), [tK, t_QT], [tps])
            P.add("act", lambda e, ps=ps, Eb=Eb: e.activation(out=Eb[:].rearrange("p g q -> p (g q)"), in_=ps[:],
                                                              func=AF.Exp, scale=SCALE), [tps], [tE])
            rhsP, tP = maskfn(Eb, tE, i)
            P.add("pe", lambda e, rhsP=rhsP: e.matmul(OT[0:65, :], lhsT=vE, rhs=rhsP[:].rearrange("p g q -> p (g q)"),
                                                      start=first, stop=last), [tV, tP], [tOT])

        for m in range(TCT):
            cb, tcb = cand[m % 2], t_cand[m % 2]
            fb, tfb = forc[m % 2], t_forc[m % 2]
            P.dma("sp", cb[:], a["cand"][m], [], tcb)
            P.dma("act", fb[:], a["forced"][m], [], tfb)
            for h in range(2):
                rows = slice(h * 64, (h + 1) * 64)
                qv = QT[rows, :, m * 128:(m + 1) * 128]
                P.add("dve", lambda e: e.memset(st[:], 0.0), [], [t_st])
                for g in range(4):
                    for nb in range(NB):
                        ps, tps = cx.psb[2 + nb % 2]
                        P.add("pe", lambda e, ps=ps, g=g, nb=nb, rows=rows, m=m: e.matmul(
                            ps[:, 0:NBW], lhsT=QT[rows, g, m * 128:(m + 1) * 128], rhs=KcmpT[rows, nb * NBW:(nb + 1) * NBW],
                            start=True, stop=True), [t_QT, t_Kcmp], [tps])
                        P.add("act", lambda e, ps=ps, nb=nb: e.activation(out=Etm[:, nb * NBW:(nb + 1) * NBW],
                                                                          in_=ps[:, 0:NBW], func=AF.Exp, scale=SCALE),
                              [tps], [t_Etm])
                    P.add("dve", lambda e, g=g, m=m: e.scalar_tensor_tensor(
                        out=Pm_tm[:], in0=iota16[:], scalar=thrq[:, m:m + 1], in1=Etm[:], op0=ALU.is_le, op1=ALU.mult,
                        accum_out=st[:, g:g + 1]), [t_cst, t_Etm, t_st], [t_Pmtm, t_st])
                    P.add("dve", lambda e, g=g: e.tensor_scalar(out=st[:, 8 + g:9 + g], in0=st[:, g:g + 1],
                                                                scalar1=1e-30, scalar2=None, op0=ALU.max),
                          [t_st], [t_st])
                    P.add("dve", lambda e, g=g: e.reciprocal(out=st[:, 4 + g:5 + g], in_=st[:, 8 + g:9 + g]),
                          [t_st], [t_st])
                    pv4 = Pm_tm[:].rearrange("p (j r) -> p j r", r=4)
                    P.add("dve", lambda e, pv4=pv4: e.tensor_reduce(out=t4[:], in_=pv4, axis=AX.X, op=ALU.add),
                          [t_Pmtm], [t_t4])
                    P.add("dve", lambda e, pv4=pv4: e.tensor_tensor(out=t4[:, 1:NSEL], in0=t4[:, 1:NSEL],
                                                                    in1=pv4[:, 0:NSEL - 1, 3], op=ALU.add),
                          [t_Pmtm, t_t4], [t_t4])
                    if g == 0:
                        P.add("dve", lambda e: e.tensor_scalar(out=imp[:], in0=t4[:], scalar1=st[:, 4:5], scalar2=None,
                                                               op0=ALU.mult), [t_t4, t_st], [t_imp])
                    else:
                        P.add("dve", lambda e, g=g: e.scalar_tensor_tensor(out=imp[:], in0=t4[:], scalar=st[:, 4 + g:5 + g],
                                                                          in1=imp[:], op0=ALU.mult, op1=ALU.add),
                              [t_t4, t_st, t_imp], [t_imp])
                P.add("dve", lambda e, cb=cb: e.tensor_tensor(out=score[:], in0=imp[:], in1=cb[:], op=ALU.mult),
                      [t_imp, tcb], [t_sc])
                P.add("dve", lambda e, fb=fb: e.tensor_tensor(out=score[:], in0=score[:], in1=fb[:], op=ALU.add),
                      [t_sc, tfb], [t_sc])
                P.add("dve", lambda e: e.max(out=m8[:, 0:8], in_=score[:]), [t_sc], [t_m8])
                P.add("dve", lambda e: e.match_replace(out=sc2[:], in_to_replace=m8[:, 0:8], in_values=score[:],
                                                       imm_value=-1e30), [t_sc, t_m8], [t_sc])
                P.add("dve", lambda e: e.max(out=m8[:, 8:16], in_=sc2[:]), [t_sc], [t_m8])
                P.add("dve", lambda e: e.tensor_scalar(out=m8[:, 0:1], in0=m8[:, 15:16], scalar1=1e-30, scalar2=None,
                                                       op0=ALU.max), [t_m8], [t_m8])
                P.add("dve", lambda e: e.tensor_scalar(out=sel[:], in0=score[:], scalar1=m8[:, 0:1], scalar2=None,
                                                       op0=ALU.is_ge), [t_sc, t_m8], [t_sel])
                for jh in range(NJH):
                    pt, tpt = cx.psb[3]
                    P.add("pe", lambda e, jh=jh, pt=pt: e.matmul(pt[0:JH, 0:128], lhsT=sel[:, jh * 128:jh * 128 + JH],
                                                                 rhs=cx.ident[:], start=True, stop=True),
                          [t_sel, cx.t_ident], [tpt])
                    P.add("act", lambda e, jh=jh, pt=pt: e.activation(out=selT[0:JH, jh, :], in_=pt[0:JH, 0:128],
                                                                      func=AF.Copy), [tpt], [t_selT])
                OTb = {0: cx.psb[4], 1: cx.psb[5], 2: cx.psb[6]}
                for kc in range(NCH):
                    def mf(Eb, tE, i, kc=kc, m=m):
                        pmk, tpmk = cx.psb[2 + i % 2]
                        jh = (2 * kc) // 128
                        off = 128 * (kc % 64)
                        P.add("pe", lambda e, pmk=pmk: e.matmul(pmk[:, 0:128], lhsT=Ebig[0:JH, off:off + 128],
                                                                rhs=selT[0:JH, jh, :], start=True, stop=True),
                              [t_cst, t_selT], [tpmk])
                        Pb, tPb = Pm[i % 2], t_Pm[i % 2]
                        P.add("dve", lambda e, pmk=pmk, Pb=Pb, Eb=Eb: e.tensor_tensor(
                            out=Pb[:], in0=Eb[:], in1=pmk[:, 0:128].unsqueeze(1).broadcast_to([128, 4, 128]),
                            op=ALU.mult), [tE, tpmk], [tPb])
                        if kc % TCT == m:
                            ci = o_d + m * 4 + kc // TCT
                            P.add("dve", lambda e, Pb=Pb, ci=ci: e.scalar_tensor_tensor(
                                out=Pb[:], in0=Akq[:].unsqueeze(1).broadcast_to([128, 4, 128]), scalar=thr[:, ci:ci + 1],
                                in1=Pb[:], op0=ALU.is_le, op1=ALU.mult), [tPb, t_cst], [tPb])
                        return Pb, tPb
                    attn_step(h, KsT[rows, kc * 128:(kc + 1) * 128], t_KsT, qv, VsE[:, kc, h, :], t_VsE, mf,
                              OTb[1][0], OTb[1][1], kc == 0, kc == NCH - 1)
                for c in range(NCC):
                    def mf(Eb, tE, i, c=c, m=m):
                        Pb, tPb = Pm[i % 2], t_Pm[i % 2]
                        ci = o_c + m * NCC + c
                        P.add("dve", lambda e, Pb=Pb, Eb=Eb, ci=ci: e.scalar_tensor_tensor(
                            out=Pb[:], in0=Ac[:].unsqueeze(1).broadcast_to([128, 4, 128]), scalar=thr[:, ci:ci + 1],
                            in1=Eb[:], op0=ALU.is_le, op1=ALU.mult), [tE, t_cst], [tPb])
                        return Pb, tPb
                    attn_step(h, KcmpT[rows, c * 128:(c + 1) * 128], t_Kcmp, qv, VcE[:, c, h, :], t_VcE, mf,
                              OTb[0][0], OTb[0][1], c == 0, c == NCC - 1)
                for c in range(5):
                    def mf(Eb, tE, i, c=c, m=m):
                        if c in (0, 4) or m + c < 4:
                            Pb, tPb = Pm[i % 2], t_Pm[i % 2]
                            ci = o_w + m * 5 + c
                            op = ALU.is_gt if c == 0 else ALU.is_le
                            P.add("dve", lambda e, Pb=Pb, Eb=Eb, ci=ci, op=op: e.scalar_tensor_tensor(
                                out=Pb[:], in0=Akq[:].unsqueeze(1).broadcast_to([128, 4, 128]), scalar=thr[:, ci:ci + 1],
                                in1=Eb[:], op0=op, op1=ALU.mult), [tE, t_cst], [tPb])
                            return Pb, tPb
                        return Eb, tE
                    attn_step(h, KwT[rows, (m + c) * 128:(m + c + 1) * 128], t_KwT, qv, VwE[:, m + c, h, :], t_VwE, mf,
                              OTb[2][0], OTb[2][1], c == 0, c == 4)
                for br in range(3):
                    OT, tOT = OTb[br]
                    P.add("act", lambda e, OT=OT: e.activation(out=OTs[:], in_=OT[0:65, :], func=AF.Copy), [tOT], [t_OTs])
                    pt, tpt = cx.psb[7]
                    for g in range(4):
                        P.add("pe", lambda e, g=g, pt=pt: e.matmul(pt[:, g * 65:(g + 1) * 65],
                                                                   lhsT=OTs[0:65, g * 128:(g + 1) * 128],
                                                                   rhs=identf[0:65, 0:65], start=True, stop=True),
                              [t_OTs, t_cst], [tpt])
                    ptv = pt[:, 0:260].rearrange("p (g e) -> p g e", e=65)
                    P.add("dve", lambda e, ptv=ptv: e.tensor_scalar(out=w4[:, 0:4], in0=ptv[:, :, 64], scalar1=1e-30,
                                                                    scalar2=None, op0=ALU.max), [tpt], [t_w4])
                    P.add("dve", lambda e: e.reciprocal(out=w4[:, 4:8], in_=w4[:, 0:4]), [t_w4], [t_w4])
                    gv = gate[:, m, h * 12:(h + 1) * 12].rearrange("p (g b) -> p g b", b=3)[:, :, br]
                    P.add("dve", lambda e, gv=gv: e.tensor_tensor(out=w4[:, 0:4], in0=w4[:, 4:8], in1=gv, op=ALU.mult),
                          [t_w4, t_cst], [t_w4])
                    oav = oacc[:, h * 256:(h + 1) * 256].rearrange("p (g d) -> p g d", d=64)
                    wb = w4[:, 0:4].unsqueeze(2).broadcast_to([128, 4, 64])
                    if br == 0:
                        P.add("dve", lambda e, ptv=ptv, oav=oav, wb=wb: e.tensor_tensor(
                            out=oav, in0=ptv[:, :, 0:64], in1=wb, op=ALU.mult), [tpt, t_w4], [t_oacc])
                    else:
                        P.add("dve", lambda e, ptv=ptv, wb=wb: e.tensor_tensor(
                            out=tmpo[:], in0=ptv[:, :, 0:64], in1=wb, op=ALU.mult), [tpt, t_w4], [t_tmpo])
                        P.add("dve", lambda e, oav=oav: e.tensor_tensor(out=oav, in0=oav, in1=tmpo[:], op=ALU.add),
                              [t_tmpo, t_oacc], [t_oacc])
            P.add("act", lambda e: e.activation(out=ob[:], in_=oacc[:], func=AF.Copy), [t_oacc], [t_ob])
            for c4 in range(4):
                pt, tpt = cx.psb[7]
                P.add("pe", lambda e, c4=c4, pt=pt: e.matmul(pt[:, 0:128], lhsT=ob[:, c4 * 128:(c4 + 1) * 128],
                                                             rhs=cx.ident[:], start=True, stop=True),
                      [t_ob, cx.t_ident], [tpt])
                otb, totb = oT[c4 % 2], t_oT[c4 % 2]
                P.add("act", lambda e, pt=pt, otb=otb: e.activation(out=otb[:], in_=pt[:, 0:128], func=AF.Copy),
                      [tpt], [totb])
                P.dma("sp", yT_d[:, 4 + c4, m * 128:(m + 1) * 128], otb[:], [totb], t_yT)
        P.barrier()


def emit_wout(cx, yT_d, t_yT, w_out, gpost, h_in, t_hin, h_out, TC):
    P, nc = cx.P, cx.nc
    t_hout = P.tok("hmid")
    with ExitStack() as es:
        sb = lambda n, s, d: cx.sb(es, "wo" + n, s, d)
        tl = Tail(cx, es, gpost, 1.0, "wo")
        wo = sb("w", [128, 8, D], BF16); t_wo = P.tok("wo")
        stg = [sb("stg", [128, D], F32) for _ in range(2)]; t_stg = P.toks("stg", 2)
        w_v = w_out.rearrange("(k p) d -> p k d", p=128)
        for k in range(8):
            load_cast(cx, wo[:, k, :], t_wo, w_v[:, k, :], stg[k % 2][:], t_stg[k % 2],
                      "sp" if k % 2 == 0 else "act", "act" if k % 2 == 0 else "dve")
        xt = [sb("xt", [128, 4, D], F32) for _ in range(2)]; t_xt = P.toks("xt", 2)
        yt = [sb("yt", [128, 8, ST], BF16) for _ in range(2)]; t_yt = P.toks("yt", 2)
        h_in_v = h_in.rearrange("(s j p) d -> s p j d", p=128, j=4)
        h_out_v = h_out.rearrange("(s j p) d -> s j p d", p=128, j=4)
        for s in range(TC // ST):
            xb, txb = xt[s % 2], t_xt[s % 2]
            yb, tyb = yt[s % 2], t_yt[s % 2]
            P.dma("sp", xb[:], h_in_v[s], [t_hin], txb)
            P.dma("act", yb[:], yT_d[:, :, s * ST:(s + 1) * ST], [t_yT], tyb)
            for j in range(4):
                tl.run(lambda k, j=j, yb=yb: yb[:, k, j * 128:(j + 1) * 128], 8,
                       lambda k, hf: wo[:, k, hf * 512:(hf + 1) * 512], [tyb], t_wo,
                       xb[:, j, :], txb, h_out_v[s, j], t_hout)
        P.barrier()
    return t_hout


import numpy as np
import ml_dtypes
BF = ml_dtypes.bfloat16
BIGV = 1.0e9

def consts(S):
    NCM = S // 16
    c = {}
    x = np.arange(8192)
    c["Ebig"] = (np.arange(128)[:, None] == (x // 64)[None, :]).astype(np.float32).astype(BF)
    k = np.arange(128, dtype=np.float32)
    c["Akq"] = (k[:, None] - k[None, :]).astype(np.float32)
    c["Ac"] = (16 * k[:, None] + 31 - k[None, :]).astype(np.float32)
    c["identf"] = np.eye(128, dtype=np.float32)
    c["iota16"] = np.tile((16.0 * np.arange(NCM, dtype=np.float32))[None, :], (128, 1))
    c["ident"] = np.eye(128, dtype=np.float32).astype(BF)
    inv = (10000.0 ** (-np.arange(0, 64, 2, dtype=np.float32) / 64)).astype(np.float32)
    c["invtab"] = np.tile(inv[None, :], (128, 1)).astype(np.float32)
    return c

def tables(qd, S, TC):
    TCT, NSEL, NCM = TC // 128, S // 64, S // 16
    NCC = max(NCM // 128, 1)
    t = {}
    q = np.arange(128)
    cand = np.zeros((TCT, 128, NSEL), np.float32)
    forced = np.zeros((TCT, 128, NSEL), np.float32)
    j = np.arange(NSEL)
    thr_c = np.zeros((TCT, NCC), np.float32); thr_d = np.zeros((TCT, 4), np.float32); thr_w = np.zeros((TCT, 5), np.float32)
    thrq = np.zeros((128, TCT), np.float32)
    for m in range(TCT):
        i = TCT * qd + m
        cur = (128 * i + q) // 64
        valid = j[None, :] <= cur[:, None]
        f = (j[None, :] == 0) | (j[None, :] == cur[:, None]) | (j[None, :] == cur[:, None] - 1)
        f = f & valid
        cand[m] = (valid & ~f).astype(np.float32)
        forced[m] = f.astype(np.float32) * BIGV
        for c in range(NCC):
            thr_c[m, c] = 128.0 * (i - 16 * c)
        for r in range(4):
            thr_d[m, r] = 0.0 if r == qd else BIGV
        for c in range(5):
            ok = (i - 4 + c) >= 0
            if c == 0:
                thr_w[m, c] = 0.0 if ok else BIGV
            elif c == 4:
                thr_w[m, c] = 0.0
            else:
                thr_w[m, c] = BIGV if ok else -BIGV
        thrq[:, m] = 128.0 * i + q - 31.0
    row = np.concatenate([thr_c.ravel(), thr_d.ravel(), thr_w.ravel()]).astype(np.float32)
    t["thr"] = np.tile(row[None, :], (128, 1))
    t["thrq"] = thrq
    t["cand"] = cand
    t["forced"] = forced
    fix = np.ones((128, 2, 16), np.float32)
    if qd == 0:
        for cc in range(2):
            for hh in range(2):
                w = (2, 4, 8, 16)[2 * cc + hh]
                fix[hh * 64:(hh + 1) * 64, cc, :] = w / np.minimum(np.arange(16) + 1, w)
    t["poolfix"] = fix
    NLC = S // 512
    flag = np.zeros((128, NLC), np.float32)
    flag[:, NLC - ((qd + 1) * TC) // 512:] = 1.0
    t["lruflag"] = flag
    return t

def chT(a):
    return np.ascontiguousarray(a.T.reshape(2, 128, -1).transpose(1, 0, 2))

def layout_B(qd, S, TC, f32p, qkv, gates):
    t0 = qd * TC
    TCT, NCH = TC // 128, S // 128
    d = {}
    xp, xl, gl = f32p[:, 0:256], f32p[:, 256:512], f32p[:, 512:768]
    d["xpT"] = chT(np.concatenate([np.zeros((16, 256), np.float32), xp])[t0:t0 + 16 + TC])
    d["xlT"] = chT(np.concatenate([np.zeros((4 + S, 256), np.float32), xl])[t0 + TC:t0 + TC + 4 + S])
    d["glT"] = chT(gl[t0:t0 + TC])
    q = qkv[:, 0:512]
    kc, vc, ks, vs, kw, vw = (qkv[:, 512 + 128 * i:640 + 128 * i] for i in range(6))
    d["QT"] = np.ascontiguousarray(q[t0:t0 + TC].reshape(TC, 2, 4, 64).transpose(1, 3, 2, 0).reshape(128, 4, TC))
    d["KsT"] = np.ascontiguousarray(ks.T)
    d["Vs"] = np.ascontiguousarray(vs.reshape(NCH, 128, 128).transpose(1, 0, 2))
    d["KcT"] = np.ascontiguousarray(kc.T)
    d["VcT"] = np.ascontiguousarray(vc.T)
    z = np.zeros((512, 128), kw.dtype)
    d["KwT"] = np.ascontiguousarray(np.concatenate([z, kw])[t0:t0 + 512 + TC].T)
    d["Vw"] = np.ascontiguousarray(np.concatenate([z, vw])[t0:t0 + 512 + TC].reshape(4 + TCT, 128, 128).transpose(1, 0, 2))
    d["gate"] = np.ascontiguousarray(gates[t0:t0 + TC].reshape(TCT, 128, 24).transpose(1, 0, 2))
    return d


from concourse.bass_utils import run_bass_kernel_spmd

_MK_S = 16384
_PROGS = {}


def _build(kind, S, TC):
    TCT, NCH, NSEL, NCM = TC // 128, S // 128, S // 64, S // 16
    NCC = max(NCM // 128, 1)
    NTHR = TCT * NCC + TCT * 4 + TCT * 5
    nc = bass.Bass("TRN2", target_bir_lowering=False)
    dt = lambda n, s, d, k="ExternalInput": nc.dram_tensor(n, s, d, kind=k).ap()
    P = Prog(nc)
    fin = []
    with ExitStack() as es:
        cx = Ctx(nc, P, es)
        ident = dt("ident", [128, 128], BF16)
        load_consts(cx, es, ident)
        has_B = kind in ("BA", "B")
        has_A = kind in ("A", "BA")
        t_h = P.tok("h0")
        if has_B:
            a = dict(QT=dt("QT", [128, 4, TC], BF16), KsT=dt("KsT", [128, S], BF16), Vs=dt("Vs", [128, NCH, 128], BF16),
                     KcT=dt("KcT", [128, S], BF16), VcT=dt("VcT", [128, S], BF16), KwT=dt("KwT", [128, 512 + TC], BF16),
                     Vw=dt("Vw", [128, 4 + TCT, 128], BF16), gate=dt("gate", [128, TCT, 24], F32),
                     Ebig=dt("Ebig", [128, 8192], BF16), Akq=dt("Akq", [128, 128], F32), Ac=dt("Ac", [128, 128], F32),
                     identf=dt("identf", [128, 128], F32), iota16=dt("iota16", [128, NCM], F32),
                     thr=dt("thr", [128, NTHR], F32), thrq=dt("thrq", [128, TCT], F32),
                     cand=dt("cand", [TCT, 128, NSEL], F32), forced=dt("forced", [TCT, 128, NSEL], F32),
                     cmp_w_k=dt("cmp_w_k", [32, 64, 64], F32), cmp_w_v=dt("cmp_w_v", [32, 64, 64], F32),
                     cmp_pe=dt("cmp_pe", [32, 64], F32))
            xpT = dt("xpT", [128, 2, 16 + TC], F32); poolfix = dt("poolfix", [128, 2, 16], F32)
            pool_w = dt("pool_w", [4, 64, 64], F32); pool_scale = dt("pool_scale", [256], F32)
            xlT = dt("xlT", [128, 2, 4 + S], F32); glT = dt("glT", [128, 2, TC], F32)
            lruflag = dt("lruflag", [128, S // 512], F32)
            conv_w = dt("conv_w", [4, 256], F32); conv_b = dt("conv_b", [256], F32)
            w_r = dt("lru_w_r", [4, 64, 64], F32); b_r = dt("lru_b_r", [256], F32)
            w_i = dt("lru_w_i", [4, 64, 64], F32); b_i = dt("lru_b_i", [256], F32); lam = dt("lru_lambda", [256], F32)
            w_out = dt("w_out", [D, D], F32); mix_post_g = dt("mix_post_g", [D], F32)
            h1 = dt("h1_in", [TC, D], F32)
            f2 = dict(wg=dt("f2_wg", [D, DFF], F32), wu=dt("f2_wu", [D, DFF], F32), wd=dt("f2_wd", [DFF, D], F32),
                      gpre=dt("f2_gpre", [D], F32), gpost=dt("f2_gpost", [D], F32))
            yT = dt("yT_s", [128, 8, TC], BF16, "Internal")
            h_mid = dt("h_mid_s", [TC, D], F32, "Internal")
            t_yT = P.tok("yT")
            emit_pool(cx, xpT, poolfix, pool_w, pool_scale, yT, t_yT, TC)
            emit_lru(cx, xlT, glT, lruflag, conv_w, conv_b, w_r, b_r, w_i, b_i, lam, yT, t_yT, S, TC)
            emit_attn(cx, a, yT, t_yT, S, TC)
            t_hm = emit_wout(cx, yT, t_yT, w_out, mix_post_g, h1, P.tok("h1in"), h_mid, TC)
            if has_A:
                h2 = dt("h2_s", [TC, D], F32, "Internal")
            else:
                h2 = dt("h_out", [TC, D], F32, "ExternalOutput")
            t_h = emit_ffn(cx, h_mid, t_hm, h2, f2["wg"], f2["wu"], f2["wd"], f2["gpre"], f2["gpost"], TC, "f2")
            h_cur = h2
            if not has_A:
                fin.append(t_h)
        else:
            h_cur = dt("h_in", [TC, D], F32)
        if has_A:
            f1 = dict(wg=dt("f1_wg", [D, DFF], F32), wu=dt("f1_wu", [D, DFF], F32), wd=dt("f1_wd", [DFF, D], F32),
                      gpre=dt("f1_gpre", [D], F32), gpost=dt("f1_gpost", [D], F32))
            w_in = dt("w_in", [D, IN_COLS], F32); mix_pre_g = dt("mix_pre_g", [D], F32)
            pos = dt("pos", [TC], I32); invtab = dt("invtab", [128, 32], F32)
            h1o = dt("h1_out", [TC, D], F32, "ExternalOutput")
            o_f32 = dt("o_f32", [TC, 768], F32, "ExternalOutput")
            o_qkv = dt("o_qkv", [TC, 1280], BF16, "ExternalOutput")
            o_gate = dt("o_gate", [TC, 24], F32, "ExternalOutput")
            t_h1 = emit_ffn(cx, h_cur, t_h, h1o, f1["wg"], f1["wu"], f1["wd"], f1["gpre"], f1["gpost"], TC, "f1")
            t_o = emit_proj(cx, h1o, t_h1, w_in, mix_pre_g, pos, invtab, o_f32, o_qkv, o_gate, TC, "pj")
            fin += [t_h1] + list(t_o)
        P.emit(fin)
    return nc


def _get_prog(kind, S, TC):
    key = (kind, S, TC)
    if key not in _PROGS:
        _PROGS[key] = _build(kind, S, TC)
    return _PROGS[key]


_RUN = [None]


def _run(nc, in_maps):
    if _RUN[0] is not None:
        return _RUN[0](nc, in_maps)
    return run_bass_kernel_spmd(nc, in_maps, core_ids=list(range(len(in_maps)))).results


def kernel(**inp):
    x = np.asarray(inp["x"])
    NBt, S, _ = x.shape
    NQ = 4
    TC = S // NQ
    ncores = NBt * NQ
    cs = consts(S)
    g = lambda k, l: np.ascontiguousarray(np.asarray(inp[k])[l])
    positions = np.asarray(inp["positions"]).astype(np.int32)

    def a_inputs(l):
        return dict(f1_wg=g("ffn1_w_gate", l), f1_wu=g("ffn1_w_up", l), f1_wd=g("ffn1_w_down", l),
                    f1_gpre=g("ffn1_pre_g", l), f1_gpost=g("ffn1_post_g", l), w_in=g("w_in", l),
                    mix_pre_g=g("mix_pre_g", l), invtab=cs["invtab"])

    def b_inputs(l):
        d = dict(f2_wg=g("ffn2_w_gate", l), f2_wu=g("ffn2_w_up", l), f2_wd=g("ffn2_w_down", l),
                 f2_gpre=g("ffn2_pre_g", l), f2_gpost=g("ffn2_post_g", l), w_out=g("w_out", l),
                 mix_post_g=g("mix_post_g", l))
        for k in ("pool_w", "pool_scale", "conv_w", "conv_b", "lru_w_r", "lru_b_r", "lru_w_i", "lru_b_i",
                  "lru_lambda", "cmp_w_k", "cmp_w_v", "cmp_pe"):
            d[k] = g(k, l)
        for k in ("Ebig", "Akq", "Ac", "identf", "iota16"):
            d[k] = cs[k]
        return d

    tabs = [tables(qd, S, TC) for qd in range(NQ)]

    def gather(res, key, width, dtype):
        out = np.zeros((NBt, S, width), dtype)
        for c in range(ncores):
            b, qd = divmod(c, NQ)
            out[b, qd * TC:(qd + 1) * TC] = np.asarray(res[c][key]).reshape(TC, width)
        return out

    base = a_inputs(0)
    in_maps = []
    for c in range(ncores):
        b, qd = divmod(c, NQ)
        m = dict(base)
        m["ident"] = cs["ident"]
        m["h_in"] = np.ascontiguousarray(x[b, qd * TC:(qd + 1) * TC])
        m["pos"] = np.ascontiguousarray(positions[b, qd * TC:(qd + 1) * TC])
        in_maps.append(m)
    res = _run(_get_prog("A", S, TC), in_maps)
    DEPTH = np.asarray(inp["w_in"]).shape[0]
    for l in range(DEPTH):
        f32p = gather(res, "o_f32", 768, np.float32)
        qkv = gather(res, "o_qkv", 1280, BF)
        gates = gather(res, "o_gate", 24, np.float32)
        last = (l == DEPTH - 1)
        base = b_inputs(l)
        if not last:
            base.update(a_inputs(l + 1))
        in_maps = []
        for c in range(ncores):
            b, qd = divmod(c, NQ)
            m = dict(base)
            m["ident"] = cs["ident"]
            m.update(layout_B(qd, S, TC, f32p[b], qkv[b], gates[b]))
            for k in ("thr", "thrq", "cand", "forced", "poolfix", "lruflag"):
                m[k] = tabs[qd][k]
            m["h1_in"] = np.asarray(res[c]["h1_out"]).reshape(TC, D)
            if not last:
                m["pos"] = np.ascontiguousarray(positions[b, qd * TC:(qd + 1) * TC])
            in_maps.append(m)
        res = _run(_get_prog("B" if last else "BA", S, TC), in_maps)
    out = gather(res, "h_out", D, np.float32)
    return out
```

```python
import numpy as np
import concourse.bass as bass
import concourse.mybir as mybir

F32 = mybir.dt.float32
BF16 = mybir.dt.bfloat16
I32 = mybir.dt.int32
ALU = mybir.AluOpType
AF = mybir.ActivationFunctionType
AX = mybir.AxisListType


class Tok:
    __slots__ = ("name", "last_w", "readers", "sem", "dma_cnt", "fslot")

    def __init__(self, name):
        self.name = name
        self.last_w = None
        self.readers = []
        self.sem = None
        self.dma_cnt = 0


class Op:
    __slots__ = ("eng", "fn", "deps", "signal", "idx", "is_dma", "dtok", "dval", "dslot")

    def __init__(self, eng, fn, is_dma=False):
        self.eng = eng
        self.fn = fn
        self.deps = []
        self.signal = False
        self.idx = None
        self.is_dma = is_dma
        self.dtok = None
        self.dval = None


ENGS = ("pe", "act", "dve", "pool", "sp")


class Prog:
    def __init__(self, nc):
        self.nc = nc
        self.ops = []
        self.nslots = 0
        self.slot_cnt = {}
        self.free_slots = []
        self.active = []

    def tok(self, name):
        return Tok(name)

    def toks(self, name, n):
        return [Tok(f"{name}{i}") for i in range(n)]

    def _track(self, op, reads, writes):
        deps = []
        for t in reads:
            if t.last_w is not None:
                deps.append(t.last_w)
        for t in writes:
            if t.last_w is not None:
                deps.append(t.last_w)
            deps.extend(t.readers)
        seen = set()
        for d in deps:
            if d is op or id(d) in seen:
                continue
            seen.add(id(d))
            if d.eng == "pe" and op.eng == "pe" and not d.is_dma and not op.is_dma:
                continue
            op.deps.append(d)
            if not d.is_dma:
                d.signal = True
        for t in writes:
            t.last_w = op
            t.readers = []
        for t in reads:
            if t not in writes:
                t.readers.append(op)

    def add(self, eng, fn, reads=(), writes=()):
        op = Op(eng, fn)
        self._track(op, list(reads), list(writes))
        self.ops.append(op)
        return op

    def dma(self, eng, out, in_, reads, write, **kw):
        op = Op(eng, lambda e: e.dma_start(out=out, in_=in_, **kw), is_dma=True)
        self._track(op, list(reads), [write])
        if write.sem is None:
            if self.free_slots:
                write.sem = self.free_slots.pop()
            else:
                write.sem = self.nslots
                self.nslots += 1
                self.slot_cnt[write.sem] = 0
            self.active.append(write)
        self.slot_cnt[write.sem] += 16
        op.dtok = write
        op.dslot = write.sem
        op.dval = self.slot_cnt[write.sem]
        write.dma_cnt = op.dval
        self.ops.append(op)
        return op

    def barrier(self):
        last = {}
        for op in self.ops:
            if op.fn is None:
                continue
            if op.is_dma:
                last[("d", op.dslot)] = op
            else:
                last[("e", op.eng)] = op
        deps = list(last.values())
        for t in self.active:
            self.free_slots.append(t.sem)
            t.fslot = t.sem
            t.sem = None
        self.active = []
        for e in ENGS:
            if e == 'pool':
                continue
            b = Op(e, None)
            for d in deps:
                b.deps.append(d)
                if not d.is_dma:
                    d.signal = True
            self.ops.append(b)

    def emit(self, final_toks=()):
        nc = self.nc
        from contextlib import ExitStack
        with ExitStack() as es:
            esem = {e: es.enter_context(nc.semaphore(f"s_{e}")) for e in ENGS}
            dsem = [es.enter_context(nc.semaphore(f"d_{i}")) for i in range(self.nslots)]
            cnt = {e: 0 for e in ENGS}
            waited = {e: {} for e in ENGS}
            streams = {e: [] for e in ENGS}
            nwaits = 0
            for op in self.ops:
                waits = []
                for d in op.deps:
                    if d.is_dma:
                        sem, val = dsem[d.dslot], d.dval
                    else:
                        sem, val = esem[d.eng], d.idx
                    key = id(sem)
                    if waited[op.eng].get(key, 0) >= val:
                        continue
                    waited[op.eng][key] = val
                    waits.append((sem, val))
                nwaits += len(waits)
                inc = None
                if op.is_dma:
                    inc = (dsem[op.dslot], 16)
                elif op.signal:
                    cnt[op.eng] += 1
                    op.idx = cnt[op.eng]
                    inc = (esem[op.eng], 1)
                streams[op.eng].append((waits, op.fn, inc))
            fin = []
            for t in final_toks:
                fin.append((dsem[t.sem if t.sem is not None else t.fslot], t.dma_cnt))
            self.stats = dict(n_ops=len(self.ops), n_waits=nwaits,
                              per_eng={e: len(streams[e]) for e in ENGS})

            def run(eng_obj, items, final=None):
                for waits, fn, inc in items:
                    for sem, val in waits:
                        eng_obj.wait_ge(sem, val)
                    if fn is None:
                        continue
                    ins = fn(eng_obj)
                    if inc is not None:
                        ins.then_inc(inc[0], inc[1])
                if final:
                    for sem, val in final:
                        eng_obj.wait_ge(sem, val)

            with nc.Block() as block:
                @block.tensor
                def _(e):
                    run(e, streams["pe"])

                @block.scalar
                def _(e):
                    run(e, streams["act"])

                @block.vector
                def _(e):
                    run(e, streams["dve"])

                @block.gpsimd
                def _(e):
                    run(e, streams["pool"])

                @block.sync
                def _(e):
                    run(e, streams["sp"], fin)


import math
import numpy as np
from contextlib import ExitStack

D = 1024
DFF = 2816
NF = DFF // 128
ST = 512
EPS = 1e-6
IN_COLS = 2072
PI = math.pi


class Ctx:
    def __init__(self, nc, P, es):
        self.nc, self.P, self.es = nc, P, es
        self.psb = []
        for i in range(8):
            t = es.enter_context(nc.psum_tensor(f"ps{i}", [128, 512], F32))
            self.psb.append((t, P.tok(f"ps{i}")))
        self.n = 0

    def sb(self, es, name, shape, dt):
        self.n += 1
        return es.enter_context(self.nc.sbuf_tensor(f"{name}_{self.n}", shape, dt))


def load_consts(cx, es, ident_d):
    P = cx.P
    cx.ident = cx.sb(es, "ident", [128, 128], BF16)
    cx.t_ident = P.tok("ident")
    P.dma("sp", cx.ident[:], ident_d, [], cx.t_ident)


def load_cast(cx, dst, t_dst, src, stg, t_stg, q, eng):
    P = cx.P
    P.dma(q, stg, src, [], t_stg)
    if eng == "act":
        P.add("act", lambda e: e.activation(out=dst, in_=stg, func=AF.Copy), [t_stg], [t_dst])
    else:
        P.add("dve", lambda e: e.tensor_copy(out=dst, in_=stg), [t_stg], [t_dst])


class NormT:
    def __init__(self, cx, es, g_dram, tag):
        P = cx.P
        self.cx = cx
        self.gcol = cx.sb(es, tag + "gcol", [128, 8], F32)
        self.t_gcol = P.tok("gcol")
        P.dma("sp", self.gcol[:], g_dram.rearrange("(k p) -> p k", p=128), [], self.t_gcol,
              allow_slow_non_contiguous=True)
        self.xs = [cx.sb(es, tag + "xs", [128, D], BF16) for _ in range(4)]
        self.t_xs = P.toks("xs", 4)
        self.junk = cx.sb(es, tag + "junk", [128, D], BF16)
        self.ss = cx.sb(es, tag + "ss", [128, 8], F32)
        self.rstd = cx.sb(es, tag + "rstd", [128, 8], F32)
        self.xnT = cx.sb(es, tag + "xnT", [128, 8, ST], BF16)
        self.t_junk, self.t_ss, self.t_rstd, self.t_xnT = P.tok("junk"), P.tok("ss"), P.tok("rstd"), P.tok("xnT")

    def run(self, xb, txb):
        cx, P = self.cx, self.cx.P
        ss, rstd, junk, xs, xnT = self.ss, self.rstd, self.junk, self.xs, self.xnT
        P.add("dve", lambda e: e.memset(ss[:], 0.0), [], [self.t_ss])
        for j in range(4):
            P.add("act", lambda e, j=j: e.activation(out=junk[:], in_=xb[:, j, :], func=AF.Square,
                                                     accum_out=ss[:, j:j + 1]), [txb], [self.t_junk, self.t_ss])
        P.add("dve", lambda e: e.tensor_scalar(out=rstd[:, 0:4], in0=ss[:, 0:4], scalar1=1.0 / D, scalar2=EPS,
                                               op0=ALU.mult, op1=ALU.add), [self.t_ss], [self.t_rstd])
        P.add("act", lambda e: e.activation(out=rstd[:, 4:8], in_=rstd[:, 0:4], func=AF.Sqrt),
              [self.t_rstd], [self.t_rstd])
        P.add("dve", lambda e: e.reciprocal(out=rstd[:, 0:4], in_=rstd[:, 4:8]), [self.t_rstd], [self.t_rstd])
        for j in range(4):
            P.add("dve", lambda e, j=j: e.tensor_scalar(out=xs[j][:], in0=xb[:, j, :], scalar1=rstd[:, j:j + 1],
                                                        scalar2=None, op0=ALU.mult),
                  [txb, self.t_rstd], [self.t_xs[j]])
        for k in range(8):
            pt, tpt = cx.psb[6 + (k % 2)]
            for j in range(4):
                P.add("pe", lambda e, j=j, k=k, pt=pt: e.matmul(pt[:, j * 128:(j + 1) * 128],
                                                               lhsT=xs[j][:, k * 128:(k + 1) * 128],
                                                               rhs=cx.ident[:], start=True, stop=True),
                      [self.t_xs[j], cx.t_ident], [tpt])
            P.add("act", lambda e, k=k, pt=pt: e.activation(out=xnT[:, k, :], in_=pt[:], func=AF.Copy,
                                                            scale=self.gcol[:, k:k + 1]),
                  [tpt, self.t_gcol], [self.t_xnT])
        return xnT, self.t_xnT


class Tail:
    def __init__(self, cx, es, g_dram, scale, tag):
        P = cx.P
        self.cx = cx
        self.gpb = cx.sb(es, tag + "gpb", [128, D], F32)
        self.t_gpb = P.tok("gpb")
        P.dma("sp", self.gpb[:], g_dram.partition_broadcast(128), [], self.t_gpb)
        if scale != 1.0:
            P.add("dve", lambda e: e.tensor_scalar(out=self.gpb[:], in0=self.gpb[:], scalar1=scale, scalar2=None,
                                                   op0=ALU.mult), [self.t_gpb], [self.t_gpb])
        self.y = [cx.sb(es, tag + "y", [128, D], F32) for _ in range(2)]
        self.t_y = P.toks("y", 2)
        self.ot = [cx.sb(es, tag + "ot", [128, D], F32) for _ in range(2)]
        self.t_ot = P.toks("ot", 2)
        self.junk = cx.sb(es, tag + "junk2", [128, D], BF16)
        self.ss2 = cx.sb(es, tag + "ss2", [128, 4], F32)
        self.rstd2 = cx.sb(es, tag + "rstd2", [128, 2], F32)
        self.t_junk, self.t_ss2, self.t_rstd2 = P.tok("junk2"), P.tok("ss2"), P.tok("rstd2")
        self.cnt = 0

    def run(self, lhs_fn, nk, w_fn, lhs_toks, t_w, resid, t_resid, out_ap, t_out):
        cx, P = self.cx, self.cx.P
        i = self.cnt
        self.cnt += 1
        yb, tyb = self.y[i % 2], self.t_y[i % 2]
        ob, tob = self.ot[i % 2], self.t_ot[i % 2]
        ss2, rstd2, junk, gpb = self.ss2, self.rstd2, self.junk, self.gpb
        for hf in range(2):
            py, tpy = cx.psb[4 + hf]
            for k in range(nk):
                P.add("pe", lambda e, k=k, hf=hf, py=py: e.matmul(py[:], lhsT=lhs_fn(k), rhs=w_fn(k, hf),
                                                                 start=(k == 0), stop=(k == nk - 1)),
                      list(lhs_toks) + [t_w], [tpy])
            P.add("dve", lambda e, py=py, hf=hf: e.tensor_copy(out=yb[:, hf * 512:(hf + 1) * 512], in_=py[:]),
                  [tpy], [tyb])
        P.add("dve", lambda e: e.memset(ss2[:], 0.0), [], [self.t_ss2])
        P.add("act", lambda e: e.activation(out=junk[:], in_=yb[:], func=AF.Square, accum_out=ss2[:, 0:1]),
              [tyb], [self.t_junk, self.t_ss2])
        P.add("dve", lambda e: e.tensor_scalar(out=ss2[:, 1:2], in0=ss2[:, 0:1], scalar1=1.0 / D, scalar2=EPS,
                                               op0=ALU.mult, op1=ALU.add), [self.t_ss2], [self.t_ss2])
        P.add("act", lambda e: e.activation(out=ss2[:, 2:3], in_=ss2[:, 1:2], func=AF.Sqrt),
              [self.t_ss2], [self.t_ss2])
        P.add("dve", lambda e: e.reciprocal(out=rstd2[:, 0:1], in_=ss2[:, 2:3]), [self.t_ss2], [self.t_rstd2])
        P.add("dve", lambda e: e.scalar_tensor_tensor(out=ob[:], in0=yb[:], scalar=rstd2[:, 0:1], in1=gpb[:],
                                                      op0=ALU.mult, op1=ALU.mult),
              [tyb, self.t_rstd2, self.t_gpb], [tob])
        P.add("dve", lambda e: e.tensor_tensor(out=ob[:], in0=ob[:], in1=resid, op=ALU.add),
              [tob, t_resid], [tob])
        P.dma("sp", out_ap, ob[:], [tob], t_out)


def emit_ffn(cx, h_in, t_hin, h_out, wg, wu, wd, gpre, gpost, NT, tag):
    P, nc = cx.P, cx.nc
    nst = NT // ST
    t_hout = P.tok("hout")
    with ExitStack() as es:
        sb = lambda name, shape, dt: cx.sb(es, tag + name, shape, dt)
        nt = NormT(cx, es, gpre, tag)
        tl = Tail(cx, es, gpost, 0.5, tag)
        wd_sb = sb("wd", [128, NF, D], BF16)
        t_wd = P.tok("wd")
        wgu = [sb("wgu", [128, 2, 8, 256], BF16) for _ in range(2)]
        t_wgu = P.toks("wgu", 2)
        stg = [[sb("stg", [128, 8, 256], F32) for _ in range(2)] for _ in range(2)]
        t_stg = [[P.tok("stg") for _ in range(2)] for _ in range(2)]
        xt = [sb("xt", [128, 4, D], F32) for _ in range(2)]
        t_xt = P.toks("xt", 2)
        act = sb("act", [128, NF, ST], BF16)
        t_act = P.toks("act", NF)
        sil = [sb("sil", [128, ST], F32) for _ in range(2)]
        t_sil = P.toks("sil", 2)
        wd_v = wd.rearrange("(f p) d -> p f d", p=128)
        for ii, f0 in enumerate(range(0, NF, 2)):
            sg, tsg = stg[ii % 2][(ii // 2) % 2], t_stg[ii % 2][(ii // 2) % 2]
            sgv = sg[:].rearrange("p k c -> p (k c)").rearrange("p (f d) -> p f d", f=2)
            load_cast(cx, wd_sb[:, f0:f0 + 2, :], t_wd, wd_v[:, f0:f0 + 2, :], sgv, tsg,
                      "sp" if ii % 2 == 0 else "act", "act" if ii % 2 == 0 else "dve")
        wg_v = wg.rearrange("(k p) c -> p k c", p=128)
        wu_v = wu.rearrange("(k p) c -> p k c", p=128)
        h_in_v = h_in.rearrange("(s j p) d -> s p j d", p=128, j=4)
        h_out_v = h_out.rearrange("(s j p) d -> s j p d", p=128, j=4)

        def load_x(s):
            P.dma("sp", xt[s % 2][:], h_in_v[s], [t_hin], t_xt[s % 2])

        def load_w(g, slot):
            load_cast(cx, wgu[slot][:, 0], t_wgu[slot], wg_v[:, :, g * 256:(g + 1) * 256], stg[slot][0][:],
                      t_stg[slot][0], "sp", "act")
            load_cast(cx, wgu[slot][:, 1], t_wgu[slot], wu_v[:, :, g * 256:(g + 1) * 256], stg[slot][1][:],
                      t_stg[slot][1], "act", "dve")

        gi = 0
        load_x(0)
        load_w(0, 0)
        for s in range(nst):
            xb, txb = xt[s % 2], t_xt[s % 2]
            if s + 1 < nst:
                load_x(s + 1)
            xnT, t_xnT = nt.run(xb, txb)
            for g in range(NF // 2):
                slot = gi % 2
                gi += 1
                if g + 1 < NF // 2:
                    load_w(g + 1, gi % 2)
                elif s + 1 < nst:
                    load_w(0, gi % 2)
                for c in range(2):
                    f = g * 2 + c
                    pg, tpg = cx.psb[f % 2]
                    pu, tpu = cx.psb[2 + f % 2]
                    for k in range(8):
                        P.add("pe", lambda e, k=k, c=c, pg=pg, slot=slot: e.matmul(
                            pg[:], lhsT=wgu[slot][:, 0, k, c * 128:(c + 1) * 128], rhs=xnT[:, k, :],
                            start=(k == 0), stop=(k == 7)), [t_wgu[slot], t_xnT], [tpg])
                    for k in range(8):
                        P.add("pe", lambda e, k=k, c=c, pu=pu, slot=slot: e.matmul(
                            pu[:], lhsT=wgu[slot][:, 1, k, c * 128:(c + 1) * 128], rhs=xnT[:, k, :],
                            start=(k == 0), stop=(k == 7)), [t_wgu[slot], t_xnT], [tpu])
                    sl, tsl = sil[f % 2], t_sil[f % 2]
                    P.add("act", lambda e, pg=pg, sl=sl: e.activation(out=sl[:], in_=pg[:], func=AF.Silu),
                          [tpg], [tsl])
                    P.add("dve", lambda e, pu=pu, sl=sl, f=f: e.tensor_tensor(out=act[:, f, :], in0=sl[:],
                                                                             in1=pu[:], op=ALU.mult),
                          [tsl, tpu], [t_act[f]])
            for j in range(4):
                tl.run(lambda k, j=j: act[:, k, j * 128:(j + 1) * 128], NF,
                       lambda k, hf: wd_sb[:, k, hf * 512:(hf + 1) * 512], t_act, t_wd,
                       xb[:, j, :], txb, h_out_v[s, j], t_hout)
        P.barrier()
    return t_hout


def emit_proj(cx, h_in, t_hin, w_in, gpre, pos, invtab_d, o_f32, o_qkv, o_gate, NT, tag):
    P, nc = cx.P, cx.nc
    nst = NT // ST
    t_o = [P.tok("of32"), P.tok("oqkv"), P.tok("ogate")]
    with ExitStack() as es:
        sb = lambda name, shape, dt: cx.sb(es, tag + name, shape, dt)
        nt = NormT(cx, es, gpre, tag)
        win = sb("win", [128, 8, IN_COLS], BF16)
        t_win = P.tok("win")
        stg = [sb("stg", [128, IN_COLS], F32) for _ in range(2)]
        t_stg = P.toks("stg", 2)
        w_v = w_in.rearrange("(k p) c -> p k c", p=128)
        for k in range(8):
            load_cast(cx, win[:, k, :], t_win, w_v[:, k, :], stg[k % 2][:], t_stg[k % 2],
                      "sp" if k % 2 == 0 else "act", "act" if k % 2 == 0 else "dve")
        xt = [sb("xt", [128, 4, D], F32) for _ in range(2)]
        t_xt = P.toks("xt", 2)
        invtab = sb("invtab", [128, 32], F32)
        t_inv = P.tok("inv")
        P.dma("sp", invtab[:], invtab_d, [], t_inv)
        posi = sb("posi", [128, NT // 128], I32)
        posf = sb("posf", [128, NT // 128], F32)
        t_pos = P.tok("pos")
        P.dma("sp", posi[:], pos.rearrange("(t p) -> p t", p=128), [], t_pos, allow_slow_non_contiguous=True)
        P.add("dve", lambda e: e.tensor_copy(out=posf[:], in_=posi[:]), [t_pos], [t_pos])
        pr = [sb("pr", [128, IN_COLS], F32) for _ in range(2)]
        t_pr = P.toks("pr", 2)
        pof = [sb("pof", [128, 768], F32) for _ in range(2)]
        poq = [sb("poq", [128, 1280], BF16) for _ in range(2)]
        pog = [sb("pog", [128, 24], F32) for _ in range(2)]
        t_pof, t_poq, t_pog = P.toks("pof", 2), P.toks("poq", 2), P.toks("pog", 2)
        tr = {n: sb(n, [128, 32], F32) for n in ("ang", "kf", "r", "fl", "sin", "cos", "ang2")}
        ki = sb("ki", [128, 32], I32)
        t_trig = P.tok("trig")
        tmp = [sb("tmp", [128, 10, 32], F32) for _ in range(4)]
        t_tmp = P.tok("tmp")
        h_in_v = h_in.rearrange("(s j p) d -> s p j d", p=128, j=4)
        of_v = o_f32.rearrange("(t p) c -> t p c", p=128)
        oq_v = o_qkv.rearrange("(t p) c -> t p c", p=128)
        og_v = o_gate.rearrange("(t p) c -> t p c", p=128)
        cgroups = [(0, 512), (512, 512), (1024, 512), (1536, 512), (2048, 24)]

        def load_x(s):
            P.dma("sp", xt[s % 2][:], h_in_v[s], [t_hin], t_xt[s % 2])

        def trig(dst, src, tt):
            ang, kf, r, fl = src, tr["kf"], tr["r"], tr["fl"]
            P.add("dve", lambda e: e.tensor_scalar(out=kf[:], in0=ang[:], scalar1=1.0 / (2 * PI), scalar2=None,
                                                   op0=ALU.mult), [tt], [tt])
            P.add("dve", lambda e: e.tensor_copy(out=ki[:], in_=kf[:]), [tt], [tt])
            P.add("dve", lambda e: e.tensor_copy(out=kf[:], in_=ki[:]), [tt], [tt])
            P.add("dve", lambda e: e.scalar_tensor_tensor(out=r[:], in0=kf[:], scalar=-2 * PI, in1=ang[:],
                                                          op0=ALU.mult, op1=ALU.add), [tt], [tt])
            P.add("dve", lambda e: e.tensor_scalar(out=fl[:], in0=r[:], scalar1=PI, scalar2=None, op0=ALU.is_gt),
                  [tt], [tt])
            P.add("dve", lambda e: e.scalar_tensor_tensor(out=kf[:], in0=fl[:], scalar=-2 * PI, in1=r[:],
                                                          op0=ALU.mult, op1=ALU.add), [tt], [tt])
            P.add("dve", lambda e: e.tensor_scalar(out=fl[:], in0=kf[:], scalar1=-PI, scalar2=None, op0=ALU.is_lt),
                  [tt], [tt])
            P.add("dve", lambda e: e.scalar_tensor_tensor(out=r[:], in0=fl[:], scalar=2 * PI, in1=kf[:],
                                                          op0=ALU.mult, op1=ALU.add), [tt], [tt])
            P.add("act", lambda e: e.activation(out=dst[:], in_=r[:], func=AF.Sin), [tt], [tt])

        load_x(0)
        ti = 0
        for s in range(nst):
            xb, txb = xt[s % 2], t_xt[s % 2]
            if s + 1 < nst:
                load_x(s + 1)
            xnT, t_xnT = nt.run(xb, txb)
            for j in range(4):
                tix = s * 4 + j
                prb, tprb = pr[tix % 2], t_pr[tix % 2]
                for gi_, (c0, cw) in enumerate(cgroups):
                    pp, tpp = cx.psb[gi_ % 4]
                    for k in range(8):
                        P.add("pe", lambda e, k=k, j=j, c0=c0, cw=cw, pp=pp: e.matmul(
                            pp[:, 0:cw], lhsT=xnT[:, k, j * 128:(j + 1) * 128], rhs=win[:, k, c0:c0 + cw],
                            start=(k == 0), stop=(k == 7)), [t_xnT, t_win], [tpp])
                    if gi_ % 2 == 0:
                        P.add("act", lambda e, pp=pp, c0=c0, cw=cw, prb=prb: e.activation(
                            out=prb[:, c0:c0 + cw], in_=pp[:, 0:cw], func=AF.Copy), [tpp], [tprb])
                    else:
                        P.add("dve", lambda e, pp=pp, c0=c0, cw=cw, prb=prb: e.tensor_copy(
                            out=prb[:, c0:c0 + cw], in_=pp[:, 0:cw]), [tpp], [tprb])
                ang = tr["ang"]
                P.add("dve", lambda e, tix=tix: e.tensor_scalar(out=ang[:], in0=invtab[:],
                                                                scalar1=posf[:, tix:tix + 1], scalar2=None,
                                                                op0=ALU.mult), [t_inv, t_pos], [t_trig])
                trig(tr["sin"], ang, t_trig)
                P.add("dve", lambda e: e.tensor_scalar(out=tr["ang2"][:], in0=ang[:], scalar1=PI / 2, scalar2=None,
                                                       op0=ALU.add), [t_trig], [t_trig])
                trig(tr["cos"], tr["ang2"], t_trig)
                fo, tfo = pof[tix % 2], t_pof[tix % 2]
                qo, tqo = poq[tix % 2], t_poq[tix % 2]
                go, tgo = pog[tix % 2], t_pog[tix % 2]
                P.add("act", lambda e, fo=fo, prb=prb: e.activation(out=fo[:], in_=prb[:, 0:768], func=AF.Copy),
                      [tprb], [tfo])
                P.add("act", lambda e, go=go, prb=prb: e.activation(out=go[:], in_=prb[:, 2048:2072],
                                                                    func=AF.Sigmoid), [tprb], [tgo])
                for c0 in (1408, 1664, 1920):
                    P.add("dve", lambda e, qo=qo, prb=prb, c0=c0: e.tensor_copy(
                        out=qo[:, c0 - 768:c0 - 768 + 128], in_=prb[:, c0:c0 + 128]), [tprb], [tqo])
                for (c0, nh) in ((768, 10), (1536, 2), (1792, 2)):
                    xv = prb[:, c0:c0 + nh * 64].rearrange("p (h t f) -> p h t f", t=2, f=32)
                    ov = qo[:, c0 - 768:c0 - 768 + nh * 64].rearrange("p (h t f) -> p h t f", t=2, f=32)
                    x1, x2 = xv[:, :, 0, :], xv[:, :, 1, :]
                    cb = tr["cos"][:].unsqueeze(1).broadcast_to([128, nh, 32])
                    sb_ = tr["sin"][:].unsqueeze(1).broadcast_to([128, nh, 32])
                    a, b, c, d = (t[:, 0:nh, :] for t in tmp)
                    P.add("dve", lambda e, a=a, x1=x1, cb=cb: e.tensor_tensor(out=a, in0=x1, in1=cb, op=ALU.mult),
                          [tprb, t_trig], [t_tmp])
                    P.add("dve", lambda e, b=b, x2=x2, sb_=sb_: e.tensor_tensor(out=b, in0=x2, in1=sb_, op=ALU.mult),
                          [tprb, t_trig], [t_tmp])
                    P.add("dve", lambda e, a=a, b=b, ov=ov: e.tensor_tensor(out=ov[:, :, 0, :], in0=a, in1=b,
                                                                           op=ALU.subtract), [t_tmp], [tqo])
                    P.add("dve", lambda e, c=c, x2=x2, cb=cb: e.tensor_tensor(out=c, in0=x2, in1=cb, op=ALU.mult),
                          [tprb, t_trig], [t_tmp])
                    P.add("dve", lambda e, d=d, x1=x1, sb_=sb_: e.tensor_tensor(out=d, in0=x1, in1=sb_, op=ALU.mult),
                          [tprb, t_trig], [t_tmp])
                    P.add("dve", lambda e, c=c, d=d, ov=ov: e.tensor_tensor(out=ov[:, :, 1, :], in0=c, in1=d,
                                                                           op=ALU.add), [t_tmp], [tqo])
                P.dma("sp", of_v[tix], fo[:], [tfo], t_o[0])
                P.dma("act", oq_v[tix], qo[:], [tqo], t_o[1])
                P.dma("sp", og_v[tix], go[:], [tgo], t_o[2])
        P.barrier()
    return t_o


POOL_WINDOWS = (2, 4, 8, 16)
SCALE = 0.125
BIG = 1.0e9


def _bd_build(cx, es, P, name, w_d, idx0, idx1):
    f = cx.sb(es, name + "f", [128, 128], F32)
    b = cx.sb(es, name + "b", [128, 128], BF16)
    t = P.tok(name)
    P.add("dve", lambda e: e.memset(f[:], 0.0), [], [t])
    P.dma("sp", f[0:64, 0:64], w_d[idx0], [], t)
    P.dma("sp", f[64:128, 64:128], w_d[idx1], [], t)
    P.add("dve", lambda e: e.tensor_copy(out=b[:], in_=f[:]), [t], [t])
    return b, t


def emit_pool(cx, xpT_d, poolfix_d, pool_w, pool_scale, yT_d, t_yT, TC):
    P, nc = cx.P, cx.nc
    L = 16 + TC
    with ExitStack() as es:
        sb = lambda n, s, d: cx.sb(es, "pl" + n, s, d)
        xp = sb("xp", [128, 2, L], F32)
        t_xp = P.tok("xp")
        P.dma("sp", xp[:], xpT_d, [], t_xp)
        fix = sb("fix", [128, 2, 16], F32)
        t_fix = P.tok("fix")
        P.dma("sp", fix[:], poolfix_d, [], t_fix)
        psc = sb("psc", [128, 2], F32)
        t_psc = P.tok("psc")
        P.dma("sp", psc[:], pool_scale.rearrange("(c p) -> p c", p=128), [], t_psc, allow_slow_non_contiguous=True)
        wA = sb("wA", [128, L], F32)
        wB = sb("wB", [128, L], F32)
        t_wA, t_wB = P.tok("wA"), P.tok("wB")
        dfb = sb("dfb", [128, TC], BF16)
        t_dfb = P.tok("dfb")
        yo = [sb("yo", [128, 512], BF16) for _ in range(2)]
        t_yo = P.toks("yo", 2)
        for cc in range(2):
            bd, t_bd = _bd_build(cx, es, P, f"plbd{cc}", pool_w, 2 * cc, 2 * cc + 1)
            for hh in range(2):
                w = POOL_WINDOWS[2 * cc + hh]
                rows = slice(hh * 64, hh * 64 + 64)
                src, tsrc = xp[rows, cc, :], t_xp
                bufs = [(wA, t_wA), (wB, t_wB)]
                lo, sh, bi = 0, 1, 0
                while sh < w:
                    dst, tdst = bufs[bi]
                    lo2 = lo + sh
                    P.add("dve", lambda e, dst=dst, src=src, lo2=lo2, sh=sh, rows=rows: e.tensor_tensor(
                        out=dst[rows, lo2:L], in0=src[:, lo2:L], in1=src[:, lo2 - sh:L - sh], op=ALU.add),
                        [tsrc], [tdst])
                    src, tsrc = dst[rows, :], tdst
                    lo, sh, bi = lo2, sh * 2, 1 - bi
                dst, tdst = bufs[bi]
                P.add("dve", lambda e, dst=dst, src=src, rows=rows, w=w: e.tensor_scalar(
                    out=dst[rows, 16:L], in0=src[:, 16:L], scalar1=1.0 / w, scalar2=None, op0=ALU.mult),
                    [tsrc], [tdst])
                P.add("dve", lambda e, dst=dst, rows=rows, cc=cc: e.tensor_tensor(
                    out=dst[rows, 16:32], in0=dst[rows, 16:32], in1=fix[rows, cc, :], op=ALU.mult),
                    [tdst, t_fix], [tdst])
                P.add("dve", lambda e, dst=dst, rows=rows, cc=cc: e.tensor_tensor(
                    out=dfb[rows, :], in0=dst[rows, 16:L], in1=xp[rows, cc, 16:L], op=ALU.subtract),
                    [tdst, t_xp], [t_dfb])
            for ch in range(TC // 512):
                pp, tpp = cx.psb[ch % 2]
                P.add("pe", lambda e, pp=pp, ch=ch, bd=bd: e.matmul(pp[:], lhsT=bd[:], rhs=dfb[:, ch * 512:(ch + 1) * 512],
                                                                   start=True, stop=True), [t_bd, t_dfb], [tpp])
                yb, tyb = yo[ch % 2], t_yo[ch % 2]
                P.add("act", lambda e, pp=pp, yb=yb, cc=cc: e.activation(out=yb[:], in_=pp[:], func=AF.Copy,
                                                                        scale=psc[:, cc:cc + 1]), [tpp, t_psc], [tyb])
                P.dma("sp", yT_d[:, cc, ch * 512:(ch + 1) * 512], yb[:], [tyb], t_yT)
        P.barrier()


def emit_lru(cx, xlT_d, glT_d, lruflag_d, conv_w, conv_b, w_r, b_r, w_i, b_i, lam, yT_d, t_yT, S, TC):
    P, nc = cx.P, cx.nc
    NLC = S // 512
    own0 = NLC - TC // 512
    with ExitStack() as es:
        sb = lambda n, s, d: cx.sb(es, "lr" + n, s, d)
        flag = sb("flag", [128, NLC], F32)
        t_flag = P.tok("flag")
        P.dma("sp", flag[:], lruflag_d, [], t_flag)
        one = sb("one", [128, 1], F32)
        t_one = P.tok("one")
        P.add("dve", lambda e: e.memset(one[:], 1.0), [], [t_one])
        for cc in range(2):
            cw = sb("cw", [128, 4], F32)
            cst = sb("cst", [128, 8], F32)
            t_c = P.tok("lrc")
            P.dma("sp", cw[:], conv_w[:, cc * 128:(cc + 1) * 128].rearrange("k p -> p k"), [], t_c,
                  allow_slow_non_contiguous=True)
            for ci, v in enumerate((conv_b, b_r, b_i, lam)):
                P.dma("sp", cst[:, ci:ci + 1], v[cc * 128:(cc + 1) * 128].rearrange("(p o) -> p o", o=1), [], t_c,
                      allow_slow_non_contiguous=True)
            P.add("act", lambda e, cst=cst: e.activation(out=cst[:, 4:5], in_=cst[:, 3:4], func=AF.Exp, scale=-1.0),
                  [t_c], [t_c])
            P.add("dve", lambda e, cst=cst: e.tensor_scalar(out=cst[:, 4:5], in0=cst[:, 4:5], scalar1=1.0,
                                                            scalar2=None, op0=ALU.add), [t_c], [t_c])
            P.add("act", lambda e, cst=cst: e.activation(out=cst[:, 5:6], in_=cst[:, 4:5], func=AF.Ln), [t_c], [t_c])
            P.add("dve", lambda e, cst=cst: e.tensor_scalar(out=cst[:, 6:7], in0=cst[:, 5:6], scalar1=-8.0,
                                                            scalar2=None, op0=ALU.mult), [t_c], [t_c])
            P.add("dve", lambda e, cst=cst: e.tensor_scalar(out=cst[:, 7:8], in0=cst[:, 5:6], scalar1=-16.0,
                                                            scalar2=None, op0=ALU.mult), [t_c], [t_c])
            bdr, t_bdr = _bd_build(cx, es, P, f"bdr{cc}", w_r, 2 * cc, 2 * cc + 1)
            bdi, t_bdi = _bd_build(cx, es, P, f"bdi{cc}", w_i, 2 * cc, 2 * cc + 1)
            hst = sb("hst", [128, 2], F32)
            t_hst = P.tok("hst")
            P.add("dve", lambda e, hst=hst: e.memset(hst[:], 0.0), [], [t_hst])
            X = [sb("X", [128, 516], F32) for _ in range(2)]
            t_X = P.toks("X", 2)
            G = [sb("G", [128, 512], F32) for _ in range(2)]
            t_G = P.toks("G", 2)
            w = {n: sb(n, [128, 512], F32) for n in ("xc", "r", "i", "a", "a2", "m", "t", "b", "h", "g2", "u", "sg", "ge")}
            tw = {n: P.tok(n) for n in w}
            xcb = sb("xcb", [128, 512], BF16)
            t_xcb = P.tok("xcb")
            yo = [sb("yo", [128, 512], BF16) for _ in range(2)]
            t_yo = P.toks("yo", 2)
            for ch in range(NLC):
                Xb, tX = X[ch % 2], t_X[ch % 2]
                P.dma("sp", Xb[:], xlT_d[:, cc, ch * 512:ch * 512 + 516], [], tX)
                xc = w["xc"]
                P.add("dve", lambda e, Xb=Xb, cw=cw, cst=cst: e.tensor_scalar(
                    out=xc[:], in0=Xb[:, 4:516], scalar1=cw[:, 3:4], scalar2=cst[:, 0:1], op0=ALU.mult, op1=ALU.add),
                    [tX, t_c], [tw["xc"]])
                for k in range(3):
                    P.add("dve", lambda e, Xb=Xb, cw=cw, k=k: e.scalar_tensor_tensor(
                        out=xc[:], in0=Xb[:, 1 + k:513 + k], scalar=cw[:, k:k + 1], in1=xc[:], op0=ALU.mult,
                        op1=ALU.add), [tX, t_c, tw["xc"]], [tw["xc"]])
                P.add("act", lambda e: e.activation(out=xcb[:], in_=xc[:], func=AF.Copy), [tw["xc"]], [t_xcb])
                pr, tpr = cx.psb[0]
                pi_, tpi = cx.psb[1]
                P.add("pe", lambda e, bdr=bdr, pr=pr: e.matmul(pr[:], lhsT=bdr[:], rhs=xcb[:], start=True, stop=True),
                      [t_bdr, t_xcb], [tpr])
                P.add("pe", lambda e, bdi=bdi, pi_=pi_: e.matmul(pi_[:], lhsT=bdi[:], rhs=xcb[:], start=True, stop=True),
                      [t_bdi, t_xcb], [tpi])
                P.add("act", lambda e, pr=pr, cst=cst: e.activation(out=w["r"][:], in_=pr[:], func=AF.Sigmoid,
                                                                    bias=cst[:, 1:2]), [tpr, t_c], [tw["r"]])
                P.add("act", lambda e, pi_=pi_, cst=cst: e.activation(out=w["i"][:], in_=pi_[:], func=AF.Sigmoid,
                                                                      bias=cst[:, 2:3]), [tpi, t_c], [tw["i"]])
                P.add("act", lambda e, cst=cst: e.activation(out=w["a"][:], in_=w["r"][:], func=AF.Exp,
                                                             scale=cst[:, 6:7]), [tw["r"], t_c], [tw["a"]])
                P.add("act", lambda e, cst=cst: e.activation(out=w["a2"][:], in_=w["r"][:], func=AF.Exp,
                                                             scale=cst[:, 7:8]), [tw["r"], t_c], [tw["a2"]])
                P.add("act", lambda e: e.activation(out=w["m"][:], in_=w["a2"][:], func=AF.Sqrt, scale=-1.0,
                                                    bias=one[:, 0:1]), [tw["a2"], t_one], [tw["m"]])
                P.add("dve", lambda e: e.tensor_tensor(out=w["t"][:], in0=w["i"][:], in1=xc[:], op=ALU.mult),
                      [tw["i"], tw["xc"]], [tw["t"]])
                P.add("dve", lambda e, ch=ch: e.scalar_tensor_tensor(out=w["b"][:], in0=w["t"][:],
                                                                    scalar=flag[:, ch:ch + 1], in1=w["m"][:],
                                                                    op0=ALU.mult, op1=ALU.mult),
                      [tw["t"], tw["m"], t_flag], [tw["b"]])
                P.add("dve", lambda e, hst=hst: e.tensor_tensor_scan(out=w["h"][:], data0=w["a"][:], data1=w["b"][:],
                                                                    initial=hst[:, 0:1], op0=ALU.mult, op1=ALU.add),
                      [tw["a"], tw["b"], t_hst], [tw["h"]])
                P.add("dve", lambda e, hst=hst: e.tensor_copy(out=hst[:, 0:1], in_=w["h"][:, 511:512]),
                      [tw["h"]], [t_hst])
                if ch >= own0:
                    oc = ch - own0
                    Gb, tG = G[oc % 2], t_G[oc % 2]
                    P.dma("act", Gb[:], glT_d[:, cc, oc * 512:(oc + 1) * 512], [], tG)
                    P.add("dve", lambda e, Gb=Gb: e.tensor_tensor(out=w["g2"][:], in0=Gb[:], in1=Gb[:], op=ALU.mult),
                          [tG], [tw["g2"]])
                    P.add("dve", lambda e: e.tensor_scalar(out=w["u"][:], in0=w["g2"][:], scalar1=0.044715,
                                                           scalar2=1.0, op0=ALU.mult, op1=ALU.add),
                          [tw["g2"]], [tw["u"]])
                    P.add("dve", lambda e, Gb=Gb: e.tensor_tensor(out=w["g2"][:], in0=w["u"][:], in1=Gb[:],
                                                                  op=ALU.mult), [tw["u"], tG], [tw["g2"]])
                    P.add("act", lambda e: e.activation(out=w["sg"][:], in_=w["g2"][:], func=AF.Sigmoid,
                                                        scale=1.5957691216057308), [tw["g2"]], [tw["sg"]])
                    P.add("dve", lambda e, Gb=Gb: e.tensor_tensor(out=w["ge"][:], in0=w["sg"][:], in1=Gb[:],
                                                                  op=ALU.mult), [tw["sg"], tG], [tw["ge"]])
                    yb, tyb = yo[oc % 2], t_yo[oc % 2]
                    P.add("dve", lambda e, yb=yb: e.tensor_tensor(out=yb[:], in0=w["h"][:], in1=w["ge"][:],
                                                                  op=ALU.mult), [tw["h"], tw["ge"]], [tyb])
                    P.dma("sp", yT_d[:, 2 + cc, oc * 512:(oc + 1) * 512], yb[:], [tyb], t_yT)
        P.barrier()


def emit_attn(cx, a, yT_d, t_yT, S, TC):
    P, nc = cx.P, cx.nc
    TCT, NCH, NSEL, NCM = TC // 128, S // 128, S // 64, S // 16
    NCC = max(NCM // 128, 1)
    JH = min(NSEL, 128)
    NJH = max(NSEL // 128, 1)
    NB = max(NCM // 512, 1)
    NBW = min(NCM, 512)
    with ExitStack() as es:
        sb = lambda n, s, d: cx.sb(es, "at" + n, s, d)
        T = P.tok
        QT = sb("QT", [128, 4, TC], BF16); t_QT = T("QT")
        P.dma("sp", QT[:], a["QT"], [], t_QT)
        KsT = sb("KsT", [128, S], BF16); t_KsT = T("KsT")
        P.dma("act", KsT[:], a["KsT"], [], t_KsT)
        VsE = sb("VsE", [128, NCH, 2, 65], BF16); t_VsE = T("VsE")
        P.add("dve", lambda e: e.memset(VsE[:], 1.0), [], [t_VsE])
        P.dma("sp", VsE[:, :, :, 0:64], a["Vs"].rearrange("p c (h d) -> p c h d", h=2), [], t_VsE)
        KwT = sb("KwT", [128, 512 + TC], BF16); t_KwT = T("KwT")
        P.dma("act", KwT[:], a["KwT"], [], t_KwT)
        VwE = sb("VwE", [128, 4 + TCT, 2, 65], BF16); t_VwE = T("VwE")
        P.add("dve", lambda e: e.memset(VwE[:], 1.0), [], [t_VwE])
        P.dma("sp", VwE[:, :, :, 0:64], a["Vw"].rearrange("p c (h d) -> p c h d", h=2), [], t_VwE)
        Ebig = sb("Ebig", [128, 8192], BF16); t_cst = T("cst")
        P.dma("act", Ebig[:], a["Ebig"], [], t_cst)
        Akq = sb("Akq", [128, 128], F32); Ac = sb("Ac", [128, 128], F32); identf = sb("idf", [128, 128], F32)
        iota16 = sb("iota16", [128, NCM], F32)
        for dst, src in ((Akq, "Akq"), (Ac, "Ac"), (identf, "identf"), (iota16, "iota16")):
            P.dma("sp", dst[:], a[src], [], t_cst)
        gate = sb("gate", [128, TCT, 24], F32)
        P.dma("sp", gate[:], a["gate"], [], t_cst)
        NTHR = a["thr"].shape[1]
        thr = sb("thr", [128, NTHR], F32)
        P.dma("sp", thr[:], a["thr"], [], t_cst)
        thrq = sb("thrq", [128, TCT], F32)
        P.dma("sp", thrq[:], a["thrq"], [], t_cst)
        o_c, o_d, o_w = 0, TCT * NCC, TCT * NCC + TCT * 4
        KcmpT = sb("KcmpT", [128, NCM], BF16); t_Kcmp = T("Kcmp")
        VcE = sb("VcE", [128, NCC, 2, 65], BF16); t_VcE = T("VcE")
        P.add("dve", lambda e: e.memset(VcE[:], 1.0), [], [t_VcE])
        with ExitStack() as es2:
            sb2 = lambda n, s, d: cx.sb(es2, "cm" + n, s, d)
            src = sb2("src", [128, S + 16], BF16); t_src = T("src")
            wst = sb2("wst", [128, 32, 64], F32); t_wst = T("wst")
            BD = sb2("BD", [128, 32, 128], BF16); t_BD = T("BD")
            pes = sb2("pes", [128, 32], F32); PEt = sb2("PEt", [128, 32], BF16); t_pe = T("pe")
            cpe = sb2("cpe", [128, 2], F32); t_cpe = T("cpe")
            VcT = sb2("VcT", [128, NCM], BF16); t_VcT = T("VcT")
            for hh in range(2):
                P.dma("sp", pes[hh * 64:(hh + 1) * 64, :], a["cmp_pe"].rearrange("l d -> d l"), [], t_pe,
                      allow_slow_non_contiguous=True)
            P.add("dve", lambda e: e.tensor_copy(out=PEt[:], in_=pes[:]), [t_pe], [t_pe])
            for which, (srcname, wname) in enumerate((("KcT", "cmp_w_k"), ("VcT", "cmp_w_v"))):
                P.add("dve", lambda e: e.memset(src[:, S:S + 16], 0.0), [], [t_src])
                P.dma("sp", src[:, 0:S], a[srcname], [], t_src)
                for hh in range(2):
                    P.dma("act", wst[hh * 64:(hh + 1) * 64, :, :], a[wname].rearrange("l d e -> d l e"), [], t_wst)
                P.add("dve", lambda e: e.memset(BD[:], 0.0), [], [t_BD])
                for hh in range(2):
                    P.add("dve", lambda e, hh=hh: e.tensor_copy(out=BD[hh * 64:(hh + 1) * 64, :, hh * 64:(hh + 1) * 64],
                                                                in_=wst[hh * 64:(hh + 1) * 64, :, :]), [t_wst], [t_BD])
                pc, tpc = cx.psb[2]
                for l in range(32):
                    P.add("pe", lambda e, l=l, pc=pc: e.matmul(pc[:, 0:1], lhsT=BD[:, l, :], rhs=PEt[:, l:l + 1],
                                                               start=(l == 0), stop=(l == 31)), [t_BD, t_pe], [tpc])
                P.add("dve", lambda e, pc=pc, which=which: e.tensor_copy(out=cpe[:, which:which + 1], in_=pc[:, 0:1]),
                      [tpc], [t_cpe])
                sv = src[:].rearrange("p (n s) -> p n s", s=16)
                dstT = KcmpT if which == 0 else VcT
                tdst = t_Kcmp if which == 0 else t_VcT
                for nb in range(NB):
                    pk, tpk = cx.psb[nb % 2]
                    n0 = nb * NBW
                    for l in range(32):
                        rhs = sv[:, n0:n0 + NBW, l] if l < 16 else sv[:, n0 + 1:n0 + 1 + NBW, l - 16]
                        P.add("pe", lambda e, l=l, pk=pk, rhs=rhs: e.matmul(pk[:, 0:NBW], lhsT=BD[:, l, :], rhs=rhs,
                                                                          start=(l == 0), stop=(l == 31)),
                              [t_BD, t_src], [tpk])
                    P.add("act", lambda e, pk=pk, n0=n0, dstT=dstT, which=which: e.activation(
                        out=dstT[:, n0:n0 + NBW], in_=pk[:, 0:NBW], func=AF.Identity, bias=cpe[:, which:which + 1]),
                        [tpk, t_cpe], [tdst])
            for c in range(NCC):
                pv, tpv = cx.psb[3]
                P.add("pe", lambda e, c=c, pv=pv: e.matmul(pv[:, 0:128], lhsT=VcT[:, c * 128:(c + 1) * 128],
                                                           rhs=cx.ident[:], start=True, stop=True),
                      [t_VcT, cx.t_ident], [tpv])
                P.add("dve", lambda e, c=c, pv=pv: e.tensor_copy(
                    out=VcE[:, c, :, 0:64], in_=pv[:, 0:128].rearrange("p (h d) -> p h d", h=2)), [tpv], [t_VcE])
            P.barrier()
        cand = [sb("cand", [128, NSEL], F32) for _ in range(2)]; t_cand = P.toks("cand", 2)
        forc = [sb("forc", [128, NSEL], F32) for _ in range(2)]; t_forc = P.toks("forc", 2)
        Etm = sb("Etm", [128, NCM], F32); t_Etm = T("Etm")
        Pm_tm = sb("Pmtm", [128, NCM], F32); t_Pmtm = T("Pmtm")
        st = sb("st", [128, 16], F32); t_st = T("st")
        t4 = sb("t4", [128, NSEL], F32); t_t4 = T("t4")
        imp = sb("imp", [128, NSEL], F32); t_imp = T("imp")
        score = sb("score", [128, NSEL], F32); sc2 = sb("sc2", [128, NSEL], F32); t_sc = T("score")
        m8 = sb("m8", [128, 16], F32); t_m8 = T("m8")
        sel = sb("sel", [128, NSEL], BF16); t_sel = T("sel")
        selT = sb("selT", [128, NJH, 128], BF16); t_selT = T("selT")
        P.add("dve", lambda e: e.memset(selT[:], 0.0), [], [t_selT])
        E = [sb("E", [128, 4, 128], BF16) for _ in range(2)]; t_E = P.toks("E", 2)
        Pm = [sb("Pm", [128, 4, 128], BF16) for _ in range(2)]; t_Pm = P.toks("Pm", 2)
        OTs = sb("OTs", [65, 512], F32); t_OTs = T("OTs")
        w4 = sb("w4", [128, 8], F32); t_w4 = T("w4")
        tmpo = sb("tmpo", [128, 4, 64], F32); t_tmpo = T("tmpo")
        oacc = sb("oacc", [128, 512], F32); t_oacc = T("oacc")
        ob = sb("ob", [128, 512], BF16); t_ob = T("ob")
        oT = [sb("oT", [128, 128], BF16) for _ in range(2)]; t_oT = P.toks("oT", 2)
        step = [0]

        def run_steps(steps, qv):
            n = len(steps)
            fr = [None] * n

            def front(k):
                i = step[0]; step[0] += 1
                stp = steps[k]
                ps, tps = cx.psb[i % 2]
                P.add("pe", lambda e, ps=ps, stp=stp: e.matmul(ps[:], lhsT=stp["lhsK"], rhs=qv, start=True, stop=True),
                      [stp["tK"], t_QT], [tps])
                aux = stp["pre"](i) if stp.get("pre") else None
                fr[k] = (i, ps, tps, aux)

            def back(k):
                stp = steps[k]
                i, ps, tps, aux = fr[k]
                Eb, tE = E[i % 2], t_E[i % 2]
                P.add("act", lambda e, ps=ps, Eb=Eb: e.activation(out=Eb[:].rearrange("p g q -> p (g q)"), in_=ps[:],
                                                                  func=AF.Exp, scale=SCALE), [tps], [tE])
                rhsP, tP = stp["mask"](Eb, tE, i, aux)
                OT, tOT = stp["OT"]
                P.add("pe", lambda e, rhsP=rhsP, stp=stp, OT=OT: e.matmul(
                    OT[0:65, :], lhsT=stp["vE"], rhs=rhsP[:].rearrange("p g q -> p (g q)"),
                    start=stp["first"], stop=stp["last"]), [stp["tV"], tP], [tOT])

            for k in range(n + 1):
                if k < n:
                    front(k)
                if k >= 1:
                    back(k - 1)

        for m in range(TCT):
            cb, tcb = cand[m % 2], t_cand[m % 2]
            fb, tfb = forc[m % 2], t_forc[m % 2]
            P.dma("sp", cb[:], a["cand"][m], [], tcb)
            P.dma("act", fb[:], a["forced"][m], [], tfb)
            for h in range(2):
                rows = slice(h * 64, (h + 1) * 64)
                qv = QT[rows, :, m * 128:(m + 1) * 128]
                P.add("dve", lambda e: e.memset(st[:], 0.0), [], [t_st])
                for g in range(4):
                    for nb in range(NB):
                        ps, tps = cx.psb[2 + nb % 2]
                        P.add("pe", lambda e, ps=ps, g=g, nb=nb, rows=rows, m=m: e.matmul(
                            ps[:, 0:NBW], lhsT=QT[rows, g, m * 128:(m + 1) * 128], rhs=KcmpT[rows, nb * NBW:(nb + 1) * NBW],
                            start=True, stop=True), [t_QT, t_Kcmp], [tps])
                        P.add("act", lambda e, ps=ps, nb=nb: e.activation(out=Etm[:, nb * NBW:(nb + 1) * NBW],
                                                                          in_=ps[:, 0:NBW], func=AF.Exp, scale=SCALE),
                              [tps], [t_Etm])
                    P.add("dve", lambda e, g=g, m=m: e.scalar_tensor_tensor(
                        out=Pm_tm[:], in0=iota16[:], scalar=thrq[:, m:m + 1], in1=Etm[:], op0=ALU.is_le, op1=ALU.mult,
                        accum_out=st[:, g:g + 1]), [t_cst, t_Etm, t_st], [t_Pmtm, t_st])
                    P.add("dve", lambda e, g=g: e.tensor_scalar(out=st[:, 8 + g:9 + g], in0=st[:, g:g + 1],
                                                                scalar1=1e-30, scalar2=None, op0=ALU.max),
                          [t_st], [t_st])
                    P.add("dve", lambda e, g=g: e.reciprocal(out=st[:, 4 + g:5 + g], in_=st[:, 8 + g:9 + g]),
                          [t_st], [t_st])
                    pv4 = Pm_tm[:].rearrange("p (j r) -> p j r", r=4)
                    P.add("dve", lambda e, pv4=pv4: e.tensor_reduce(out=t4[:], in_=pv4, axis=AX.X, op=ALU.add),
                          [t_Pmtm], [t_t4])
                    P.add("dve", lambda e, pv4=pv4: e.tensor_tensor(out=t4[:, 1:NSEL], in0=t4[:, 1:NSEL],
                                                                    in1=pv4[:, 0:NSEL - 1, 3], op=ALU.add),
                          [t_Pmtm, t_t4], [t_t4])
                    if g == 0:
                        P.add("dve", lambda e: e.tensor_scalar(out=imp[:], in0=t4[:], scalar1=st[:, 4:5], scalar2=None,
                                                               op0=ALU.mult), [t_t4, t_st], [t_imp])
                    else:
                        P.add("dve", lambda e, g=g: e.scalar_tensor_tensor(out=imp[:], in0=t4[:], scalar=st[:, 4 + g:5 + g],
                                                                          in1=imp[:], op0=ALU.mult, op1=ALU.add),
                              [t_t4, t_st, t_imp], [t_imp])
                P.add("dve", lambda e, cb=cb: e.tensor_tensor(out=score[:], in0=imp[:], in1=cb[:], op=ALU.mult),
                      [t_imp, tcb], [t_sc])
                P.add("dve", lambda e, fb=fb: e.tensor_tensor(out=score[:], in0=score[:], in1=fb[:], op=ALU.add),
                      [t_sc, tfb], [t_sc])
                P.add("dve", lambda e: e.max(out=m8[:, 0:8], in_=score[:]), [t_sc], [t_m8])
                P.add("dve", lambda e: e.match_replace(out=sc2[:], in_to_replace=m8[:, 0:8], in_values=score[:],
                                                       imm_value=-1e30), [t_sc, t_m8], [t_sc])
                P.add("dve", lambda e: e.max(out=m8[:, 8:16], in_=sc2[:]), [t_sc], [t_m8])
                P.add("dve", lambda e: e.tensor_scalar(out=m8[:, 0:1], in0=m8[:, 15:16], scalar1=1e-30, scalar2=None,
                                                       op0=ALU.max), [t_m8], [t_m8])
                P.add("dve", lambda e: e.tensor_scalar(out=sel[:], in0=score[:], scalar1=m8[:, 0:1], scalar2=None,
                                                       op0=ALU.is_ge), [t_sc, t_m8], [t_sel])
                for jh in range(NJH):
                    pt, tpt = cx.psb[3]
                    P.add("pe", lambda e, jh=jh, pt=pt: e.matmul(pt[0:JH, 0:128], lhsT=sel[:, jh * 128:jh * 128 + JH],
                                                                 rhs=cx.ident[:], start=True, stop=True),
                          [t_sel, cx.t_ident], [tpt])
                    P.add("act", lambda e, jh=jh, pt=pt: e.activation(out=selT[0:JH, jh, :], in_=pt[0:JH, 0:128],
                                                                      func=AF.Copy), [tpt], [t_selT])
                OTb = {0: cx.psb[4], 1: cx.psb[5], 2: cx.psb[6]}
                steps = []
                for kc in range(NCH):
                    def pre(i, kc=kc):
                        pmk, tpmk = cx.psb[2 + i % 2]
                        jh = (2 * kc) // 128
                        off = 128 * (kc % 64)
                        P.add("pe", lambda e, pmk=pmk: e.matmul(pmk[:, 0:128], lhsT=Ebig[0:JH, off:off + 128],
                                                                rhs=selT[0:JH, jh, :], start=True, stop=True),
                              [t_cst, t_selT], [tpmk])
                        return pmk, tpmk

                    def mf(Eb, tE, i, aux, kc=kc, m=m):
                        pmk, tpmk = aux
                        Pb, tPb = Pm[i % 2], t_Pm[i % 2]
                        P.add("dve", lambda e, pmk=pmk, Pb=Pb, Eb=Eb: e.tensor_tensor(
                            out=Pb[:], in0=Eb[:], in1=pmk[:, 0:128].unsqueeze(1).broadcast_to([128, 4, 128]),
                            op=ALU.mult), [tE, tpmk], [tPb])
                        if kc % TCT == m:
                            ci = o_d + m * 4 + kc // TCT
                            P.add("dve", lambda e, Pb=Pb, ci=ci: e.scalar_tensor_tensor(
                                out=Pb[:], in0=Akq[:].unsqueeze(1).broadcast_to([128, 4, 128]), scalar=thr[:, ci:ci + 1],
                                in1=Pb[:], op0=ALU.is_le, op1=ALU.mult), [tPb, t_cst], [tPb])
                        return Pb, tPb
                    steps.append(dict(lhsK=KsT[rows, kc * 128:(kc + 1) * 128], tK=t_KsT, vE=VsE[:, kc, h, :], tV=t_VsE,
                                      pre=pre, mask=mf, OT=OTb[1], first=(kc == 0), last=(kc == NCH - 1)))
                for c in range(NCC):
                    def mf(Eb, tE, i, aux, c=c, m=m):
                        Pb, tPb = Pm[i % 2], t_Pm[i % 2]
                        ci = o_c + m * NCC + c
                        P.add("dve", lambda e, Pb=Pb, Eb=Eb, ci=ci: e.scalar_tensor_tensor(
                            out=Pb[:], in0=Ac[:].unsqueeze(1).broadcast_to([128, 4, 128]), scalar=thr[:, ci:ci + 1],
                            in1=Eb[:], op0=ALU.is_le, op1=ALU.mult), [tE, t_cst], [tPb])
                        return Pb, tPb
                    steps.append(dict(lhsK=KcmpT[rows, c * 128:(c + 1) * 128], tK=t_Kcmp, vE=VcE[:, c, h, :], tV=t_VcE,
                                      mask=mf, OT=OTb[0], first=(c == 0), last=(c == NCC - 1)))
                for c in range(5):
                    def mf(Eb, tE, i, aux, c=c, m=m):
                        if c in (0, 4) or m + c < 4:
                            Pb, tPb = Pm[i % 2], t_Pm[i % 2]
                            ci = o_w + m * 5 + c
                            op = ALU.is_gt if c == 0 else ALU.is_le
                            P.add("dve", lambda e, Pb=Pb, Eb=Eb, ci=ci, op=op: e.scalar_tensor_tensor(
                                out=Pb[:], in0=Akq[:].unsqueeze(1).broadcast_to([128, 4, 128]), scalar=thr[:, ci:ci + 1],
                                in1=Eb[:], op0=op, op1=ALU.mult), [tE, t_cst], [tPb])
                            return Pb, tPb
                        return Eb, tE
                    steps.append(dict(lhsK=KwT[rows, (m + c) * 128:(m + c + 1) * 128], tK=t_KwT,
                                      vE=VwE[:, m + c, h, :], tV=t_VwE, mask=mf, OT=OTb[2], first=(c == 0), last=(c == 4)))
                run_steps(steps, qv)
                for br in range(3):
                    OT, tOT = OTb[br]
                    P.add("act", lambda e, OT=OT: e.activation(out=OTs[:], in_=OT[0:65, :], func=AF.Copy), [tOT], [t_OTs])
                    pt, tpt = cx.psb[7]
                    for g in range(4):
                        P.add("pe", lambda e, g=g, pt=pt: e.matmul(pt[:, g * 65:(g + 1) * 65],
                                                                   lhsT=OTs[0:65, g * 128:(g + 1) * 128],
                                                                   rhs=identf[0:65, 0:65], start=True, stop=True),
                              [t_OTs, t_cst], [tpt])
                    ptv = pt[:, 0:260].rearrange("p (g e) -> p g e", e=65)
                    P.add("dve", lambda e, ptv=ptv: e.tensor_scalar(out=w4[:, 0:4], in0=ptv[:, :, 64], scalar1=1e-30,
                                                                    scalar2=None, op0=ALU.max), [tpt], [t_w4])
                    P.add("dve", lambda e: e.reciprocal(out=w4[:, 4:8], in_=w4[:, 0:4]), [t_w4], [t_w4])
                    gv = gate[:, m, h * 12:(h + 1) * 12].rearrange("p (g b) -> p g b", b=3)[:, :, br]
                    P.add("dve", lambda e, gv=gv: e.tensor_tensor(out=w4[:, 0:4], in0=w4[:, 4:8], in1=gv, op=ALU.mult),
                          [t_w4, t_cst], [t_w4])
                    oav = oacc[:, h * 256:(h + 1) * 256].rearrange("p (g d) -> p g d", d=64)
                    wb = w4[:, 0:4].unsqueeze(2).broadcast_to([128, 4, 64])
                    if br == 0:
                        P.add("dve", lambda e, ptv=ptv, oav=oav, wb=wb: e.tensor_tensor(
                            out=oav, in0=ptv[:, :, 0:64], in1=wb, op=ALU.mult), [tpt, t_w4], [t_oacc])
                    else:
                        P.add("dve", lambda e, ptv=ptv, wb=wb: e.tensor_tensor(
                            out=tmpo[:], in0=ptv[:, :, 0:64], in1=wb, op=ALU.mult), [tpt, t_w4], [t_tmpo])
                        P.add("dve", lambda e, oav=oav: e.tensor_tensor(out=oav, in0=oav, in1=tmpo[:], op=ALU.add),
                              [t_tmpo, t_oacc], [t_oacc])
            P.add("act", lambda e: e.activation(out=ob[:], in_=oacc[:], func=AF.Copy), [t_oacc], [t_ob])
            for c4 in range(4):
                pt, tpt = cx.psb[7]
                P.add("pe", lambda e, c4=c4, pt=pt: e.matmul(pt[:, 0:128], lhsT=ob[:, c4 * 128:(c4 + 1) * 128],
                                                             rhs=cx.ident[:], start=True, stop=True),
                      [t_ob, cx.t_ident], [tpt])
                otb, totb = oT[c4 % 2], t_oT[c4 % 2]
                P.add("act", lambda e, pt=pt, otb=otb: e.activation(out=otb[:], in_=pt[:, 0:128], func=AF.Copy),
                      [tpt], [totb])
                P.dma("sp", yT_d[:, 4 + c4, m * 128:(m + 1) * 128], otb[:], [totb], t_yT)
        P.barrier()


def emit_wout(cx, yT_d, t_yT, w_out, gpost, h_in, t_hin, h_out, TC):
    P, nc = cx.P, cx.nc
    t_hout = P.tok("hmid")
    with ExitStack() as es:
        sb = lambda n, s, d: cx.sb(es, "wo" + n, s, d)
        tl = Tail(cx, es, gpost, 1.0, "wo")
        wo = sb("w", [128, 8, D], BF16); t_wo = P.tok("wo")
        stg = [sb("stg", [128, D], F32) for _ in range(2)]; t_stg = P.toks("stg", 2)
        w_v = w_out.rearrange("(k p) d -> p k d", p=128)
        for k in range(8):
            load_cast(cx, wo[:, k, :], t_wo, w_v[:, k, :], stg[k % 2][:], t_stg[k % 2],
                      "sp" if k % 2 == 0 else "act", "act" if k % 2 == 0 else "dve")
        xt = [sb("xt", [128, 4, D], F32) for _ in range(2)]; t_xt = P.toks("xt", 2)
        yt = [sb("yt", [128, 8, ST], BF16) for _ in range(2)]; t_yt = P.toks("yt", 2)
        h_in_v = h_in.rearrange("(s j p) d -> s p j d", p=128, j=4)
        h_out_v = h_out.rearrange("(s j p) d -> s j p d", p=128, j=4)
        for s in range(TC // ST):
            xb, txb = xt[s % 2], t_xt[s % 2]
            yb, tyb = yt[s % 2], t_yt[s % 2]
            P.dma("sp", xb[:], h_in_v[s], [t_hin], txb)
            P.dma("act", yb[:], yT_d[:, :, s * ST:(s + 1) * ST], [t_yT], tyb)
            for j in range(4):
                tl.run(lambda k, j=j, yb=yb: yb[:, k, j * 128:(j + 1) * 128], 8,
                       lambda k, hf: wo[:, k, hf * 512:(hf + 1) * 512], [tyb], t_wo,
                       xb[:, j, :], txb, h_out_v[s, j], t_hout)
        P.barrier()
    return t_hout


import numpy as np
import ml_dtypes
BF = ml_dtypes.bfloat16
BIGV = 1.0e9

def consts(S):
    NCM = S // 16
    c = {}
    x = np.arange(8192)
    c["Ebig"] = (np.arange(128)[:, None] == (x // 64)[None, :]).astype(np.float32).astype(BF)
    k = np.arange(128, dtype=np.float32)
    c["Akq"] = (k[:, None] - k[None, :]).astype(np.float32)
    c["Ac"] = (16 * k[:, None] + 31 - k[None, :]).astype(np.float32)
    c["identf"] = np.eye(128, dtype=np.float32)
    c["iota16"] = np.tile((16.0 * np.arange(NCM, dtype=np.float32))[None, :], (128, 1))
    c["ident"] = np.eye(128, dtype=np.float32).astype(BF)
    inv = (10000.0 ** (-np.arange(0, 64, 2, dtype=np.float32) / 64)).astype(np.float32)
    c["invtab"] = np.tile(inv[None, :], (128, 1)).astype(np.float32)
    return c

def tables(qd, S, TC):
    TCT, NSEL, NCM = TC // 128, S // 64, S // 16
    NCC = max(NCM // 128, 1)
    t = {}
    q = np.arange(128)
    cand = np.zeros((TCT, 128, NSEL), np.float32)
    forced = np.zeros((TCT, 128, NSEL), np.float32)
    j = np.arange(NSEL)
    thr_c = np.zeros((TCT, NCC), np.float32); thr_d = np.zeros((TCT, 4), np.float32); thr_w = np.zeros((TCT, 5), np.float32)
    thrq = np.zeros((128, TCT), np.float32)
    for m in range(TCT):
        i = TCT * qd + m
        cur = (128 * i + q) // 64
        valid = j[None, :] <= cur[:, None]
        f = (j[None, :] == 0) | (j[None, :] == cur[:, None]) | (j[None, :] == cur[:, None] - 1)
        f = f & valid
        cand[m] = (valid & ~f).astype(np.float32)
        forced[m] = f.astype(np.float32) * BIGV
        for c in range(NCC):
            thr_c[m, c] = 128.0 * (i - 16 * c)
        for r in range(4):
            thr_d[m, r] = 0.0 if r == qd else BIGV
        for c in range(5):
            ok = (i - 4 + c) >= 0
            if c == 0:
                thr_w[m, c] = 0.0 if ok else BIGV
            elif c == 4:
                thr_w[m, c] = 0.0
            else:
                thr_w[m, c] = BIGV if ok else -BIGV
        thrq[:, m] = 128.0 * i + q - 31.0
    row = np.concatenate([thr_c.ravel(), thr_d.ravel(), thr_w.ravel()]).astype(np.float32)
    t["thr"] = np.tile(row[None, :], (128, 1))
    t["thrq"] = thrq
    t["cand"] = cand
    t["forced"] = forced
    fix = np.ones((128, 2, 16), np.float32)
    if qd == 0:
        for cc in range(2):
            for hh in range(2):
                w = (2, 4, 8, 16)[2 * cc + hh]
                fix[hh * 64:(hh + 1) * 64, cc, :] = w / np.minimum(np.arange(16) + 1, w)
    t["poolfix"] = fix
    NLC = S // 512
    flag = np.zeros((128, NLC), np.float32)
    flag[:, NLC - ((qd + 1) * TC) // 512:] = 1.0
    t["lruflag"] = flag
    return t

def chT(a):
    return np.ascontiguousarray(a.T.reshape(2, 128, -1).transpose(1, 0, 2))

def layout_B(qd, S, TC, f32p, qkv, gates):
    t0 = qd * TC
    TCT, NCH = TC // 128, S // 128
    d = {}
    xp, xl, gl = f32p[:, 0:256], f32p[:, 256:512], f32p[:, 512:768]
    d["xpT"] = chT(np.concatenate([np.zeros((16, 256), np.float32), xp])[t0:t0 + 16 + TC])
    d["xlT"] = chT(np.concatenate([np.zeros((4 + S, 256), np.float32), xl])[t0 + TC:t0 + TC + 4 + S])
    d["glT"] = chT(gl[t0:t0 + TC])
    q = qkv[:, 0:512]
    kc, vc, ks, vs, kw, vw = (qkv[:, 512 + 128 * i:640 + 128 * i] for i in range(6))
    d["QT"] = np.ascontiguousarray(q[t0:t0 + TC].reshape(TC, 2, 4, 64).transpose(1, 3, 2, 0).reshape(128, 4, TC))
    d["KsT"] = np.ascontiguousarray(ks.T)
    d["Vs"] = np.ascontiguousarray(vs.reshape(NCH, 128, 128).transpose(1, 0, 2))
    d["KcT"] = np.ascontiguousarray(kc.T)
    d["VcT"] = np.ascontiguousarray(vc.T)
    z = np.zeros((512, 128), kw.dtype)
    d["KwT"] = np.ascontiguousarray(np.concatenate([z, kw])[t0:t0 + 512 + TC].T)
    d["Vw"] = np.ascontiguousarray(np.concatenate([z, vw])[t0:t0 + 512 + TC].reshape(4 + TCT, 128, 128).transpose(1, 0, 2))
    d["gate"] = np.ascontiguousarray(gates[t0:t0 + TC].reshape(TCT, 128, 24).transpose(1, 0, 2))
    return d


from concourse.bass_utils import run_bass_kernel_spmd

_MK_S = 16384
_PROGS = {}


def _build(kind, S, TC):
    TCT, NCH, NSEL, NCM = TC // 128, S // 128, S // 64, S // 16
    NCC = max(NCM // 128, 1)
    NTHR = TCT * NCC + TCT * 4 + TCT * 5
    nc = bass.Bass("TRN2", target_bir_lowering=False)
    dt = lambda n, s, d, k="ExternalInput": nc.dram_tensor(n, s, d, kind=k).ap()
    P = Prog(nc)
    fin = []
    with ExitStack() as es:
        cx = Ctx(nc, P, es)
        ident = dt("ident", [128, 128], BF16)
        load_consts(cx, es, ident)
        has_B = kind in ("BA", "B")
        has_A = kind in ("A", "BA")
        t_h = P.tok("h0")
        if has_B:
            a = dict(QT=dt("QT", [128, 4, TC], BF16), KsT=dt("KsT", [128, S], BF16), Vs=dt("Vs", [128, NCH, 128], BF16),
                     KcT=dt("KcT", [128, S], BF16), VcT=dt("VcT", [128, S], BF16), KwT=dt("KwT", [128, 512 + TC], BF16),
                     Vw=dt("Vw", [128, 4 + TCT, 128], BF16), gate=dt("gate", [128, TCT, 24], F32),
                     Ebig=dt("Ebig", [128, 8192], BF16), Akq=dt("Akq", [128, 128], F32), Ac=dt("Ac", [128, 128], F32),
                     identf=dt("identf", [128, 128], F32), iota16=dt("iota16", [128, NCM], F32),
                     thr=dt("thr", [128, NTHR], F32), thrq=dt("thrq", [128, TCT], F32),
                     cand=dt("cand", [TCT, 128, NSEL], F32), forced=dt("forced", [TCT, 128, NSEL], F32),
                     cmp_w_k=dt("cmp_w_k", [32, 64, 64], F32), cmp_w_v=dt("cmp_w_v", [32, 64, 64], F32),
                     cmp_pe=dt("cmp_pe", [32, 64], F32))
            xpT = dt("xpT", [128, 2, 16 + TC], F32); poolfix = dt("poolfix", [128, 2, 16], F32)
            pool_w = dt("pool_w", [4, 64, 64], F32); pool_scale = dt("pool_scale", [256], F32)
            xlT = dt("xlT", [128, 2, 4 + S], F32); glT = dt("glT", [128, 2, TC], F32)
            lruflag = dt("lruflag", [128, S // 512], F32)
            conv_w = dt("conv_w", [4, 256], F32); conv_b = dt("conv_b", [256], F32)
            w_r = dt("lru_w_r", [4, 64, 64], F32); b_r = dt("lru_b_r", [256], F32)
            w_i = dt("lru_w_i", [4, 64, 64], F32); b_i = dt("lru_b_i", [256], F32); lam = dt("lru_lambda", [256], F32)
            w_out = dt("w_out", [D, D], F32); mix_post_g = dt("mix_post_g", [D], F32)
            h1 = dt("h1_in", [TC, D], F32)
            f2 = dict(wg=dt("f2_wg", [D, DFF], F32), wu=dt("f2_wu", [D, DFF], F32), wd=dt("f2_wd", [DFF, D], F32),
                      gpre=dt("f2_gpre", [D], F32), gpost=dt("f2_gpost", [D], F32))
            yT = dt("yT_s", [128, 8, TC], BF16, "Internal")
            h_mid = dt("h_mid_s", [TC, D], F32, "Internal")
            t_yT = P.tok("yT")
            emit_pool(cx, xpT, poolfix, pool_w, pool_scale, yT, t_yT, TC)
            emit_lru(cx, xlT, glT, lruflag, conv_w, conv_b, w_r, b_r, w_i, b_i, lam, yT, t_yT, S, TC)
            emit_attn(cx, a, yT, t_yT, S, TC)
            t_hm = emit_wout(cx, yT, t_yT, w_out, mix_post_g, h1, P.tok("h1in"), h_mid, TC)
            if has_A:
                h2 = dt("h2_s", [TC, D], F32, "Internal")
            else:
                h2 = dt("h_out", [TC, D], F32, "ExternalOutput")
            t_h = emit_ffn(cx, h_mid, t_hm, h2, f2["wg"], f2["wu"], f2["wd"], f2["gpre"], f2["gpost"], TC, "f2")
            h_cur = h2
            if not has_A:
                fin.append(t_h)
        else:
            h_cur = dt("h_in", [TC, D], F32)
        if has_A:
            f1 = dict(wg=dt("f1_wg", [D, DFF], F32), wu=dt("f1_wu", [D, DFF], F32), wd=dt("f1_wd", [DFF, D], F32),
                      gpre=dt("f1_gpre", [D], F32), gpost=dt("f1_gpost", [D], F32))
            w_in = dt("w_in", [D, IN_COLS], F32); mix_pre_g = dt("mix_pre_g", [D], F32)
            pos = dt("pos", [TC], I32); invtab = dt("invtab", [128, 32], F32)
            h1o = dt("h1_out", [TC, D], F32, "ExternalOutput")
            o_f32 = dt("o_f32", [TC, 768], F32, "ExternalOutput")
            o_qkv = dt("o_qkv", [TC, 1280], BF16, "ExternalOutput")
            o_gate = dt("o_gate", [TC, 24], F32, "ExternalOutput")
            t_h1 = emit_ffn(cx, h_cur, t_h, h1o, f1["wg"], f1["wu"], f1["wd"], f1["gpre"], f1["gpost"], TC, "f1")
            t_o = emit_proj(cx, h1o, t_h1, w_in, mix_pre_g, pos, invtab, o_f32, o_qkv, o_gate, TC, "pj")
            fin += [t_h1] + list(t_o)
        P.emit(fin)
    return nc


def _get_prog(kind, S, TC):
    key = (kind, S, TC)
    if key not in _PROGS:
        _PROGS[key] = _build(kind, S, TC)
    return _PROGS[key]


_RUN = [None]


def _run(nc, in_maps):
    if _RUN[0] is not None:
        return _RUN[0](nc, in_maps)
    return run_bass_kernel_spmd(nc, in_maps, core_ids=list(range(len(in_maps)))).results


def kernel(**inp):
    x = np.asarray(inp["x"])
    NBt, S, _ = x.shape
    NQ = 4
    TC = S // NQ
    ncores = NBt * NQ
    cs = consts(S)
    g = lambda k, l: np.ascontiguousarray(np.asarray(inp[k])[l])
    positions = np.asarray(inp["positions"]).astype(np.int32)

    def a_inputs(l):
        return dict(f1_wg=g("ffn1_w_gate", l), f1_wu=g("ffn1_w_up", l), f1_wd=g("ffn1_w_down", l),
                    f1_gpre=g("ffn1_pre_g", l), f1_gpost=g("ffn1_post_g", l), w_in=g("w_in", l),
                    mix_pre_g=g("mix_pre_g", l), invtab=cs["invtab"])

    def b_inputs(l):
        d = dict(f2_wg=g("ffn2_w_gate", l), f2_wu=g("ffn2_w_up", l), f2_wd=g("ffn2_w_down", l),
                 f2_gpre=g("ffn2_pre_g", l), f2_gpost=g("ffn2_post_g", l), w_out=g("w_out", l),
                 mix_post_g=g("mix_post_g", l))
        for k in ("pool_w", "pool_scale", "conv_w", "conv_b", "lru_w_r", "lru_b_r", "lru_w_i", "lru_b_i",
                  "lru_lambda", "cmp_w_k", "cmp_w_v", "cmp_pe"):
            d[k] = g(k, l)
        for k in ("Ebig", "Akq", "Ac", "identf", "iota16"):
            d[k] = cs[k]
        return d

    tabs = [tables(qd, S, TC) for qd in range(NQ)]

    def gather(res, key, width, dtype):
        out = np.zeros((NBt, S, width), dtype)
        for c in range(ncores):
            b, qd = divmod(c, NQ)
            out[b, qd * TC:(qd + 1) * TC] = np.asarray(res[c][key]).reshape(TC, width)
        return out

    base = a_inputs(0)
    in_maps = []
    for c in range(ncores):
        b, qd = divmod(c, NQ)
        m = dict(base)
        m["ident"] = cs["ident"]
        m["h_in"] = np.ascontiguousarray(x[b, qd * TC:(qd + 1) * TC])
        m["pos"] = np.ascontiguousarray(positions[b, qd * TC:(qd + 1) * TC])
        in_maps.append(m)
    res = _run(_get_prog("A", S, TC), in_maps)
    DEPTH = np.asarray(inp["w_in"]).shape[0]
    for l in range(DEPTH):
        f32p = gather(res, "o_f32", 768, np.float32)
        qkv = gather(res, "o_qkv", 1280, BF)
        gates = gather(res, "o_gate", 24, np.float32)
        last = (l == DEPTH - 1)
        base = b_inputs(l)
        if not last:
            base.update(a_inputs(l + 1))
        in_maps = []
        for c in range(ncores):
            b, qd = divmod(c, NQ)
            m = dict(base)
            m["ident"] = cs["ident"]
            m.update(layout_B(qd, S, TC, f32p[b], qkv[b], gates[b]))
            for k in ("thr", "thrq", "cand", "forced", "poolfix", "lruflag"):
                m[k] = tabs[qd][k]
            m["h1_in"] = np.asarray(res[c]["h1_out"]).reshape(TC, D)
            if not last:
                m["pos"] = np.ascontiguousarray(positions[b, qd * TC:(qd + 1) * TC])
            in_maps.append(m)
        res = _run(_get_prog("B" if last else "BA", S, TC), in_maps)
    out = gather(res, "h_out", D, np.float32)
    return out
```

```python
import numpy as np
import concourse.bass as bass
import concourse.mybir as mybir

F32 = mybir.dt.float32
BF16 = mybir.dt.bfloat16
I32 = mybir.dt.int32
ALU = mybir.AluOpType
AF = mybir.ActivationFunctionType
AX = mybir.AxisListType


class Tok:
    __slots__ = ("name", "last_w", "readers", "sem", "dma_cnt", "fslot")

    def __init__(self, name):
        self.name = name
        self.last_w = None
        self.readers = []
        self.sem = None
        self.dma_cnt = 0


class Op:
    __slots__ = ("eng", "fn", "deps", "signal", "idx", "is_dma", "dtok", "dval", "dslot")

    def __init__(self, eng, fn, is_dma=False):
        self.eng = eng
        self.fn = fn
        self.deps = []
        self.signal = False
        self.idx = None
        self.is_dma = is_dma
        self.dtok = None
        self.dval = None


ENGS = ("pe", "act", "dve", "pool", "sp")


class Prog:
    def __init__(self, nc):
        self.nc = nc
        self.ops = []
        self.nslots = 0
        self.slot_cnt = {}
        self.free_slots = []
        self.active = []

    def tok(self, name):
        return Tok(name)

    def toks(self, name, n):
        return [Tok(f"{name}{i}") for i in range(n)]

    def _track(self, op, reads, writes):
        deps = []
        for t in reads:
            if t.last_w is not None:
                deps.append(t.last_w)
        for t in writes:
            if t.last_w is not None:
                deps.append(t.last_w)
            deps.extend(t.readers)
        seen = set()
        for d in deps:
            if d is op or id(d) in seen:
                continue
            seen.add(id(d))
            if d.eng == "pe" and op.eng == "pe" and not d.is_dma and not op.is_dma:
                continue
            op.deps.append(d)
            if not d.is_dma:
                d.signal = True
        for t in writes:
            t.last_w = op
            t.readers = []
        for t in reads:
            if t not in writes:
                t.readers.append(op)

    def add(self, eng, fn, reads=(), writes=()):
        op = Op(eng, fn)
        self._track(op, list(reads), list(writes))
        self.ops.append(op)
        return op

    def dma(self, eng, out, in_, reads, write, **kw):
        op = Op(eng, lambda e: e.dma_start(out=out, in_=in_, **kw), is_dma=True)
        self._track(op, list(reads), [write])
        if write.sem is None:
            if self.free_slots:
                write.sem = self.free_slots.pop()
            else:
                write.sem = self.nslots
                self.nslots += 1
                self.slot_cnt[write.sem] = 0
            self.active.append(write)
        self.slot_cnt[write.sem] += 16
        op.dtok = write
        op.dslot = write.sem
        op.dval = self.slot_cnt[write.sem]
        write.dma_cnt = op.dval
        self.ops.append(op)
        return op

    def barrier(self):
        last = {}
        for op in self.ops:
            if op.fn is None:
                continue
            if op.is_dma:
                last[("d", op.dslot)] = op
            else:
                last[("e", op.eng)] = op
        deps = list(last.values())
        for t in self.active:
            self.free_slots.append(t.sem)
            t.fslot = t.sem
            t.sem = None
        self.active = []
        for e in ENGS:
            if e == 'pool':
                continue
            b = Op(e, None)
            for d in deps:
                b.deps.append(d)
                if not d.is_dma:
                    d.signal = True
            self.ops.append(b)

    def emit(self, final_toks=()):
        nc = self.nc
        from contextlib import ExitStack
        with ExitStack() as es:
            esem = {e: es.enter_context(nc.semaphore(f"s_{e}")) for e in ENGS}
            dsem = [es.enter_context(nc.semaphore(f"d_{i}")) for i in range(self.nslots)]
            cnt = {e: 0 for e in ENGS}
            waited = {e: {} for e in ENGS}
            streams = {e: [] for e in ENGS}
            nwaits = 0
            for op in self.ops:
                waits = []
                for d in op.deps:
                    if d.is_dma:
                        sem, val = dsem[d.dslot], d.dval
                    else:
                        sem, val = esem[d.eng], d.idx
                    key = id(sem)
                    if waited[op.eng].get(key, 0) >= val:
                        continue
                    waited[op.eng][key] = val
                    waits.append((sem, val))
                nwaits += len(waits)
                inc = None
                if op.is_dma:
                    inc = (dsem[op.dslot], 16)
                elif op.signal:
                    cnt[op.eng] += 1
                    op.idx = cnt[op.eng]
                    inc = (esem[op.eng], 1)
                streams[op.eng].append((waits, op.fn, inc))
            fin = []
            for t in final_toks:
                fin.append((dsem[t.sem if t.sem is not None else t.fslot], t.dma_cnt))
            self.stats = dict(n_ops=len(self.ops), n_waits=nwaits,
                              per_eng={e: len(streams[e]) for e in ENGS})

            def run(eng_obj, items, final=None):
                for waits, fn, inc in items:
                    for sem, val in waits:
                        eng_obj.wait_ge(sem, val)
                    if fn is None:
                        continue
                    ins = fn(eng_obj)
                    if inc is not None:
                        ins.then_inc(inc[0], inc[1])
                if final:
                    for sem, val in final:
                        eng_obj.wait_ge(sem, val)

            with nc.Block() as block:
                @block.tensor
                def _(e):
                    run(e, streams["pe"])

                @block.scalar
                def _(e):
                    run(e, streams["act"])

                @block.vector
                def _(e):
                    run(e, streams["dve"])

                @block.gpsimd
                def _(e):
                    run(e, streams["pool"])

                @block.sync
                def _(e):
                    run(e, streams["sp"], fin)


import math
import numpy as np
from contextlib import ExitStack

D = 1024
DFF = 2816
NF = DFF // 128
ST = 512
EPS = 1e-6
IN_COLS = 2072
PI = math.pi


class Ctx:
    def __init__(self, nc, P, es):
        self.nc, self.P, self.es = nc, P, es
        self.psb = []
        for i in range(8):
            t = es.enter_context(nc.psum_tensor(f"ps{i}", [128, 512], F32))
            self.psb.append((t, P.tok(f"ps{i}")))
        self.n = 0

    def sb(self, es, name, shape, dt):
        self.n += 1
        return es.enter_context(self.nc.sbuf_tensor(f"{name}_{self.n}", shape, dt))


def load_consts(cx, es, ident_d):
    P = cx.P
    cx.ident = cx.sb(es, "ident", [128, 128], BF16)
    cx.t_ident = P.tok("ident")
    P.dma("sp", cx.ident[:], ident_d, [], cx.t_ident)


def load_cast(cx, dst, t_dst, src, stg, t_stg, q, eng):
    P = cx.P
    P.dma(q, stg, src, [], t_stg)
    if eng == "act":
        P.add("act", lambda e: e.activation(out=dst, in_=stg, func=AF.Copy), [t_stg], [t_dst])
    else:
        P.add("dve", lambda e: e.tensor_copy(out=dst, in_=stg), [t_stg], [t_dst])


class NormT:
    def __init__(self, cx, es, g_dram, tag):
        P = cx.P
        self.cx = cx
        self.gcol = cx.sb(es, tag + "gcol", [128, 8], F32)
        self.t_gcol = P.tok("gcol")
        P.dma("sp", self.gcol[:], g_dram.rearrange("(k p) -> p k", p=128), [], self.t_gcol,
              allow_slow_non_contiguous=True)
        self.xs = [cx.sb(es, tag + "xs", [128, D], BF16) for _ in range(4)]
        self.t_xs = P.toks("xs", 4)
        self.junk = cx.sb(es, tag + "junk", [128, D], BF16)
        self.ss = cx.sb(es, tag + "ss", [128, 8], F32)
        self.rstd = cx.sb(es, tag + "rstd", [128, 8], F32)
        self.xnT = cx.sb(es, tag + "xnT", [128, 8, ST], BF16)
        self.t_junk, self.t_ss, self.t_rstd, self.t_xnT = P.tok("junk"), P.tok("ss"), P.tok("rstd"), P.tok("xnT")

    def run(self, xb, txb):
        cx, P = self.cx, self.cx.P
        ss, rstd, junk, xs, xnT = self.ss, self.rstd, self.junk, self.xs, self.xnT
        P.add("dve", lambda e: e.memset(ss[:], 0.0), [], [self.t_ss])
        for j in range(4):
            P.add("act", lambda e, j=j: e.activation(out=junk[:], in_=xb[:, j, :], func=AF.Square,
                                                     accum_out=ss[:, j:j + 1]), [txb], [self.t_junk, self.t_ss])
        P.add("dve", lambda e: e.tensor_scalar(out=rstd[:, 0:4], in0=ss[:, 0:4], scalar1=1.0 / D, scalar2=EPS,
                                               op0=ALU.mult, op1=ALU.add), [self.t_ss], [self.t_rstd])
        P.add("act", lambda e: e.activation(out=rstd[:, 4:8], in_=rstd[:, 0:4], func=AF.Sqrt),
              [self.t_rstd], [self.t_rstd])
        P.add("dve", lambda e: e.reciprocal(out=rstd[:, 0:4], in_=rstd[:, 4:8]), [self.t_rstd], [self.t_rstd])
        for j in range(4):
            P.add("dve", lambda e, j=j: e.tensor_scalar(out=xs[j][:], in0=xb[:, j, :], scalar1=rstd[:, j:j + 1],
                                                        scalar2=None, op0=ALU.mult),
                  [txb, self.t_rstd], [self.t_xs[j]])
        for k in range(8):
            pt, tpt = cx.psb[6 + (k % 2)]
            for j in range(4):
                P.add("pe", lambda e, j=j, k=k, pt=pt: e.matmul(pt[:, j * 128:(j + 1) * 128],
                                                               lhsT=xs[j][:, k * 128:(k + 1) * 128],
                                                               rhs=cx.ident[:], start=True, stop=True),
                      [self.t_xs[j], cx.t_ident], [tpt])
            P.add("act", lambda e, k=k, pt=pt: e.activation(out=xnT[:, k, :], in_=pt[:], func=AF.Copy,
                                                            scale=self.gcol[:, k:k + 1]),
                  [tpt, self.t_gcol], [self.t_xnT])
        return xnT, self.t_xnT


class Tail:
    def __init__(self, cx, es, g_dram, scale, tag):
        P = cx.P
        self.cx = cx
        self.gpb = cx.sb(es, tag + "gpb", [128, D], F32)
        self.t_gpb = P.tok("gpb")
        P.dma("sp", self.gpb[:], g_dram.partition_broadcast(128), [], self.t_gpb)
        if scale != 1.0:
            P.add("dve", lambda e: e.tensor_scalar(out=self.gpb[:], in0=self.gpb[:], scalar1=scale, scalar2=None,
                                                   op0=ALU.mult), [self.t_gpb], [self.t_gpb])
        self.y = [cx.sb(es, tag + "y", [128, D], F32) for _ in range(2)]
        self.t_y = P.toks("y", 2)
        self.ot = [cx.sb(es, tag + "ot", [128, D], F32) for _ in range(2)]
        self.t_ot = P.toks("ot", 2)
        self.junk = cx.sb(es, tag + "junk2", [128, D], BF16)
        self.ss2 = cx.sb(es, tag + "ss2", [128, 4], F32)
        self.rstd2 = cx.sb(es, tag + "rstd2", [128, 2], F32)
        self.t_junk, self.t_ss2, self.t_rstd2 = P.tok("junk2"), P.tok("ss2"), P.tok("rstd2")
        self.cnt = 0

    def run(self, lhs_fn, nk, w_fn, lhs_toks, t_w, resid, t_resid, out_ap, t_out):
        cx, P = self.cx, self.cx.P
        i = self.cnt
        self.cnt += 1
        yb, tyb = self.y[i % 2], self.t_y[i % 2]
        ob, tob = self.ot[i % 2], self.t_ot[i % 2]
        ss2, rstd2, junk, gpb = self.ss2, self.rstd2, self.junk, self.gpb
        for hf in range(2):
            py, tpy = cx.psb[4 + hf]
            for k in range(nk):
                P.add("pe", lambda e, k=k, hf=hf, py=py: e.matmul(py[:], lhsT=lhs_fn(k), rhs=w_fn(k, hf),
                                                                 start=(k == 0), stop=(k == nk - 1)),
                      list(lhs_toks) + [t_w], [tpy])
            P.add("dve", lambda e, py=py, hf=hf: e.tensor_copy(out=yb[:, hf * 512:(hf + 1) * 512], in_=py[:]),
                  [tpy], [tyb])
        P.add("dve", lambda e: e.memset(ss2[:], 0.0), [], [self.t_ss2])
        P.add("act", lambda e: e.activation(out=junk[:], in_=yb[:], func=AF.Square, accum_out=ss2[:, 0:1]),
              [tyb], [self.t_junk, self.t_ss2])
        P.add("dve", lambda e: e.tensor_scalar(out=ss2[:, 1:2], in0=ss2[:, 0:1], scalar1=1.0 / D, scalar2=EPS,
                                               op0=ALU.mult, op1=ALU.add), [self.t_ss2], [self.t_ss2])
        P.add("act", lambda e: e.activation(out=ss2[:, 2:3], in_=ss2[:, 1:2], func=AF.Sqrt),
              [self.t_ss2], [self.t_ss2])
        P.add("dve", lambda e: e.reciprocal(out=rstd2[:, 0:1], in_=ss2[:, 2:3]), [self.t_ss2], [self.t_rstd2])
        P.add("dve", lambda e: e.scalar_tensor_tensor(out=ob[:], in0=yb[:], scalar=rstd2[:, 0:1], in1=gpb[:],
                                                      op0=ALU.mult, op1=ALU.mult),
              [tyb, self.t_rstd2, self.t_gpb], [tob])
        P.add("dve", lambda e: e.tensor_tensor(out=ob[:], in0=ob[:], in1=resid, op=ALU.add),
              [tob, t_resid], [tob])
        P.dma("sp", out_ap, ob[:], [tob], t_out)


def emit_ffn(cx, h_in, t_hin, h_out, wg, wu, wd, gpre, gpost, NT, tag):
    P, nc = cx.P, cx.nc
    nst = NT // ST
    t_hout = P.tok("hout")
    with ExitStack() as es:
        sb = lambda name, shape, dt: cx.sb(es, tag + name, shape, dt)
        nt = NormT(cx, es, gpre, tag)
        tl = Tail(cx, es, gpost, 0.5, tag)
        wd_sb = sb("wd", [128, NF, D], BF16)
        t_wd = P.tok("wd")
        wgu = [sb("wgu", [128, 2, 8, 256], BF16) for _ in range(2)]
        t_wgu = P.toks("wgu", 2)
        stg = [[sb("stg", [128, 8, 256], F32) for _ in range(2)] for _ in range(2)]
        t_stg = [[P.tok("stg") for _ in range(2)] for _ in range(2)]
        xt = [sb("xt", [128, 4, D], F32) for _ in range(2)]
        t_xt = P.toks("xt", 2)
        act = sb("act", [128, NF, ST], BF16)
        t_act = P.toks("act", NF)
        sil = [sb("sil", [128, ST], F32) for _ in range(2)]
        t_sil = P.toks("sil", 2)
        wd_v = wd.rearrange("(f p) d -> p f d", p=128)
        for ii, f0 in enumerate(range(0, NF, 2)):
            sg, tsg = stg[ii % 2][(ii // 2) % 2], t_stg[ii % 2][(ii // 2) % 2]
            sgv = sg[:].rearrange("p k c -> p (k c)").rearrange("p (f d) -> p f d", f=2)
            load_cast(cx, wd_sb[:, f0:f0 + 2, :], t_wd, wd_v[:, f0:f0 + 2, :], sgv, tsg,
                      "sp" if ii % 2 == 0 else "act", "act" if ii % 2 == 0 else "dve")
        wg_v = wg.rearrange("(k p) c -> p k c", p=128)
        wu_v = wu.rearrange("(k p) c -> p k c", p=128)
        h_in_v = h_in.rearrange("(s j p) d -> s p j d", p=128, j=4)
        h_out_v = h_out.rearrange("(s j p) d -> s j p d", p=128, j=4)

        def load_x(s):
            P.dma("sp", xt[s % 2][:], h_in_v[s], [t_hin], t_xt[s % 2])

        def load_w(g, slot):
            load_cast(cx, wgu[slot][:, 0], t_wgu[slot], wg_v[:, :, g * 256:(g + 1) * 256], stg[slot][0][:],
                      t_stg[slot][0], "sp", "act")
            load_cast(cx, wgu[slot][:, 1], t_wgu[slot], wu_v[:, :, g * 256:(g + 1) * 256], stg[slot][1][:],
                      t_stg[slot][1], "act", "dve")

        gi = 0
        load_x(0)
        load_w(0, 0)
        for s in range(nst):
            xb, txb = xt[s % 2], t_xt[s % 2]
            if s + 1 < nst:
                load_x(s + 1)
            xnT, t_xnT = nt.run(xb, txb)
            for g in range(NF // 2):
                slot = gi % 2
                gi += 1
                if g + 1 < NF // 2:
                    load_w(g + 1, gi % 2)
                elif s + 1 < nst:
                    load_w(0, gi % 2)
                for c in range(2):
                    f = g * 2 + c
                    pg, tpg = cx.psb[f % 2]
                    pu, tpu = cx.psb[2 + f % 2]
                    for k in range(8):
                        P.add("pe", lambda e, k=k, c=c, pg=pg, slot=slot: e.matmul(
                            pg[:], lhsT=wgu[slot][:, 0, k, c * 128:(c + 1) * 128], rhs=xnT[:, k, :],
                            start=(k == 0), stop=(k == 7)), [t_wgu[slot], t_xnT], [tpg])
                    for k in range(8):
                        P.add("pe", lambda e, k=k, c=c, pu=pu, slot=slot: e.matmul(
                            pu[:], lhsT=wgu[slot][:, 1, k, c * 128:(c + 1) * 128], rhs=xnT[:, k, :],
                            start=(k == 0), stop=(k == 7)), [t_wgu[slot], t_xnT], [tpu])
                    sl, tsl = sil[f % 2], t_sil[f % 2]
                    P.add("act", lambda e, pg=pg, sl=sl: e.activation(out=sl[:], in_=pg[:], func=AF.Silu),
                          [tpg], [tsl])
                    P.add("dve", lambda e, pu=pu, sl=sl, f=f: e.tensor_tensor(out=act[:, f, :], in0=sl[:],
                                                                             in1=pu[:], op=ALU.mult),
                          [tsl, tpu], [t_act[f]])
            for j in range(4):
                tl.run(lambda k, j=j: act[:, k, j * 128:(j + 1) * 128], NF,
                       lambda k, hf: wd_sb[:, k, hf * 512:(hf + 1) * 512], t_act, t_wd,
                       xb[:, j, :], txb, h_out_v[s, j], t_hout)
        P.barrier()
    return t_hout


def emit_proj(cx, h_in, t_hin, w_in, gpre, pos, invtab_d, o_f32, o_qkv, o_gate, NT, tag):
    P, nc = cx.P, cx.nc
    nst = NT // ST
    t_o = [P.tok("of32"), P.tok("oqkv"), P.tok("ogate")]
    with ExitStack() as es:
        sb = lambda name, shape, dt: cx.sb(es, tag + name, shape, dt)
        nt = NormT(cx, es, gpre, tag)
        win = sb("win", [128, 8, IN_COLS], BF16)
        t_win = P.tok("win")
        stg = [sb("stg", [128, IN_COLS], F32) for _ in range(2)]
        t_stg = P.toks("stg", 2)
        w_v = w_in.rearrange("(k p) c -> p k c", p=128)
        for k in range(8):
            load_cast(cx, win[:, k, :], t_win, w_v[:, k, :], stg[k % 2][:], t_stg[k % 2],
                      "sp" if k % 2 == 0 else "act", "act" if k % 2 == 0 else "dve")
        xt = [sb("xt", [128, 4, D], F32) for _ in range(2)]
        t_xt = P.toks("xt", 2)
        invtab = sb("invtab", [128, 32], F32)
        t_inv = P.tok("inv")
        P.dma("sp", invtab[:], invtab_d, [], t_inv)
        posi = sb("posi", [128, NT // 128], I32)
        posf = sb("posf", [128, NT // 128], F32)
        t_pos = P.tok("pos")
        P.dma("sp", posi[:], pos.rearrange("(t p) -> p t", p=128), [], t_pos, allow_slow_non_contiguous=True)
        P.add("dve", lambda e: e.tensor_copy(out=posf[:], in_=posi[:]), [t_pos], [t_pos])
        pr = [sb("pr", [128, IN_COLS], F32) for _ in range(2)]
        t_pr = P.toks("pr", 2)
        pof = [sb("pof", [128, 768], F32) for _ in range(2)]
        poq = [sb("poq", [128, 1280], BF16) for _ in range(2)]
        pog = [sb("pog", [128, 24], F32) for _ in range(2)]
        t_pof, t_poq, t_pog = P.toks("pof", 2), P.toks("poq", 2), P.toks("pog", 2)
        tr = {n: sb(n, [128, 32], F32) for n in ("ang", "kf", "r", "fl", "sin", "cos", "ang2")}
        ki = sb("ki", [128, 32], I32)
        t_trig = P.tok("trig")
        tmp = [sb("tmp", [128, 10, 32], F32) for _ in range(4)]
        t_tmp = P.tok("tmp")
        h_in_v = h_in.rearrange("(s j p) d -> s p j d", p=128, j=4)
        of_v = o_f32.rearrange("(t p) c -> t p c", p=128)
        oq_v = o_qkv.rearrange("(t p) c -> t p c", p=128)
        og_v = o_gate.rearrange("(t p) c -> t p c", p=128)
        cgroups = [(0, 512), (512, 512), (1024, 512), (1536, 512), (2048, 24)]

        def load_x(s):
            P.dma("sp", xt[s % 2][:], h_in_v[s], [t_hin], t_xt[s % 2])

        def trig(dst, src, tt):
            ang, kf, r, fl = src, tr["kf"], tr["r"], tr["fl"]
            P.add("dve", lambda e: e.tensor_scalar(out=kf[:], in0=ang[:], scalar1=1.0 / (2 * PI), scalar2=None,
                                                   op0=ALU.mult), [tt], [tt])
            P.add("dve", lambda e: e.tensor_copy(out=ki[:], in_=kf[:]), [tt], [tt])
            P.add("dve", lambda e: e.tensor_copy(out=kf[:], in_=ki[:]), [tt], [tt])
            P.add("dve", lambda e: e.scalar_tensor_tensor(out=r[:], in0=kf[:], scalar=-2 * PI, in1=ang[:],
                                                          op0=ALU.mult, op1=ALU.add), [tt], [tt])
            P.add("dve", lambda e: e.tensor_scalar(out=fl[:], in0=r[:], scalar1=PI, scalar2=None, op0=ALU.is_gt),
                  [tt], [tt])
            P.add("dve", lambda e: e.scalar_tensor_tensor(out=kf[:], in0=fl[:], scalar=-2 * PI, in1=r[:],
                                                          op0=ALU.mult, op1=ALU.add), [tt], [tt])
            P.add("dve", lambda e: e.tensor_scalar(out=fl[:], in0=kf[:], scalar1=-PI, scalar2=None, op0=ALU.is_lt),
                  [tt], [tt])
            P.add("dve", lambda e: e.scalar_tensor_tensor(out=r[:], in0=fl[:], scalar=2 * PI, in1=kf[:],
                                                          op0=ALU.mult, op1=ALU.add), [tt], [tt])
            P.add("act", lambda e: e.activation(out=dst[:], in_=r[:], func=AF.Sin), [tt], [tt])

        load_x(0)
        ti = 0
        for s in range(nst):
            xb, txb = xt[s % 2], t_xt[s % 2]
            if s + 1 < nst:
                load_x(s + 1)
            xnT, t_xnT = nt.run(xb, txb)
            for j in range(4):
                tix = s * 4 + j
                prb, tprb = pr[tix % 2], t_pr[tix % 2]
                for gi_, (c0, cw) in enumerate(cgroups):
                    pp, tpp = cx.psb[gi_ % 4]
                    for k in range(8):
                        P.add("pe", lambda e, k=k, j=j, c0=c0, cw=cw, pp=pp: e.matmul(
                            pp[:, 0:cw], lhsT=xnT[:, k, j * 128:(j + 1) * 128], rhs=win[:, k, c0:c0 + cw],
                            start=(k == 0), stop=(k == 7)), [t_xnT, t_win], [tpp])
                    if gi_ % 2 == 0:
                        P.add("act", lambda e, pp=pp, c0=c0, cw=cw, prb=prb: e.activation(
                            out=prb[:, c0:c0 + cw], in_=pp[:, 0:cw], func=AF.Copy), [tpp], [tprb])
                    else:
                        P.add("dve", lambda e, pp=pp, c0=c0, cw=cw, prb=prb: e.tensor_copy(
                            out=prb[:, c0:c0 + cw], in_=pp[:, 0:cw]), [tpp], [tprb])
                ang = tr["ang"]
                P.add("dve", lambda e, tix=tix: e.tensor_scalar(out=ang[:], in0=invtab[:],
                                                                scalar1=posf[:, tix:tix + 1], scalar2=None,
                                                                op0=ALU.mult), [t_inv, t_pos], [t_trig])
                trig(tr["sin"], ang, t_trig)
                P.add("dve", lambda e: e.tensor_scalar(out=tr["ang2"][:], in0=ang[:], scalar1=PI / 2, scalar2=None,
                                                       op0=ALU.add), [t_trig], [t_trig])
                trig(tr["cos"], tr["ang2"], t_trig)
                fo, tfo = pof[tix % 2], t_pof[tix % 2]
                qo, tqo = poq[tix % 2], t_poq[tix % 2]
                go, tgo = pog[tix % 2], t_pog[tix % 2]
                P.add("act", lambda e, fo=fo, prb=prb: e.activation(out=fo[:], in_=prb[:, 0:768], func=AF.Copy),
                      [tprb], [tfo])
                P.add("act", lambda e, go=go, prb=prb: e.activation(out=go[:], in_=prb[:, 2048:2072],
                                                                    func=AF.Sigmoid), [tprb], [tgo])
                for c0 in (1408, 1664, 1920):
                    P.add("dve", lambda e, qo=qo, prb=prb, c0=c0: e.tensor_copy(
                        out=qo[:, c0 - 768:c0 - 768 + 128], in_=prb[:, c0:c0 + 128]), [tprb], [tqo])
                for (c0, nh) in ((768, 10), (1536, 2), (1792, 2)):
                    xv = prb[:, c0:c0 + nh * 64].rearrange("p (h t f) -> p h t f", t=2, f=32)
                    ov = qo[:, c0 - 768:c0 - 768 + nh * 64].rearrange("p (h t f) -> p h t f", t=2, f=32)
                    x1, x2 = xv[:, :, 0, :], xv[:, :, 1, :]
                    cb = tr["cos"][:].unsqueeze(1).broadcast_to([128, nh, 32])
                    sb_ = tr["sin"][:].unsqueeze(1).broadcast_to([128, nh, 32])
                    a, b, c, d = (t[:, 0:nh, :] for t in tmp)
                    P.add("dve", lambda e, a=a, x1=x1, cb=cb: e.tensor_tensor(out=a, in0=x1, in1=cb, op=ALU.mult),
                          [tprb, t_trig], [t_tmp])
                    P.add("dve", lambda e, b=b, x2=x2, sb_=sb_: e.tensor_tensor(out=b, in0=x2, in1=sb_, op=ALU.mult),
                          [tprb, t_trig], [t_tmp])
                    P.add("dve", lambda e, a=a, b=b, ov=ov: e.tensor_tensor(out=ov[:, :, 0, :], in0=a, in1=b,
                                                                           op=ALU.subtract), [t_tmp], [tqo])
                    P.add("dve", lambda e, c=c, x2=x2, cb=cb: e.tensor_tensor(out=c, in0=x2, in1=cb, op=ALU.mult),
                          [tprb, t_trig], [t_tmp])
                    P.add("dve", lambda e, d=d, x1=x1, sb_=sb_: e.tensor_tensor(out=d, in0=x1, in1=sb_, op=ALU.mult),
                          [tprb, t_trig], [t_tmp])
                    P.add("dve", lambda e, c=c, d=d, ov=ov: e.tensor_tensor(out=ov[:, :, 1, :], in0=c, in1=d,
                                                                           op=ALU.add), [t_tmp], [tqo])
                P.dma("sp", of_v[tix], fo[:], [tfo], t_o[0])
                P.dma("act", oq_v[tix], qo[:], [tqo], t_o[1])
                P.dma("sp", og_v[tix], go[:], [tgo], t_o[2])
        P.barrier()
    return t_o


POOL_WINDOWS = (2, 4, 8, 16)
SCALE = 0.125
BIG = 1.0e9


def _bd_build(cx, es, P, name, w_d, idx0, idx1):
    f = cx.sb(es, name + "f", [128, 128], F32)
    b = cx.sb(es, name + "b", [128, 128], BF16)
    t = P.tok(name)
    P.add("dve", lambda e: e.memset(f[:], 0.0), [], [t])
    P.dma("sp", f[0:64, 0:64], w_d[idx0], [], t)
    P.dma("sp", f[64:128, 64:128], w_d[idx1], [], t)
    P.add("dve", lambda e: e.tensor_copy(out=b[:], in_=f[:]), [t], [t])
    return b, t


def emit_pool(cx, xpT_d, poolfix_d, pool_w, pool_scale, yT_d, t_yT, TC):
    P, nc = cx.P, cx.nc
    L = 16 + TC
    with ExitStack() as es:
        sb = lambda n, s, d: cx.sb(es, "pl" + n, s, d)
        xp = sb("xp", [128, 2, L], F32)
        t_xp = P.tok("xp")
        P.dma("sp", xp[:], xpT_d, [], t_xp)
        fix = sb("fix", [128, 2, 16], F32)
        t_fix = P.tok("fix")
        P.dma("sp", fix[:], poolfix_d, [], t_fix)
        psc = sb("psc", [128, 2], F32)
        t_psc = P.tok("psc")
        P.dma("sp", psc[:], pool_scale.rearrange("(c p) -> p c", p=128), [], t_psc, allow_slow_non_contiguous=True)
        wA = sb("wA", [128, L], F32)
        wB = sb("wB", [128, L], F32)
        t_wA, t_wB = P.tok("wA"), P.tok("wB")
        dfb = sb("dfb", [128, TC], BF16)
        t_dfb = P.tok("dfb")
        yo = [sb("yo", [128, 512], BF16) for _ in range(2)]
        t_yo = P.toks("yo", 2)
        for cc in range(2):
            bd, t_bd = _bd_build(cx, es, P, f"plbd{cc}", pool_w, 2 * cc, 2 * cc + 1)
            for hh in range(2):
                w = POOL_WINDOWS[2 * cc + hh]
                rows = slice(hh * 64, hh * 64 + 64)
                src, tsrc = xp[rows, cc, :], t_xp
                bufs = [(wA, t_wA), (wB, t_wB)]
                lo, sh, bi = 0, 1, 0
                while sh < w:
                    dst, tdst = bufs[bi]
                    lo2 = lo + sh
                    P.add("dve", lambda e, dst=dst, src=src, lo2=lo2, sh=sh, rows=rows: e.tensor_tensor(
                        out=dst[rows, lo2:L], in0=src[:, lo2:L], in1=src[:, lo2 - sh:L - sh], op=ALU.add),
                        [tsrc], [tdst])
                    src, tsrc = dst[rows, :], tdst
                    lo, sh, bi = lo2, sh * 2, 1 - bi
                dst, tdst = bufs[bi]
                P.add("dve", lambda e, dst=dst, src=src, rows=rows, w=w: e.tensor_scalar(
                    out=dst[rows, 16:L], in0=src[:, 16:L], scalar1=1.0 / w, scalar2=None, op0=ALU.mult),
                    [tsrc], [tdst])
                P.add("dve", lambda e, dst=dst, rows=rows, cc=cc: e.tensor_tensor(
                    out=dst[rows, 16:32], in0=dst[rows, 16:32], in1=fix[rows, cc, :], op=ALU.mult),
                    [tdst, t_fix], [tdst])
                P.add("dve", lambda e, dst=dst, rows=rows, cc=cc: e.tensor_tensor(
                    out=dfb[rows, :], in0=dst[rows, 16:L], in1=xp[rows, cc, 16:L], op=ALU.subtract),
                    [tdst, t_xp], [t_dfb])
            for ch in range(TC // 512):
                pp, tpp = cx.psb[ch % 2]
                P.add("pe", lambda e, pp=pp, ch=ch, bd=bd: e.matmul(pp[:], lhsT=bd[:], rhs=dfb[:, ch * 512:(ch + 1) * 512],
                                                                   start=True, stop=True), [t_bd, t_dfb], [tpp])
                yb, tyb = yo[ch % 2], t_yo[ch % 2]
                P.add("act", lambda e, pp=pp, yb=yb, cc=cc: e.activation(out=yb[:], in_=pp[:], func=AF.Copy,
                                                                        scale=psc[:, cc:cc + 1]), [tpp, t_psc], [tyb])
                P.dma("sp", yT_d[:, cc, ch * 512:(ch + 1) * 512], yb[:], [tyb], t_yT)
        P.barrier()


def emit_lru(cx, xlT_d, glT_d, lruflag_d, conv_w, conv_b, w_r, b_r, w_i, b_i, lam, yT_d, t_yT, S, TC):
    P, nc = cx.P, cx.nc
    NLC = S // 512
    own0 = NLC - TC // 512
    with ExitStack() as es:
        sb = lambda n, s, d: cx.sb(es, "lr" + n, s, d)
        flag = sb("flag", [128, NLC], F32)
        t_flag = P.tok("flag")
        P.dma("sp", flag[:], lruflag_d, [], t_flag)
        one = sb("one", [128, 1], F32)
        t_one = P.tok("one")
        P.add("dve", lambda e: e.memset(one[:], 1.0), [], [t_one])
        for cc in range(2):
            cw = sb("cw", [128, 4], F32)
            cst = sb("cst", [128, 8], F32)
            t_c = P.tok("lrc")
            P.dma("sp", cw[:], conv_w[:, cc * 128:(cc + 1) * 128].rearrange("k p -> p k"), [], t_c,
                  allow_slow_non_contiguous=True)
            for ci, v in enumerate((conv_b, b_r, b_i, lam)):
                P.dma("sp", cst[:, ci:ci + 1], v[cc * 128:(cc + 1) * 128].rearrange("(p o) -> p o", o=1), [], t_c,
                      allow_slow_non_contiguous=True)
            P.add("act", lambda e, cst=cst: e.activation(out=cst[:, 4:5], in_=cst[:, 3:4], func=AF.Exp, scale=-1.0),
                  [t_c], [t_c])
            P.add("dve", lambda e, cst=cst: e.tensor_scalar(out=cst[:, 4:5], in0=cst[:, 4:5], scalar1=1.0,
                                                            scalar2=None, op0=ALU.add), [t_c], [t_c])
            P.add("act", lambda e, cst=cst: e.activation(out=cst[:, 5:6], in_=cst[:, 4:5], func=AF.Ln), [t_c], [t_c])
            P.add("dve", lambda e, cst=cst: e.tensor_scalar(out=cst[:, 6:7], in0=cst[:, 5:6], scalar1=-8.0,
                                                            scalar2=None, op0=ALU.mult), [t_c], [t_c])
            P.add("dve", lambda e, cst=cst: e.tensor_scalar(out=cst[:, 7:8], in0=cst[:, 5:6], scalar1=-16.0,
                                                            scalar2=None, op0=ALU.mult), [t_c], [t_c])
            bdr, t_bdr = _bd_build(cx, es, P, f"bdr{cc}", w_r, 2 * cc, 2 * cc + 1)
            bdi, t_bdi = _bd_build(cx, es, P, f"bdi{cc}", w_i, 2 * cc, 2 * cc + 1)
            hst = sb("hst", [128, 2], F32)
            t_hst = P.tok("hst")
            P.add("dve", lambda e, hst=hst: e.memset(hst[:], 0.0), [], [t_hst])
            X = [sb("X", [128, 516], F32) for _ in range(2)]
            t_X = P.toks("X", 2)
            G = [sb("G", [128, 512], F32) for _ in range(2)]
            t_G = P.toks("G", 2)
            w = {n: sb(n, [128, 512], F32) for n in ("xc", "r", "i", "a", "a2", "m", "t", "b", "h", "g2", "u", "sg", "ge")}
            tw = {n: P.tok(n) for n in w}
            xcb = sb("xcb", [128, 512], BF16)
            t_xcb = P.tok("xcb")
            yo = [sb("yo", [128, 512], BF16) for _ in range(2)]
            t_yo = P.toks("yo", 2)
            for ch in range(NLC):
                Xb, tX = X[ch % 2], t_X[ch % 2]
                P.dma("sp", Xb[:], xlT_d[:, cc, ch * 512:ch * 512 + 516], [], tX)
                xc = w["xc"]
                P.add("dve", lambda e, Xb=Xb, cw=cw, cst=cst: e.tensor_scalar(
                    out=xc[:], in0=Xb[:, 4:516], scalar1=cw[:, 3:4], scalar2=cst[:, 0:1], op0=ALU.mult, op1=ALU.add),
                    [tX, t_c], [tw["xc"]])
                for k in range(3):
                    P.add("dve", lambda e, Xb=Xb, cw=cw, k=k: e.scalar_tensor_tensor(
                        out=xc[:], in0=Xb[:, 1 + k:513 + k], scalar=cw[:, k:k + 1], in1=xc[:], op0=ALU.mult,
                        op1=ALU.add), [tX, t_c, tw["xc"]], [tw["xc"]])
                P.add("act", lambda e: e.activation(out=xcb[:], in_=xc[:], func=AF.Copy), [tw["xc"]], [t_xcb])
                pr, tpr = cx.psb[0]
                pi_, tpi = cx.psb[1]
                P.add("pe", lambda e, bdr=bdr, pr=pr: e.matmul(pr[:], lhsT=bdr[:], rhs=xcb[:], start=True, stop=True),
                      [t_bdr, t_xcb], [tpr])
                P.add("pe", lambda e, bdi=bdi, pi_=pi_: e.matmul(pi_[:], lhsT=bdi[:], rhs=xcb[:], start=True, stop=True),
                      [t_bdi, t_xcb], [tpi])
                P.add("act", lambda e, pr=pr, cst=cst: e.activation(out=w["r"][:], in_=pr[:], func=AF.Sigmoid,
                                                                    bias=cst[:, 1:2]), [tpr, t_c], [tw["r"]])
                P.add("act", lambda e, pi_=pi_, cst=cst: e.activation(out=w["i"][:], in_=pi_[:], func=AF.Sigmoid,
                                                                      bias=cst[:, 2:3]), [tpi, t_c], [tw["i"]])
                P.add("act", lambda e, cst=cst: e.activation(out=w["a"][:], in_=w["r"][:], func=AF.Exp,
                                                             scale=cst[:, 6:7]), [tw["r"], t_c], [tw["a"]])
                P.add("act", lambda e, cst=cst: e.activation(out=w["a2"][:], in_=w["r"][:], func=AF.Exp,
                                                             scale=cst[:, 7:8]), [tw["r"], t_c], [tw["a2"]])
                P.add("act", lambda e: e.activation(out=w["m"][:], in_=w["a2"][:], func=AF.Sqrt, scale=-1.0,
                                                    bias=one[:, 0:1]), [tw["a2"], t_one], [tw["m"]])
                P.add("dve", lambda e: e.tensor_tensor(out=w["t"][:], in0=w["i"][:], in1=xc[:], op=ALU.mult),
                      [tw["i"], tw["xc"]], [tw["t"]])
                P.add("dve", lambda e, ch=ch: e.scalar_tensor_tensor(out=w["b"][:], in0=w["t"][:],
                                                                    scalar=flag[:, ch:ch + 1], in1=w["m"][:],
                                                                    op0=ALU.mult, op1=ALU.mult),
                      [tw["t"], tw["m"], t_flag], [tw["b"]])
                P.add("dve", lambda e, hst=hst: e.tensor_tensor_scan(out=w["h"][:], data0=w["a"][:], data1=w["b"][:],
                                                                    initial=hst[:, 0:1], op0=ALU.mult, op1=ALU.add),
                      [tw["a"], tw["b"], t_hst], [tw["h"]])
                P.add("dve", lambda e, hst=hst: e.tensor_copy(out=hst[:, 0:1], in_=w["h"][:, 511:512]),
                      [tw["h"]], [t_hst])
                if ch >= own0:
                    oc = ch - own0
                    Gb, tG = G[oc % 2], t_G[oc % 2]
                    P.dma("act", Gb[:], glT_d[:, cc, oc * 512:(oc + 1) * 512], [], tG)
                    P.add("dve", lambda e, Gb=Gb: e.tensor_tensor(out=w["g2"][:], in0=Gb[:], in1=Gb[:], op=ALU.mult),
                          [tG], [tw["g2"]])
                    P.add("dve", lambda e: e.tensor_scalar(out=w["u"][:], in0=w["g2"][:], scalar1=0.044715,
                                                           scalar2=1.0, op0=ALU.mult, op1=ALU.add),
                          [tw["g2"]], [tw["u"]])
                    P.add("dve", lambda e, Gb=Gb: e.tensor_tensor(out=w["g2"][:], in0=w["u"][:], in1=Gb[:],
                                                                  op=ALU.mult), [tw["u"], tG], [tw["g2"]])
                    P.add("act", lambda e: e.activation(out=w["sg"][:], in_=w["g2"][:], func=AF.Sigmoid,
                                                        scale=1.5957691216057308), [tw["g2"]], [tw["sg"]])
                    P.add("dve", lambda e, Gb=Gb: e.tensor_tensor(out=w["ge"][:], in0=w["sg"][:], in1=Gb[:],
                                                                  op=ALU.mult), [tw["sg"], tG], [tw["ge"]])
                    yb, tyb = yo[oc % 2], t_yo[oc % 2]
                    P.add("dve", lambda e, yb=yb: e.tensor_tensor(out=yb[:], in0=w["h"][:], in1=w["ge"][:],
                                                                  op=ALU.mult), [tw["h"], tw["ge"]], [tyb])
                    P.dma("sp", yT_d[:, 2 + cc, oc * 512:(oc + 1) * 512], yb[:], [tyb], t_yT)
        P.barrier()


def emit_attn(cx, a, yT_d, t_yT, S, TC):
    P, nc = cx.P, cx.nc
    TCT, NCH, NSEL, NCM = TC // 128, S // 128, S // 64, S // 16
    NCC = max(NCM // 128, 1)
    JH = min(NSEL, 128)
    NJH = max(NSEL // 128, 1)
    NB = max(NCM // 512, 1)
    NBW = min(NCM, 512)
    with ExitStack() as es:
        sb = lambda n, s, d: cx.sb(es, "at" + n, s, d)
        T = P.tok
        QT = sb("QT", [128, 4, TC], BF16); t_QT = T("QT")
        P.dma("sp", QT[:], a["QT"], [], t_QT)
        KsT = sb("KsT", [128, S], BF16); t_KsT = T("KsT")
        P.dma("act", KsT[:], a["KsT"], [], t_KsT)
        VsE = sb("VsE", [128, NCH, 2, 65], BF16); t_VsE = T("VsE")
        P.add("dve", lambda e: e.memset(VsE[:], 1.0), [], [t_VsE])
        P.dma("sp", VsE[:, :, :, 0:64], a["Vs"].rearrange("p c (h d) -> p c h d", h=2), [], t_VsE)
        KwT = sb("KwT", [128, 512 + TC], BF16); t_KwT = T("KwT")
        P.dma("act", KwT[:], a["KwT"], [], t_KwT)
        VwE = sb("VwE", [128, 4 + TCT, 2, 65], BF16); t_VwE = T("VwE")
        P.add("dve", lambda e: e.memset(VwE[:], 1.0), [], [t_VwE])
        P.dma("sp", VwE[:, :, :, 0:64], a["Vw"].rearrange("p c (h d) -> p c h d", h=2), [], t_VwE)
        Ebig = sb("Ebig", [128, 8192], BF16); t_cst = T("cst")
        P.dma("act", Ebig[:], a["Ebig"], [], t_cst)
        Akq = sb("Akq", [128, 128], F32); Ac = sb("Ac", [128, 128], F32); identf = sb("idf", [128, 128], F32)
        iota16 = sb("iota16", [128, NCM], F32)
        for dst, src in ((Akq, "Akq"), (Ac, "Ac"), (identf, "identf"), (iota16, "iota16")):
            P.dma("sp", dst[:], a[src], [], t_cst)
        gate = sb("gate", [128, TCT, 24], F32)
        P.dma("sp", gate[:], a["gate"], [], t_cst)
        NTHR = a["thr"].shape[1]
        thr = sb("thr", [128, NTHR], F32)
        P.dma("sp", thr[:], a["thr"], [], t_cst)
        thrq = sb("thrq", [128, TCT], F32)
        P.dma("sp", thrq[:], a["thrq"], [], t_cst)
        o_c, o_d, o_w = 0, TCT * NCC, TCT * NCC + TCT * 4
        KcmpT = sb("KcmpT", [128, NCM], BF16); t_Kcmp = T("Kcmp")
        VcE = sb("VcE", [128, NCC, 2, 65], BF16); t_VcE = T("VcE")
        P.add("dve", lambda e: e.memset(VcE[:], 1.0), [], [t_VcE])
        with ExitStack() as es2:
            sb2 = lambda n, s, d: cx.sb(es2, "cm" + n, s, d)
            src = sb2("src", [128, S + 16], BF16); t_src = T("src")
            wst = sb2("wst", [128, 32, 64], F32); t_wst = T("wst")
            BD = sb2("BD", [128, 32, 128], BF16); t_BD = T("BD")
            pes = sb2("pes", [128, 32], F32); PEt = sb2("PEt", [128, 32], BF16); t_pe = T("pe")
            cpe = sb2("cpe", [128, 2], F32); t_cpe = T("cpe")
            VcT = sb2("VcT", [128, NCM], BF16); t_VcT = T("VcT")
            for hh in range(2):
                P.dma("sp", pes[hh * 64:(hh + 1) * 64, :], a["cmp_pe"].rearrange("l d -> d l"), [], t_pe,
                      allow_slow_non_contiguous=True)
            P.add("dve", lambda e: e.tensor_copy(out=PEt[:], in_=pes[:]), [t_pe], [t_pe])
            for which, (srcname, wname) in enumerate((("KcT", "cmp_w_k"), ("VcT", "cmp_w_v"))):
                P.add("dve", lambda e: e.memset(src[:, S:S + 16], 0.0), [], [t_src])
                P.dma("sp", src[:, 0:S], a[srcname], [], t_src)
                for hh in range(2):
                    P.dma("act", wst[hh * 64:(hh + 1) * 64, :, :], a[wname].rearrange("l d e -> d l e"), [], t_wst)
                P.add("dve", lambda e: e.memset(BD[:], 0.0), [], [t_BD])
                for hh in range(2):
                    P.add("dve", lambda e, hh=hh: e.tensor_copy(out=BD[hh * 64:(hh + 1) * 64, :, hh * 64:(hh + 1) * 64],
                                                                in_=wst[hh * 64:(hh + 1) * 64, :, :]), [t_wst], [t_BD])
                pc, tpc = cx.psb[2]
                for l in range(32):
                    P.add("pe", lambda e, l=l, pc=pc: e.matmul(pc[:, 0:1], lhsT=BD[:, l, :], rhs=PEt[:, l:l + 1],
                                                               start=(l == 0), stop=(l == 31)), [t_BD, t_pe], [tpc])
                P.add("dve", lambda e, pc=pc, which=which: e.tensor_copy(out=cpe[:, which:which + 1], in_=pc[:, 0:1]),
                      [tpc], [t_cpe])
                sv = src[:].rearrange("p (n s) -> p n s", s=16)
                dstT = KcmpT if which == 0 else VcT
                tdst = t_Kcmp if which == 0 else t_VcT
                for nb in range(NB):
                    pk, tpk = cx.psb[nb % 2]
                    n0 = nb * NBW
                    for l in range(32):
                        rhs = sv[:, n0:n0 + NBW, l] if l < 16 else sv[:, n0 + 1:n0 + 1 + NBW, l - 16]
                        P.add("pe", lambda e, l=l, pk=pk, rhs=rhs: e.matmul(pk[:, 0:NBW], lhsT=BD[:, l, :], rhs=rhs,
                                                                          start=(l == 0), stop=(l == 31)),
                              [t_BD, t_src], [tpk])
                    P.add("act", lambda e, pk=pk, n0=n0, dstT=dstT, which=which: e.activation(
                        out=dstT[:, n0:n0 + NBW], in_=pk[:, 0:NBW], func=AF.Identity, bias=cpe[:, which:which + 1]),
                        [tpk, t_cpe], [tdst])
            for c in range(NCC):
                pv, tpv = cx.psb[3]
                P.add("pe", lambda e, c=c, pv=pv: e.matmul(pv[:, 0:128], lhsT=VcT[:, c * 128:(c + 1) * 128],
                                                           rhs=cx.ident[:], start=True, stop=True),
                      [t_VcT, cx.t_ident], [tpv])
                P.add("dve", lambda e, c=c, pv=pv: e.tensor_copy(
                    out=VcE[:, c, :, 0:64], in_=pv[:, 0:128].rearrange("p (h d) -> p h d", h=2)), [tpv], [t_VcE])
            P.barrier()
        cand = [sb("cand", [128, NSEL], F32) for _ in range(2)]; t_cand = P.toks("cand", 2)
        forc = [sb("forc", [128, NSEL], F32) for _ in range(2)]; t_forc = P.toks("forc", 2)
        Etm = sb("Etm", [128, NCM], F32); t_Etm = T("Etm")
        Pm_tm = sb("Pmtm", [128, NCM], F32); t_Pmtm = T("Pmtm")
        st = sb("st", [128, 16], F32); t_st = T("st")
        t4 = sb("t4", [128, NSEL], F32); t_t4 = T("t4")
        imp = sb("imp", [128, NSEL], F32); t_imp = T("imp")
        score = sb("score", [128, NSEL], F32); sc2 = sb("sc2", [128, NSEL], F32); t_sc = T("score")
        m8 = sb("m8", [128, 16], F32); t_m8 = T("m8")
        sel = sb("sel", [128, NSEL], BF16); t_sel = T("sel")
        selT = sb("selT", [128, NJH, 128], BF16); t_selT = T("selT")
        P.add("dve", lambda e: e.memset(selT[:], 0.0), [], [t_selT])
        E = [sb("E", [128, 4, 128], BF16) for _ in range(4)]; t_E = P.toks("E", 4)
        Pm = [sb("Pm", [128, 4, 128], BF16) for _ in range(4)]; t_Pm = P.toks("Pm", 4)
        SBK = [0, 1, 6]
        MBK = [2, 3, 7]
        LA = 2
        OTs3 = [sb("OTs3", [65, 512], F32) for _ in range(3)]; t_OTs3 = P.toks("OTs3", 3)
        OTs = sb("OTs", [65, 512], F32); t_OTs = T("OTs")
        w4 = sb("w4", [128, 8], F32); t_w4 = T("w4")
        tmpo = sb("tmpo", [128, 4, 64], F32); t_tmpo = T("tmpo")
        oacc = sb("oacc", [128, 512], F32); t_oacc = T("oacc")
        ob = sb("ob", [128, 512], BF16); t_ob = T("ob")
        oT = [sb("oT", [128, 128], BF16) for _ in range(2)]; t_oT = P.toks("oT", 2)
        step = [0]

        def run_steps(steps, qv):
            n = len(steps)
            fr = [None] * n

            def front(k):
                i = step[0]; step[0] += 1
                stp = steps[k]
                ps, tps = cx.psb[SBK[i % 3]]
                P.add("pe", lambda e, ps=ps, stp=stp: e.matmul(ps[:], lhsT=stp["lhsK"], rhs=qv, start=True, stop=True),
                      [stp["tK"], t_QT], [tps])
                aux = stp["pre"](i) if stp.get("pre") else None
                fr[k] = (i, ps, tps, aux)

            def back(k):
                stp = steps[k]
                i, ps, tps, aux = fr[k]
                Eb, tE = E[i % 3], t_E[i % 3]
                P.add("act", lambda e, ps=ps, Eb=Eb: e.activation(out=Eb[:].rearrange("p g q -> p (g q)"), in_=ps[:],
                                                                  func=AF.Exp, scale=SCALE), [tps], [tE])
                rhsP, tP = stp["mask"](Eb, tE, i, aux)
                OT, tOT = stp["OT"]
                P.add("pe", lambda e, rhsP=rhsP, stp=stp, OT=OT: e.matmul(
                    OT[0:65, :], lhsT=stp["vE"], rhs=rhsP[:].rearrange("p g q -> p (g q)"),
                    start=stp["first"], stop=stp["last"]), [stp["tV"], tP], [tOT])
                if stp["last"]:
                    br = stp["br"]
                    P.add("act", lambda e, OT=OT, br=br: e.activation(out=OTs3[br][:], in_=OT[0:65, :], func=AF.Copy),
                          [tOT], [t_OTs3[br]])

            for k in range(n + LA):
                if k < n:
                    front(k)
                if k >= LA:
                    back(k - LA)

        for m in range(TCT):
            cb, tcb = cand[m % 2], t_cand[m % 2]
            fb, tfb = forc[m % 2], t_forc[m % 2]
            P.dma("sp", cb[:], a["cand"][m], [], tcb)
            P.dma("act", fb[:], a["forced"][m], [], tfb)
            for h in range(2):
                rows = slice(h * 64, (h + 1) * 64)
                qv = QT[rows, :, m * 128:(m + 1) * 128]
                P.add("dve", lambda e: e.memset(st[:], 0.0), [], [t_st])
                for g in range(4):
                    for nb in range(NB):
                        ps, tps = cx.psb[2 + nb % 2]
                        P.add("pe", lambda e, ps=ps, g=g, nb=nb, rows=rows, m=m: e.matmul(
                            ps[:, 0:NBW], lhsT=QT[rows, g, m * 128:(m + 1) * 128], rhs=KcmpT[rows, nb * NBW:(nb + 1) * NBW],
                            start=True, stop=True), [t_QT, t_Kcmp], [tps])
                        P.add("act", lambda e, ps=ps, nb=nb: e.activation(out=Etm[:, nb * NBW:(nb + 1) * NBW],
                                                                          in_=ps[:, 0:NBW], func=AF.Exp, scale=SCALE),
                              [tps], [t_Etm])
                    P.add("dve", lambda e, g=g, m=m: e.scalar_tensor_tensor(
                        out=Pm_tm[:], in0=iota16[:], scalar=thrq[:, m:m + 1], in1=Etm[:], op0=ALU.is_le, op1=ALU.mult,
                        accum_out=st[:, g:g + 1]), [t_cst, t_Etm, t_st], [t_Pmtm, t_st])
                    P.add("dve", lambda e, g=g: e.tensor_scalar(out=st[:, 8 + g:9 + g], in0=st[:, g:g + 1],
                                                                scalar1=1e-30, scalar2=None, op0=ALU.max),
                          [t_st], [t_st])
                    P.add("dve", lambda e, g=g: e.reciprocal(out=st[:, 4 + g:5 + g], in_=st[:, 8 + g:9 + g]),
                          [t_st], [t_st])
                    pv4 = Pm_tm[:].rearrange("p (j r) -> p j r", r=4)
                    P.add("dve", lambda e, pv4=pv4: e.tensor_reduce(out=t4[:], in_=pv4, axis=AX.X, op=ALU.add),
                          [t_Pmtm], [t_t4])
                    P.add("dve", lambda e, pv4=pv4: e.tensor_tensor(out=t4[:, 1:NSEL], in0=t4[:, 1:NSEL],
                                                                    in1=pv4[:, 0:NSEL - 1, 3], op=ALU.add),
                          [t_Pmtm, t_t4], [t_t4])
                    if g == 0:
                        P.add("dve", lambda e: e.tensor_scalar(out=imp[:], in0=t4[:], scalar1=st[:, 4:5], scalar2=None,
                                                               op0=ALU.mult), [t_t4, t_st], [t_imp])
                    else:
                        P.add("dve", lambda e, g=g: e.scalar_tensor_tensor(out=imp[:], in0=t4[:], scalar=st[:, 4 + g:5 + g],
                                                                          in1=imp[:], op0=ALU.mult, op1=ALU.add),
                              [t_t4, t_st, t_imp], [t_imp])
                P.add("dve", lambda e, cb=cb: e.tensor_tensor(out=score[:], in0=imp[:], in1=cb[:], op=ALU.mult),
                      [t_imp, tcb], [t_sc])
                P.add("dve", lambda e, fb=fb: e.tensor_tensor(out=score[:], in0=score[:], in1=fb[:], op=ALU.add),
                      [t_sc, tfb], [t_sc])
                P.add("dve", lambda e: e.max(out=m8[:, 0:8], in_=score[:]), [t_sc], [t_m8])
                P.add("dve", lambda e: e.match_replace(out=sc2[:], in_to_replace=m8[:, 0:8], in_values=score[:],
                                                       imm_value=-1e30), [t_sc, t_m8], [t_sc])
                P.add("dve", lambda e: e.max(out=m8[:, 8:16], in_=sc2[:]), [t_sc], [t_m8])
                P.add("dve", lambda e: e.tensor_scalar(out=m8[:, 0:1], in0=m8[:, 15:16], scalar1=1e-30, scalar2=None,
                                                       op0=ALU.max), [t_m8], [t_m8])
                P.add("dve", lambda e: e.tensor_scalar(out=sel[:], in0=score[:], scalar1=m8[:, 0:1], scalar2=None,
                                                       op0=ALU.is_ge), [t_sc, t_m8], [t_sel])
                for jh in range(NJH):
                    pt, tpt = cx.psb[3]
                    P.add("pe", lambda e, jh=jh, pt=pt: e.matmul(pt[0:JH, 0:128], lhsT=sel[:, jh * 128:jh * 128 + JH],
                                                                 rhs=cx.ident[:], start=True, stop=True),
                          [t_sel, cx.t_ident], [tpt])
                    P.add("act", lambda e, jh=jh, pt=pt: e.activation(out=selT[0:JH, jh, :], in_=pt[0:JH, 0:128],
                                                                      func=AF.Copy), [tpt], [t_selT])
                OTb = {0: cx.psb[5], 1: cx.psb[4], 2: cx.psb[4]}
                steps = []
                for kc in range(NCH):
                    def pre(i, kc=kc):
                        pmk, tpmk = cx.psb[MBK[i % 3]][0][:, 0:128], cx.psb[MBK[i % 3]][1]
                        jh = (2 * kc) // 128
                        off = 128 * (kc % 64)
                        P.add("pe", lambda e, pmk=pmk: e.matmul(pmk, lhsT=Ebig[0:JH, off:off + 128],
                                                                rhs=selT[0:JH, jh, :], start=True, stop=True),
                              [t_cst, t_selT], [tpmk])
                        return pmk, tpmk

                    def mf(Eb, tE, i, aux, kc=kc, m=m):
                        pmk, tpmk = aux
                        Pb, tPb = Pm[i % 3], t_Pm[i % 3]
                        P.add("dve", lambda e, pmk=pmk, Pb=Pb, Eb=Eb: e.tensor_tensor(
                            out=Pb[:], in0=Eb[:], in1=pmk.unsqueeze(1).broadcast_to([128, 4, 128]),
                            op=ALU.mult), [tE, tpmk], [tPb])
                        if kc % TCT == m:
                            ci = o_d + m * 4 + kc // TCT
                            P.add("dve", lambda e, Pb=Pb, ci=ci: e.scalar_tensor_tensor(
                                out=Pb[:], in0=Akq[:].unsqueeze(1).broadcast_to([128, 4, 128]), scalar=thr[:, ci:ci + 1],
                                in1=Pb[:], op0=ALU.is_le, op1=ALU.mult), [tPb, t_cst], [tPb])
                        return Pb, tPb
                    steps.append(dict(lhsK=KsT[rows, kc * 128:(kc + 1) * 128], tK=t_KsT, vE=VsE[:, kc, h, :], tV=t_VsE,
                                      pre=pre, mask=mf, OT=OTb[1], br=1, first=(kc == 0), last=(kc == NCH - 1)))
                for c in range(NCC):
                    def mf(Eb, tE, i, aux, c=c, m=m):
                        Pb, tPb = Pm[i % 3], t_Pm[i % 3]
                        ci = o_c + m * NCC + c
                        P.add("dve", lambda e, Pb=Pb, Eb=Eb, ci=ci: e.scalar_tensor_tensor(
                            out=Pb[:], in0=Ac[:].unsqueeze(1).broadcast_to([128, 4, 128]), scalar=thr[:, ci:ci + 1],
                            in1=Eb[:], op0=ALU.is_le, op1=ALU.mult), [tE, t_cst], [tPb])
                        return Pb, tPb
                    steps.append(dict(lhsK=KcmpT[rows, c * 128:(c + 1) * 128], tK=t_Kcmp, vE=VcE[:, c, h, :], tV=t_VcE,
                                      mask=mf, OT=OTb[0], br=0, first=(c == 0), last=(c == NCC - 1)))
                for c in range(5):
                    def mf(Eb, tE, i, aux, c=c, m=m):
                        if c in (0, 4) or m + c < 4:
                            Pb, tPb = Pm[i % 3], t_Pm[i % 3]
                            ci = o_w + m * 5 + c
                            op = ALU.is_gt if c == 0 else ALU.is_le
                            P.add("dve", lambda e, Pb=Pb, Eb=Eb, ci=ci, op=op: e.scalar_tensor_tensor(
                                out=Pb[:], in0=Akq[:].unsqueeze(1).broadcast_to([128, 4, 128]), scalar=thr[:, ci:ci + 1],
                                in1=Eb[:], op0=op, op1=ALU.mult), [tE, t_cst], [tPb])
                            return Pb, tPb
                        return Eb, tE
                    steps.append(dict(lhsK=KwT[rows, (m + c) * 128:(m + c + 1) * 128], tK=t_KwT,
                                      vE=VwE[:, m + c, h, :], tV=t_VwE, mask=mf, OT=OTb[2], br=2, first=(c == 0), last=(c == 4)))
                run_steps(steps, qv)
                for br in range(3):
                    pt, tpt = cx.psb[7]
                    for g in range(4):
                        P.add("pe", lambda e, g=g, pt=pt, br=br: e.matmul(pt[:, g * 65:(g + 1) * 65],
                                                                   lhsT=OTs3[br][0:65, g * 128:(g + 1) * 128],
                                                                   rhs=identf[0:65, 0:65], start=True, stop=True),
                              [t_OTs3[br], t_cst], [tpt])
                    ptv = pt[:, 0:260].rearrange("p (g e) -> p g e", e=65)
                    P.add("dve", lambda e, ptv=ptv: e.tensor_scalar(out=w4[:, 0:4], in0=ptv[:, :, 64], scalar1=1e-30,
                                                                    scalar2=None, op0=ALU.max), [tpt], [t_w4])
                    P.add("dve", lambda e: e.reciprocal(out=w4[:, 4:8], in_=w4[:, 0:4]), [t_w4], [t_w4])
                    gv = gate[:, m, h * 12:(h + 1) * 12].rearrange("p (g b) -> p g b", b=3)[:, :, br]
                    P.add("dve", lambda e, gv=gv: e.tensor_tensor(out=w4[:, 0:4], in0=w4[:, 4:8], in1=gv, op=ALU.mult),
                          [t_w4, t_cst], [t_w4])
                    oav = oacc[:, h * 256:(h + 1) * 256].rearrange("p (g d) -> p g d", d=64)
                    wb = w4[:, 0:4].unsqueeze(2).broadcast_to([128, 4, 64])
                    if br == 0:
                        P.add("dve", lambda e, ptv=ptv, oav=oav, wb=wb: e.tensor_tensor(
                            out=oav, in0=ptv[:, :, 0:64], in1=wb, op=ALU.mult), [tpt, t_w4], [t_oacc])
                    else:
                        P.add("dve", lambda e, ptv=ptv, wb=wb: e.tensor_tensor(
                            out=tmpo[:], in0=ptv[:, :, 0:64], in1=wb, op=ALU.mult), [tpt, t_w4], [t_tmpo])
                        P.add("dve", lambda e, oav=oav: e.tensor_tensor(out=oav, in0=oav, in1=tmpo[:], op=ALU.add),
                              [t_tmpo, t_oacc], [t_oacc])
            P.add("act", lambda e: e.activation(out=ob[:], in_=oacc[:], func=AF.Copy), [t_oacc], [t_ob])
            for c4 in range(4):
                pt, tpt = cx.psb[7]
                P.add("pe", lambda e, c4=c4, pt=pt: e.matmul(pt[:, 0:128], lhsT=ob[:, c4 * 128:(c4 + 1) * 128],
                                                             rhs=cx.ident[:], start=True, stop=True),
                      [t_ob, cx.t_ident], [tpt])
                otb, totb = oT[c4 % 2], t_oT[c4 % 2]
                P.add("act", lambda e, pt=pt, otb=otb: e.activation(out=otb[:], in_=pt[:, 0:128], func=AF.Copy),
                      [tpt], [totb])
                P.dma("sp", yT_d[:, 4 + c4, m * 128:(m + 1) * 128], otb[:], [totb], t_yT)
        P.barrier()


def emit_wout(cx, yT_d, t_yT, w_out, gpost, h_in, t_hin, h_out, TC):
    P, nc = cx.P, cx.nc
    t_hout = P.tok("hmid")
    with ExitStack() as es:
        sb = lambda n, s, d: cx.sb(es, "wo" + n, s, d)
        tl = Tail(cx, es, gpost, 1.0, "wo")
        wo = sb("w", [128, 8, D], BF16); t_wo = P.tok("wo")
        stg = [sb("stg", [128, D], F32) for _ in range(2)]; t_stg = P.toks("stg", 2)
        w_v = w_out.rearrange("(k p) d -> p k d", p=128)
        for k in range(8):
            load_cast(cx, wo[:, k, :], t_wo, w_v[:, k, :], stg[k % 2][:], t_stg[k % 2],
                      "sp" if k % 2 == 0 else "act", "act" if k % 2 == 0 else "dve")
        xt = [sb("xt", [128, 4, D], F32) for _ in range(2)]; t_xt = P.toks("xt", 2)
        yt = [sb("yt", [128, 8, ST], BF16) for _ in range(2)]; t_yt = P.toks("yt", 2)
        h_in_v = h_in.rearrange("(s j p) d -> s p j d", p=128, j=4)
        h_out_v = h_out.rearrange("(s j p) d -> s j p d", p=128, j=4)
        for s in range(TC // ST):
            xb, txb = xt[s % 2], t_xt[s % 2]
            yb, tyb = yt[s % 2], t_yt[s % 2]
            P.dma("sp", xb[:], h_in_v[s], [t_hin], txb)
            P.dma("act", yb[:], yT_d[:, :, s * ST:(s + 1) * ST], [t_yT], tyb)
            for j in range(4):
                tl.run(lambda k, j=j, yb=yb: yb[:, k, j * 128:(j + 1) * 128], 8,
                       lambda k, hf: wo[:, k, hf * 512:(hf + 1) * 512], [tyb], t_wo,
                       xb[:, j, :], txb, h_out_v[s, j], t_hout)
        P.barrier()
    return t_hout


import numpy as np
import ml_dtypes
BF = ml_dtypes.bfloat16
BIGV = 1.0e9

def consts(S):
    NCM = S // 16
    c = {}
    x = np.arange(8192)
    c["Ebig"] = (np.arange(128)[:, None] == (x // 64)[None, :]).astype(np.float32).astype(BF)
    k = np.arange(128, dtype=np.float32)
    c["Akq"] = (k[:, None] - k[None, :]).astype(np.float32)
    c["Ac"] = (16 * k[:, None] + 31 - k[None, :]).astype(np.float32)
    c["identf"] = np.eye(128, dtype=np.float32)
    c["iota16"] = np.tile((16.0 * np.arange(NCM, dtype=np.float32))[None, :], (128, 1))
    c["ident"] = np.eye(128, dtype=np.float32).astype(BF)
    inv = (10000.0 ** (-np.arange(0, 64, 2, dtype=np.float32) / 64)).astype(np.float32)
    c["invtab"] = np.tile(inv[None, :], (128, 1)).astype(np.float32)
    return c

def tables(qd, S, TC):
    TCT, NSEL, NCM = TC // 128, S // 64, S // 16
    NCC = max(NCM // 128, 1)
    t = {}
    q = np.arange(128)
    cand = np.zeros((TCT, 128, NSEL), np.float32)
    forced = np.zeros((TCT, 128, NSEL), np.float32)
    j = np.arange(NSEL)
    thr_c = np.zeros((TCT, NCC), np.float32); thr_d = np.zeros((TCT, 4), np.float32); thr_w = np.zeros((TCT, 5), np.float32)
    thrq = np.zeros((128, TCT), np.float32)
    for m in range(TCT):
        i = TCT * qd + m
        cur = (128 * i + q) // 64
        valid = j[None, :] <= cur[:, None]
        f = (j[None, :] == 0) | (j[None, :] == cur[:, None]) | (j[None, :] == cur[:, None] - 1)
        f = f & valid
        cand[m] = (valid & ~f).astype(np.float32)
        forced[m] = f.astype(np.float32) * BIGV
        for c in range(NCC):
            thr_c[m, c] = 128.0 * (i - 16 * c)
        for r in range(4):
            thr_d[m, r] = 0.0 if r == qd else BIGV
        for c in range(5):
            ok = (i - 4 + c) >= 0
            if c == 0:
                thr_w[m, c] = 0.0 if ok else BIGV
            elif c == 4:
                thr_w[m, c] = 0.0
            else:
                thr_w[m, c] = BIGV if ok else -BIGV
        thrq[:, m] = 128.0 * i + q - 31.0
    row = np.concatenate([thr_c.ravel(), thr_d.ravel(), thr_w.ravel()]).astype(np.float32)
    t["thr"] = np.tile(row[None, :], (128, 1))
    t["thrq"] = thrq
    t["cand"] = cand
    t["forced"] = forced
    fix = np.ones((128, 2, 16), np.float32)
    if qd == 0:
        for cc in range(2):
            for hh in range(2):
                w = (2, 4, 8, 16)[2 * cc + hh]
                fix[hh * 64:(hh + 1) * 64, cc, :] = w / np.minimum(np.arange(16) + 1, w)
    t["poolfix"] = fix
    NLC = S // 512
    flag = np.zeros((128, NLC), np.float32)
    flag[:, NLC - ((qd + 1) * TC) // 512:] = 1.0
    t["lruflag"] = flag
    return t

def chT(a):
    return np.ascontiguousarray(a.T.reshape(2, 128, -1).transpose(1, 0, 2))

def layout_B(qd, S, TC, f32p, qkv, gates):
    t0 = qd * TC
    TCT, NCH = TC // 128, S // 128
    d = {}
    xp, xl, gl = f32p[:, 0:256], f32p[:, 256:512], f32p[:, 512:768]
    d["xpT"] = chT(np.concatenate([np.zeros((16, 256), np.float32), xp])[t0:t0 + 16 + TC])
    d["xlT"] = chT(np.concatenate([np.zeros((4 + S, 256), np.float32), xl])[t0 + TC:t0 + TC + 4 + S])
    d["glT"] = chT(gl[t0:t0 + TC])
    q = qkv[:, 0:512]
    kc, vc, ks, vs, kw, vw = (qkv[:, 512 + 128 * i:640 + 128 * i] for i in range(6))
    d["QT"] = np.ascontiguousarray(q[t0:t0 + TC].reshape(TC, 2, 4, 64).transpose(1, 3, 2, 0).reshape(128, 4, TC))
    d["KsT"] = np.ascontiguousarray(ks.T)
    d["Vs"] = np.ascontiguousarray(vs.reshape(NCH, 128, 128).transpose(1, 0, 2))
    d["KcT"] = np.ascontiguousarray(kc.T)
    d["VcT"] = np.ascontiguousarray(vc.T)
    z = np.zeros((512, 128), kw.dtype)
    d["KwT"] = np.ascontiguousarray(np.concatenate([z, kw])[t0:t0 + 512 + TC].T)
    d["Vw"] = np.ascontiguousarray(np.concatenate([z, vw])[t0:t0 + 512 + TC].reshape(4 + TCT, 128, 128).transpose(1, 0, 2))
    d["gate"] = np.ascontiguousarray(gates[t0:t0 + TC].reshape(TCT, 128, 24).transpose(1, 0, 2))
    return d


from concourse.bass_utils import run_bass_kernel_spmd

_MK_S = 16384
_PROGS = {}


def _build(kind, S, TC):
    TCT, NCH, NSEL, NCM = TC // 128, S // 128, S // 64, S // 16
    NCC = max(NCM // 128, 1)
    NTHR = TCT * NCC + TCT * 4 + TCT * 5
    nc = bass.Bass("TRN2", target_bir_lowering=False)
    dt = lambda n, s, d, k="ExternalInput": nc.dram_tensor(n, s, d, kind=k).ap()
    P = Prog(nc)
    fin = []
    with ExitStack() as es:
        cx = Ctx(nc, P, es)
        ident = dt("ident", [128, 128], BF16)
        load_consts(cx, es, ident)
        has_B = kind in ("BA", "B")
        has_A = kind in ("A", "BA")
        t_h = P.tok("h0")
        if has_B:
            a = dict(QT=dt("QT", [128, 4, TC], BF16), KsT=dt("KsT", [128, S], BF16), Vs=dt("Vs", [128, NCH, 128], BF16),
                     KcT=dt("KcT", [128, S], BF16), VcT=dt("VcT", [128, S], BF16), KwT=dt("KwT", [128, 512 + TC], BF16),
                     Vw=dt("Vw", [128, 4 + TCT, 128], BF16), gate=dt("gate", [128, TCT, 24], F32),
                     Ebig=dt("Ebig", [128, 8192], BF16), Akq=dt("Akq", [128, 128], F32), Ac=dt("Ac", [128, 128], F32),
                     identf=dt("identf", [128, 128], F32), iota16=dt("iota16", [128, NCM], F32),
                     thr=dt("thr", [128, NTHR], F32), thrq=dt("thrq", [128, TCT], F32),
                     cand=dt("cand", [TCT, 128, NSEL], F32), forced=dt("forced", [TCT, 128, NSEL], F32),
                     cmp_w_k=dt("cmp_w_k", [32, 64, 64], F32), cmp_w_v=dt("cmp_w_v", [32, 64, 64], F32),
                     cmp_pe=dt("cmp_pe", [32, 64], F32))
            xpT = dt("xpT", [128, 2, 16 + TC], F32); poolfix = dt("poolfix", [128, 2, 16], F32)
            pool_w = dt("pool_w", [4, 64, 64], F32); pool_scale = dt("pool_scale", [256], F32)
            xlT = dt("xlT", [128, 2, 4 + S], F32); glT = dt("glT", [128, 2, TC], F32)
            lruflag = dt("lruflag", [128, S // 512], F32)
            conv_w = dt("conv_w", [4, 256], F32); conv_b = dt("conv_b", [256], F32)
            w_r = dt("lru_w_r", [4, 64, 64], F32); b_r = dt("lru_b_r", [256], F32)
            w_i = dt("lru_w_i", [4, 64, 64], F32); b_i = dt("lru_b_i", [256], F32); lam = dt("lru_lambda", [256], F32)
            w_out = dt("w_out", [D, D], F32); mix_post_g = dt("mix_post_g", [D], F32)
            h1 = dt("h1_in", [TC, D], F32)
            f2 = dict(wg=dt("f2_wg", [D, DFF], F32), wu=dt("f2_wu", [D, DFF], F32), wd=dt("f2_wd", [DFF, D], F32),
                      gpre=dt("f2_gpre", [D], F32), gpost=dt("f2_gpost", [D], F32))
            yT = dt("yT_s", [128, 8, TC], BF16, "Internal")
            h_mid = dt("h_mid_s", [TC, D], F32, "Internal")
            t_yT = P.tok("yT")
            emit_pool(cx, xpT, poolfix, pool_w, pool_scale, yT, t_yT, TC)
            emit_lru(cx, xlT, glT, lruflag, conv_w, conv_b, w_r, b_r, w_i, b_i, lam, yT, t_yT, S, TC)
            emit_attn(cx, a, yT, t_yT, S, TC)
            t_hm = emit_wout(cx, yT, t_yT, w_out, mix_post_g, h1, P.tok("h1in"), h_mid, TC)
            if has_A:
                h2 = dt("h2_s", [TC, D], F32, "Internal")
            else:
                h2 = dt("h_out", [TC, D], F32, "ExternalOutput")
            t_h = emit_ffn(cx, h_mid, t_hm, h2, f2["wg"], f2["wu"], f2["wd"], f2["gpre"], f2["gpost"], TC, "f2")
            h_cur = h2
            if not has_A:
                fin.append(t_h)
        else:
            h_cur = dt("h_in", [TC, D], F32)
        if has_A:
            f1 = dict(wg=dt("f1_wg", [D, DFF], F32), wu=dt("f1_wu", [D, DFF], F32), wd=dt("f1_wd", [DFF, D], F32),
                      gpre=dt("f1_gpre", [D], F32), gpost=dt("f1_gpost", [D], F32))
            w_in = dt("w_in", [D, IN_COLS], F32); mix_pre_g = dt("mix_pre_g", [D], F32)
            pos = dt("pos", [TC], I32); invtab = dt("invtab", [128, 32], F32)
            h1o = dt("h1_out", [TC, D], F32, "ExternalOutput")
            o_f32 = dt("o_f32", [TC, 768], F32, "ExternalOutput")
            o_qkv = dt("o_qkv", [TC, 1280], BF16, "ExternalOutput")
            o_gate = dt("o_gate", [TC, 24], F32, "ExternalOutput")
            t_h1 = emit_ffn(cx, h_cur, t_h, h1o, f1["wg"], f1["wu"], f1["wd"], f1["gpre"], f1["gpost"], TC, "f1")
            t_o = emit_proj(cx, h1o, t_h1, w_in, mix_pre_g, pos, invtab, o_f32, o_qkv, o_gate, TC, "pj")
            fin += [t_h1] + list(t_o)
        P.emit(fin)
    return nc


def _get_prog(kind, S, TC):
    key = (kind, S, TC)
    if key not in _PROGS:
        _PROGS[key] = _build(kind, S, TC)
    return _PROGS[key]


_RUN = [None]


def _run(nc, in_maps):
    if _RUN[0] is not None:
        return _RUN[0](nc, in_maps)
    return run_bass_kernel_spmd(nc, in_maps, core_ids=list(range(len(in_maps)))).results


def kernel(**inp):
    x = np.asarray(inp["x"])
    NBt, S, _ = x.shape
    NQ = 4
    TC = S // NQ
    ncores = NBt * NQ
    cs = consts(S)
    g = lambda k, l: np.ascontiguousarray(np.asarray(inp[k])[l])
    positions = np.asarray(inp["positions"]).astype(np.int32)

    def a_inputs(l):
        return dict(f1_wg=g("ffn1_w_gate", l), f1_wu=g("ffn1_w_up", l), f1_wd=g("ffn1_w_down", l),
                    f1_gpre=g("ffn1_pre_g", l), f1_gpost=g("ffn1_post_g", l), w_in=g("w_in", l),
                    mix_pre_g=g("mix_pre_g", l), invtab=cs["invtab"])

    def b_inputs(l):
        d = dict(f2_wg=g("ffn2_w_gate", l), f2_wu=g("ffn2_w_up", l), f2_wd=g("ffn2_w_down", l),
                 f2_gpre=g("ffn2_pre_g", l), f2_gpost=g("ffn2_post_g", l), w_out=g("w_out", l),
                 mix_post_g=g("mix_post_g", l))
        for k in ("pool_w", "pool_scale", "conv_w", "conv_b", "lru_w_r", "lru_b_r", "lru_w_i", "lru_b_i",
                  "lru_lambda", "cmp_w_k", "cmp_w_v", "cmp_pe"):
            d[k] = g(k, l)
        for k in ("Ebig", "Akq", "Ac", "identf", "iota16"):
            d[k] = cs[k]
        return d

    tabs = [tables(qd, S, TC) for qd in range(NQ)]

    def gather(res, key, width, dtype):
        out = np.zeros((NBt, S, width), dtype)
        for c in range(ncores):
            b, qd = divmod(c, NQ)
            out[b, qd * TC:(qd + 1) * TC] = np.asarray(res[c][key]).reshape(TC, width)
        return out

    base = a_inputs(0)
    in_maps = []
    for c in range(ncores):
        b, qd = divmod(c, NQ)
        m = dict(base)
        m["ident"] = cs["ident"]
        m["h_in"] = np.ascontiguousarray(x[b, qd * TC:(qd + 1) * TC])
        m["pos"] = np.ascontiguousarray(positions[b, qd * TC:(qd + 1) * TC])
        in_maps.append(m)
    res = _run(_get_prog("A", S, TC), in_maps)
    DEPTH = np.asarray(inp["w_in"]).shape[0]
    for l in range(DEPTH):
        f32p = gather(res, "o_f32", 768, np.float32)
        qkv = gather(res, "o_qkv", 1280, BF)
        gates = gather(res, "o_gate", 24, np.float32)
        last = (l == DEPTH - 1)
        base = b_inputs(l)
        if not last:
            base.update(a_inputs(l + 1))
        in_maps = []
        for c in range(ncores):
            b, qd = divmod(c, NQ)
            m = dict(base)
            m["ident"] = cs["ident"]
            m.update(layout_B(qd, S, TC, f32p[b], qkv[b], gates[b]))
            for k in ("thr", "thrq", "cand", "forced", "poolfix", "lruflag"):
                m[k] = tabs[qd][k]
            m["h1_in"] = np.asarray(res[c]["h1_out"]).reshape(TC, D)
            if not last:
                m["pos"] = np.ascontiguousarray(positions[b, qd * TC:(qd + 1) * TC])
            in_maps.append(m)
        res = _run(_get_prog("B" if last else "BA", S, TC), in_maps)
    out = gather(res, "h_out", D, np.float32)
    return out
```

```python
import numpy as np
import concourse.bass as bass
import concourse.mybir as mybir

F32 = mybir.dt.float32
BF16 = mybir.dt.bfloat16
I32 = mybir.dt.int32
ALU = mybir.AluOpType
AF = mybir.ActivationFunctionType
AX = mybir.AxisListType


class Tok:
    __slots__ = ("name", "last_w", "readers", "sem", "dma_cnt", "fslot")

    def __init__(self, name):
        self.name = name
        self.last_w = None
        self.readers = []
        self.sem = None
        self.dma_cnt = 0


class Op:
    __slots__ = ("eng", "fn", "deps", "signal", "idx", "is_dma", "dtok", "dval", "dslot")

    def __init__(self, eng, fn, is_dma=False):
        self.eng = eng
        self.fn = fn
        self.deps = []
        self.signal = False
        self.idx = None
        self.is_dma = is_dma
        self.dtok = None
        self.dval = None


ENGS = ("pe", "act", "dve", "pool", "sp")


class Prog:
    def __init__(self, nc):
        self.nc = nc
        self.ops = []
        self.nslots = 0
        self.slot_cnt = {}
        self.free_slots = []
        self.active = []

    def tok(self, name):
        return Tok(name)

    def toks(self, name, n):
        return [Tok(f"{name}{i}") for i in range(n)]

    def _track(self, op, reads, writes):
        deps = []
        for t in reads:
            if t.last_w is not None:
                deps.append(t.last_w)
        for t in writes:
            if t.last_w is not None:
                deps.append(t.last_w)
            deps.extend(t.readers)
        seen = set()
        for d in deps:
            if d is op or id(d) in seen:
                continue
            seen.add(id(d))
            if d.eng == "pe" and op.eng == "pe" and not d.is_dma and not op.is_dma:
                continue
            op.deps.append(d)
            if not d.is_dma:
                d.signal = True
        for t in writes:
            t.last_w = op
            t.readers = []
        for t in reads:
            if t not in writes:
                t.readers.append(op)

    def add(self, eng, fn, reads=(), writes=()):
        op = Op(eng, fn)
        self._track(op, list(reads), list(writes))
        self.ops.append(op)
        return op

    def dma(self, eng, out, in_, reads, write, **kw):
        op = Op(eng, lambda e: e.dma_start(out=out, in_=in_, **kw), is_dma=True)
        self._track(op, list(reads), [write])
        if write.sem is None:
            if self.free_slots:
                write.sem = self.free_slots.pop()
            else:
                write.sem = self.nslots
                self.nslots += 1
                self.slot_cnt[write.sem] = 0
            self.active.append(write)
        self.slot_cnt[write.sem] += 16
        op.dtok = write
        op.dslot = write.sem
        op.dval = self.slot_cnt[write.sem]
        write.dma_cnt = op.dval
        self.ops.append(op)
        return op

    def barrier(self):
        last = {}
        for op in self.ops:
            if op.fn is None:
                continue
            if op.is_dma:
                last[("d", op.dslot)] = op
            else:
                last[("e", op.eng)] = op
        deps = list(last.values())
        for t in self.active:
            self.free_slots.append(t.sem)
            t.fslot = t.sem
            t.sem = None
        self.active = []
        for e in ENGS:
            if e == 'pool':
                continue
            b = Op(e, None)
            for d in deps:
                b.deps.append(d)
                if not d.is_dma:
                    d.signal = True
            self.ops.append(b)

    def emit(self, final_toks=()):
        nc = self.nc
        from contextlib import ExitStack
        with ExitStack() as es:
            esem = {e: es.enter_context(nc.semaphore(f"s_{e}")) for e in ENGS}
            dsem = [es.enter_context(nc.semaphore(f"d_{i}")) for i in range(self.nslots)]
            cnt = {e: 0 for e in ENGS}
            waited = {e: {} for e in ENGS}
            streams = {e: [] for e in ENGS}
            nwaits = 0
            for op in self.ops:
                waits = []
                for d in op.deps:
                    if d.is_dma:
                        sem, val = dsem[d.dslot], d.dval
                    else:
                        sem, val = esem[d.eng], d.idx
                    key = id(sem)
                    if waited[op.eng].get(key, 0) >= val:
                        continue
                    waited[op.eng][key] = val
                    waits.append((sem, val))
                nwaits += len(waits)
                inc = None
                if op.is_dma:
                    inc = (dsem[op.dslot], 16)
                elif op.signal:
                    cnt[op.eng] += 1
                    op.idx = cnt[op.eng]
                    inc = (esem[op.eng], 1)
                streams[op.eng].append((waits, op.fn, inc))
            fin = []
            for t in final_toks:
                fin.append((dsem[t.sem if t.sem is not None else t.fslot], t.dma_cnt))
            self.stats = dict(n_ops=len(self.ops), n_waits=nwaits,
                              per_eng={e: len(streams[e]) for e in ENGS})

            def run(eng_obj, items, final=None):
                for waits, fn, inc in items:
                    for sem, val in waits:
                        eng_obj.wait_ge(sem, val)
                    if fn is None:
                        continue
                    ins = fn(eng_obj)
                    if inc is not None:
                        ins.then_inc(inc[0], inc[1])
                if final:
                    for sem, val in final:
                        eng_obj.wait_ge(sem, val)

            with nc.Block() as block:
                @block.tensor
                def _(e):
                    run(e, streams["pe"])

                @block.scalar
                def _(e):
                    run(e, streams["act"])

                @block.vector
                def _(e):
                    run(e, streams["dve"])

                @block.gpsimd
                def _(e):
                    run(e, streams["pool"])

                @block.sync
                def _(e):
                    run(e, streams["sp"], fin)


import math
import numpy as np
from contextlib import ExitStack

D = 1024
DFF = 2816
NF = DFF // 128
ST = 512
EPS = 1e-6
IN_COLS = 2072
PI = math.pi


class Ctx:
    def __init__(self, nc, P, es):
        self.nc, self.P, self.es = nc, P, es
        self.psb = []
        for i in range(8):
            t = es.enter_context(nc.psum_tensor(f"ps{i}", [128, 512], F32))
            self.psb.append((t, P.tok(f"ps{i}")))
        self.n = 0

    def sb(self, es, name, shape, dt):
        self.n += 1
        return es.enter_context(self.nc.sbuf_tensor(f"{name}_{self.n}", shape, dt))


def load_consts(cx, es, ident_d):
    P = cx.P
    cx.ident = cx.sb(es, "ident", [128, 128], BF16)
    cx.t_ident = P.tok("ident")
    P.dma("sp", cx.ident[:], ident_d, [], cx.t_ident)


def load_cast(cx, dst, t_dst, src, stg, t_stg, q, eng):
    P = cx.P
    P.dma(q, stg, src, [], t_stg)
    if eng == "act":
        P.add("act", lambda e: e.activation(out=dst, in_=stg, func=AF.Copy), [t_stg], [t_dst])
    else:
        P.add("dve", lambda e: e.tensor_copy(out=dst, in_=stg), [t_stg], [t_dst])


class NormT:
    def __init__(self, cx, es, g_dram, tag):
        P = cx.P
        self.cx = cx
        self.gcol = cx.sb(es, tag + "gcol", [128, 8], F32)
        self.t_gcol = P.tok("gcol")
        P.dma("sp", self.gcol[:], g_dram.rearrange("(k p) -> p k", p=128), [], self.t_gcol,
              allow_slow_non_contiguous=True)
        self.xs = [cx.sb(es, tag + "xs", [128, D], BF16) for _ in range(4)]
        self.t_xs = P.toks("xs", 4)
        self.junk = cx.sb(es, tag + "junk", [128, D], BF16)
        self.ss = cx.sb(es, tag + "ss", [128, 8], F32)
        self.rstd = cx.sb(es, tag + "rstd", [128, 8], F32)
        self.xnT = cx.sb(es, tag + "xnT", [128, 8, ST], BF16)
        self.t_junk, self.t_ss, self.t_rstd, self.t_xnT = P.tok("junk"), P.tok("ss"), P.tok("rstd"), P.tok("xnT")

    def run(self, xb, txb):
        cx, P = self.cx, self.cx.P
        ss, rstd, junk, xs, xnT = self.ss, self.rstd, self.junk, self.xs, self.xnT
        P.add("dve", lambda e: e.memset(ss[:], 0.0), [], [self.t_ss])
        for j in range(4):
            P.add("act", lambda e, j=j: e.activation(out=junk[:], in_=xb[:, j, :], func=AF.Square,
                                                     accum_out=ss[:, j:j + 1]), [txb], [self.t_junk, self.t_ss])
        P.add("dve", lambda e: e.tensor_scalar(out=rstd[:, 0:4], in0=ss[:, 0:4], scalar1=1.0 / D, scalar2=EPS,
                                               op0=ALU.mult, op1=ALU.add), [self.t_ss], [self.t_rstd])
        P.add("act", lambda e: e.activation(out=rstd[:, 4:8], in_=rstd[:, 0:4], func=AF.Sqrt),
              [self.t_rstd], [self.t_rstd])
        P.add("dve", lambda e: e.reciprocal(out=rstd[:, 0:4], in_=rstd[:, 4:8]), [self.t_rstd], [self.t_rstd])
        for j in range(4):
            P.add("dve", lambda e, j=j: e.tensor_scalar(out=xs[j][:], in0=xb[:, j, :], scalar1=rstd[:, j:j + 1],
                                                        scalar2=None, op0=ALU.mult),
                  [txb, self.t_rstd], [self.t_xs[j]])
        for k in range(8):
            pt, tpt = cx.psb[6 + (k % 2)]
            for j in range(4):
                P.add("pe", lambda e, j=j, k=k, pt=pt: e.matmul(pt[:, j * 128:(j + 1) * 128],
                                                               lhsT=xs[j][:, k * 128:(k + 1) * 128],
                                                               rhs=cx.ident[:], start=True, stop=True),
                      [self.t_xs[j], cx.t_ident], [tpt])
            P.add("act", lambda e, k=k, pt=pt: e.activation(out=xnT[:, k, :], in_=pt[:], func=AF.Copy,
                                                            scale=self.gcol[:, k:k + 1]),
                  [tpt, self.t_gcol], [self.t_xnT])
        return xnT, self.t_xnT


class Tail:
    def __init__(self, cx, es, g_dram, scale, tag):
        P = cx.P
        self.cx = cx
        self.gpb = cx.sb(es, tag + "gpb", [128, D], F32)
        self.t_gpb = P.tok("gpb")
        P.dma("sp", self.gpb[:], g_dram.partition_broadcast(128), [], self.t_gpb)
        if scale != 1.0:
            P.add("dve", lambda e: e.tensor_scalar(out=self.gpb[:], in0=self.gpb[:], scalar1=scale, scalar2=None,
                                                   op0=ALU.mult), [self.t_gpb], [self.t_gpb])
        self.y = [cx.sb(es, tag + "y", [128, D], F32) for _ in range(2)]
        self.t_y = P.toks("y", 2)
        self.ot = [cx.sb(es, tag + "ot", [128, D], F32) for _ in range(2)]
        self.t_ot = P.toks("ot", 2)
        self.junk = cx.sb(es, tag + "junk2", [128, D], BF16)
        self.ss2 = cx.sb(es, tag + "ss2", [128, 4], F32)
        self.rstd2 = cx.sb(es, tag + "rstd2", [128, 2], F32)
        self.t_junk, self.t_ss2, self.t_rstd2 = P.tok("junk2"), P.tok("ss2"), P.tok("rstd2")
        self.cnt = 0

    def run(self, lhs_fn, nk, w_fn, lhs_toks, t_w, resid, t_resid, out_ap, t_out):
        cx, P = self.cx, self.cx.P
        i = self.cnt
        self.cnt += 1
        yb, tyb = self.y[i % 2], self.t_y[i % 2]
        ob, tob = self.ot[i % 2], self.t_ot[i % 2]
        ss2, rstd2, junk, gpb = self.ss2, self.rstd2, self.junk, self.gpb
        for hf in range(2):
            py, tpy = cx.psb[4 + hf]
            for k in range(nk):
                P.add("pe", lambda e, k=k, hf=hf, py=py: e.matmul(py[:], lhsT=lhs_fn(k), rhs=w_fn(k, hf),
                                                                 start=(k == 0), stop=(k == nk - 1)),
                      list(lhs_toks) + [t_w], [tpy])
            P.add("dve", lambda e, py=py, hf=hf: e.tensor_copy(out=yb[:, hf * 512:(hf + 1) * 512], in_=py[:]),
                  [tpy], [tyb])
        P.add("dve", lambda e: e.memset(ss2[:], 0.0), [], [self.t_ss2])
        P.add("act", lambda e: e.activation(out=junk[:], in_=yb[:], func=AF.Square, accum_out=ss2[:, 0:1]),
              [tyb], [self.t_junk, self.t_ss2])
        P.add("dve", lambda e: e.tensor_scalar(out=ss2[:, 1:2], in0=ss2[:, 0:1], scalar1=1.0 / D, scalar2=EPS,
                                               op0=ALU.mult, op1=ALU.add), [self.t_ss2], [self.t_ss2])
        P.add("act", lambda e: e.activation(out=ss2[:, 2:3], in_=ss2[:, 1:2], func=AF.Sqrt),
              [self.t_ss2], [self.t_ss2])
        P.add("dve", lambda e: e.reciprocal(out=rstd2[:, 0:1], in_=ss2[:, 2:3]), [self.t_ss2], [self.t_rstd2])
        P.add("dve", lambda e: e.scalar_tensor_tensor(out=ob[:], in0=yb[:], scalar=rstd2[:, 0:1], in1=gpb[:],
                                                      op0=ALU.mult, op1=ALU.mult),
              [tyb, self.t_rstd2, self.t_gpb], [tob])
        P.add("dve", lambda e: e.tensor_tensor(out=ob[:], in0=ob[:], in1=resid, op=ALU.add),
              [tob, t_resid], [tob])
        P.dma("sp", out_ap, ob[:], [tob], t_out)


def emit_ffn(cx, h_in, t_hin, h_out, wg, wu, wd, gpre, gpost, NT, tag):
    P, nc = cx.P, cx.nc
    nst = NT // ST
    t_hout = P.tok("hout")
    with ExitStack() as es:
        sb = lambda name, shape, dt: cx.sb(es, tag + name, shape, dt)
        nt = NormT(cx, es, gpre, tag)
        tl = Tail(cx, es, gpost, 0.5, tag)
        wd_sb = sb("wd", [128, NF, D], BF16)
        t_wd = P.tok("wd")
        wgu = [sb("wgu", [128, 2, 8, 256], BF16) for _ in range(2)]
        t_wgu = P.toks("wgu", 2)
        stg = [[sb("stg", [128, 8, 256], F32) for _ in range(2)] for _ in range(2)]
        t_stg = [[P.tok("stg") for _ in range(2)] for _ in range(2)]
        xt = [sb("xt", [128, 4, D], F32) for _ in range(2)]
        t_xt = P.toks("xt", 2)
        act = sb("act", [128, NF, ST], BF16)
        t_act = P.toks("act", NF)
        sil = [sb("sil", [128, ST], F32) for _ in range(2)]
        t_sil = P.toks("sil", 2)
        wd_v = wd.rearrange("(f p) d -> p f d", p=128)
        for ii, f0 in enumerate(range(0, NF, 2)):
            sg, tsg = stg[ii % 2][(ii // 2) % 2], t_stg[ii % 2][(ii // 2) % 2]
            sgv = sg[:].rearrange("p k c -> p (k c)").rearrange("p (f d) -> p f d", f=2)
            load_cast(cx, wd_sb[:, f0:f0 + 2, :], t_wd, wd_v[:, f0:f0 + 2, :], sgv, tsg,
                      "sp" if ii % 2 == 0 else "act", "act" if ii % 2 == 0 else "dve")
        wg_v = wg.rearrange("(k p) c -> p k c", p=128)
        wu_v = wu.rearrange("(k p) c -> p k c", p=128)
        h_in_v = h_in.rearrange("(s j p) d -> s p j d", p=128, j=4)
        h_out_v = h_out.rearrange("(s j p) d -> s j p d", p=128, j=4)

        def load_x(s):
            P.dma("sp", xt[s % 2][:], h_in_v[s], [t_hin], t_xt[s % 2])

        def load_w(g, slot):
            load_cast(cx, wgu[slot][:, 0], t_wgu[slot], wg_v[:, :, g * 256:(g + 1) * 256], stg[slot][0][:],
                      t_stg[slot][0], "sp", "act")
            load_cast(cx, wgu[slot][:, 1], t_wgu[slot], wu_v[:, :, g * 256:(g + 1) * 256], stg[slot][1][:],
                      t_stg[slot][1], "act", "dve")

        gi = 0
        load_x(0)
        load_w(0, 0)
        for s in range(nst):
            xb, txb = xt[s % 2], t_xt[s % 2]
            if s + 1 < nst:
                load_x(s + 1)
            xnT, t_xnT = nt.run(xb, txb)
            for g in range(NF // 2):
                slot = gi % 2
                gi += 1
                if g + 1 < NF // 2:
                    load_w(g + 1, gi % 2)
                elif s + 1 < nst:
                    load_w(0, gi % 2)
                for c in range(2):
                    f = g * 2 + c
                    pg, tpg = cx.psb[f % 2]
                    pu, tpu = cx.psb[2 + f % 2]
                    for k in range(8):
                        P.add("pe", lambda e, k=k, c=c, pg=pg, slot=slot: e.matmul(
                            pg[:], lhsT=wgu[slot][:, 0, k, c * 128:(c + 1) * 128], rhs=xnT[:, k, :],
                            start=(k == 0), stop=(k == 7)), [t_wgu[slot], t_xnT], [tpg])
                    for k in range(8):
                        P.add("pe", lambda e, k=k, c=c, pu=pu, slot=slot: e.matmul(
                            pu[:], lhsT=wgu[slot][:, 1, k, c * 128:(c + 1) * 128], rhs=xnT[:, k, :],
                            start=(k == 0), stop=(k == 7)), [t_wgu[slot], t_xnT], [tpu])
                    sl, tsl = sil[f % 2], t_sil[f % 2]
                    P.add("act", lambda e, pg=pg, sl=sl: e.activation(out=sl[:], in_=pg[:], func=AF.Silu),
                          [tpg], [tsl])
                    P.add("dve", lambda e, pu=pu, sl=sl, f=f: e.tensor_tensor(out=act[:, f, :], in0=sl[:],
                                                                             in1=pu[:], op=ALU.mult),
                          [tsl, tpu], [t_act[f]])
            for j in range(4):
                tl.run(lambda k, j=j: act[:, k, j * 128:(j + 1) * 128], NF,
                       lambda k, hf: wd_sb[:, k, hf * 512:(hf + 1) * 512], t_act, t_wd,
                       xb[:, j, :], txb, h_out_v[s, j], t_hout)
        P.barrier()
    return t_hout


def emit_proj(cx, h_in, t_hin, w_in, gpre, pos, invtab_d, o_f32, o_qkv, o_gate, NT, tag):
    P, nc = cx.P, cx.nc
    nst = NT // ST
    t_o = [P.tok("of32"), P.tok("oqkv"), P.tok("ogate")]
    with ExitStack() as es:
        sb = lambda name, shape, dt: cx.sb(es, tag + name, shape, dt)
        nt = NormT(cx, es, gpre, tag)
        win = sb("win", [128, 8, IN_COLS], BF16)
        t_win = P.tok("win")
        stg = [sb("stg", [128, IN_COLS], F32) for _ in range(2)]
        t_stg = P.toks("stg", 2)
        w_v = w_in.rearrange("(k p) c -> p k c", p=128)
        for k in range(8):
            load_cast(cx, win[:, k, :], t_win, w_v[:, k, :], stg[k % 2][:], t_stg[k % 2],
                      "sp" if k % 2 == 0 else "act", "act" if k % 2 == 0 else "dve")
        xt = [sb("xt", [128, 4, D], F32) for _ in range(2)]
        t_xt = P.toks("xt", 2)
        invtab = sb("invtab", [128, 32], F32)
        t_inv = P.tok("inv")
        P.dma("sp", invtab[:], invtab_d, [], t_inv)
        posi = sb("posi", [128, NT // 128], I32)
        posf = sb("posf", [128, NT // 128], F32)
        t_pos = P.tok("pos")
        P.dma("sp", posi[:], pos.rearrange("(t p) -> p t", p=128), [], t_pos, allow_slow_non_contiguous=True)
        P.add("dve", lambda e: e.tensor_copy(out=posf[:], in_=posi[:]), [t_pos], [t_pos])
        pr = [sb("pr", [128, IN_COLS], F32) for _ in range(2)]
        t_pr = P.toks("pr", 2)
        pof = [sb("pof", [128, 768], F32) for _ in range(2)]
        poq = [sb("poq", [128, 1280], BF16) for _ in range(2)]
        pog = [sb("pog", [128, 24], F32) for _ in range(2)]
        t_pof, t_poq, t_pog = P.toks("pof", 2), P.toks("poq", 2), P.toks("pog", 2)
        tr = {n: sb(n, [128, 32], F32) for n in ("ang", "kf", "r", "fl", "sin", "cos", "ang2")}
        ki = sb("ki", [128, 32], I32)
        t_trig = P.tok("trig")
        tmp = [sb("tmp", [128, 10, 32], F32) for _ in range(4)]
        t_tmp = P.tok("tmp")
        h_in_v = h_in.rearrange("(s j p) d -> s p j d", p=128, j=4)
        of_v = o_f32.rearrange("(t p) c -> t p c", p=128)
        oq_v = o_qkv.rearrange("(t p) c -> t p c", p=128)
        og_v = o_gate.rearrange("(t p) c -> t p c", p=128)
        cgroups = [(0, 512), (512, 512), (1024, 512), (1536, 512), (2048, 24)]

        def load_x(s):
            P.dma("sp", xt[s % 2][:], h_in_v[s], [t_hin], t_xt[s % 2])

        def trig(dst, src, tt):
            ang, kf, r, fl = src, tr["kf"], tr["r"], tr["fl"]
            P.add("dve", lambda e: e.tensor_scalar(out=kf[:], in0=ang[:], scalar1=1.0 / (2 * PI), scalar2=None,
                                                   op0=ALU.mult), [tt], [tt])
            P.add("dve", lambda e: e.tensor_copy(out=ki[:], in_=kf[:]), [tt], [tt])
            P.add("dve", lambda e: e.tensor_copy(out=kf[:], in_=ki[:]), [tt], [tt])
            P.add("dve", lambda e: e.scalar_tensor_tensor(out=r[:], in0=kf[:], scalar=-2 * PI, in1=ang[:],
                                                          op0=ALU.mult, op1=ALU.add), [tt], [tt])
            P.add("dve", lambda e: e.tensor_scalar(out=fl[:], in0=r[:], scalar1=PI, scalar2=None, op0=ALU.is_gt),
                  [tt], [tt])
            P.add("dve", lambda e: e.scalar_tensor_tensor(out=kf[:], in0=fl[:], scalar=-2 * PI, in1=r[:],
                                                          op0=ALU.mult, op1=ALU.add), [tt], [tt])
            P.add("dve", lambda e: e.tensor_scalar(out=fl[:], in0=kf[:], scalar1=-PI, scalar2=None, op0=ALU.is_lt),
                  [tt], [tt])
            P.add("dve", lambda e: e.scalar_tensor_tensor(out=r[:], in0=fl[:], scalar=2 * PI, in1=kf[:],
                                                          op0=ALU.mult, op1=ALU.add), [tt], [tt])
            P.add("act", lambda e: e.activation(out=dst[:], in_=r[:], func=AF.Sin), [tt], [tt])

        load_x(0)
        ti = 0
        for s in range(nst):
            xb, txb = xt[s % 2], t_xt[s % 2]
            if s + 1 < nst:
                load_x(s + 1)
            xnT, t_xnT = nt.run(xb, txb)
            for j in range(4):
                tix = s * 4 + j
                prb, tprb = pr[tix % 2], t_pr[tix % 2]
                for gi_, (c0, cw) in enumerate(cgroups):
                    pp, tpp = cx.psb[gi_ % 4]
                    for k in range(8):
                        P.add("pe", lambda e, k=k, j=j, c0=c0, cw=cw, pp=pp: e.matmul(
                            pp[:, 0:cw], lhsT=xnT[:, k, j * 128:(j + 1) * 128], rhs=win[:, k, c0:c0 + cw],
                            start=(k == 0), stop=(k == 7)), [t_xnT, t_win], [tpp])
                    if gi_ % 2 == 0:
                        P.add("act", lambda e, pp=pp, c0=c0, cw=cw, prb=prb: e.activation(
                            out=prb[:, c0:c0 + cw], in_=pp[:, 0:cw], func=AF.Copy), [tpp], [tprb])
                    else:
                        P.add("dve", lambda e, pp=pp, c0=c0, cw=cw, prb=prb: e.tensor_copy(
                            out=prb[:, c0:c0 + cw], in_=pp[:, 0:cw]), [tpp], [tprb])
                ang = tr["ang"]
                P.add("dve", lambda e, tix=tix: e.tensor_scalar(out=ang[:], in0=invtab[:],
                                                                scalar1=posf[:, tix:tix + 1], scalar2=None,
                                                                op0=ALU.mult), [t_inv, t_pos], [t_trig])
                trig(tr["sin"], ang, t_trig)
                P.add("dve", lambda e: e.tensor_scalar(out=tr["ang2"][:], in0=ang[:], scalar1=PI / 2, scalar2=None,
                                                       op0=ALU.add), [t_trig], [t_trig])
                trig(tr["cos"], tr["ang2"], t_trig)
                fo, tfo = pof[tix % 2], t_pof[tix % 2]
                qo, tqo = poq[tix % 2], t_poq[tix % 2]
                go, tgo = pog[tix % 2], t_pog[tix % 2]
                P.add("act", lambda e, fo=fo, prb=prb: e.activation(out=fo[:], in_=prb[:, 0:768], func=AF.Copy),
                      [tprb], [tfo])
                P.add("act", lambda e, go=go, prb=prb: e.activation(out=go[:], in_=prb[:, 2048:2072],
                                                                    func=AF.Sigmoid), [tprb], [tgo])
                for c0 in (1408, 1664, 1920):
                    P.add("dve", lambda e, qo=qo, prb=prb, c0=c0: e.tensor_copy(
                        out=qo[:, c0 - 768:c0 - 768 + 128], in_=prb[:, c0:c0 + 128]), [tprb], [tqo])
                for (c0, nh) in ((768, 10), (1536, 2), (1792, 2)):
                    xv = prb[:, c0:c0 + nh * 64].rearrange("p (h t f) -> p h t f", t=2, f=32)
                    ov = qo[:, c0 - 768:c0 - 768 + nh * 64].rearrange("p (h t f) -> p h t f", t=2, f=32)
                    x1, x2 = xv[:, :, 0, :], xv[:, :, 1, :]
                    cb = tr["cos"][:].unsqueeze(1).broadcast_to([128, nh, 32])
                    sb_ = tr["sin"][:].unsqueeze(1).broadcast_to([128, nh, 32])
                    a, b, c, d = (t[:, 0:nh, :] for t in tmp)
                    P.add("dve", lambda e, a=a, x1=x1, cb=cb: e.tensor_tensor(out=a, in0=x1, in1=cb, op=ALU.mult),
                          [tprb, t_trig], [t_tmp])
                    P.add("dve", lambda e, b=b, x2=x2, sb_=sb_: e.tensor_tensor(out=b, in0=x2, in1=sb_, op=ALU.mult),
                          [tprb, t_trig], [t_tmp])
                    P.add("dve", lambda e, a=a, b=b, ov=ov: e.tensor_tensor(out=ov[:, :, 0, :], in0=a, in1=b,
                                                                           op=ALU.subtract), [t_tmp], [tqo])
                    P.add("dve", lambda e, c=c, x2=x2, cb=cb: e.tensor_tensor(out=c, in0=x2, in1=cb, op=ALU.mult),
                          [tprb, t_trig], [t_tmp])
                    P.add("dve", lambda e, d=d, x1=x1, sb_=sb_: e.tensor_tensor(out=d, in0=x1, in1=sb_, op=ALU.mult),
                          [tprb, t_trig], [t_tmp])
                    P.add("dve", lambda e, c=c, d=d, ov=ov: e.tensor_tensor(out=ov[:, :, 1, :], in0=c, in1=d,
                                                                           op=ALU.add), [t_tmp], [tqo])
                P.dma("sp", of_v[tix], fo[:], [tfo], t_o[0])
                P.dma("act", oq_v[tix], qo[:], [tqo], t_o[1])
                P.dma("sp", og_v[tix], go[:], [tgo], t_o[2])
        P.barrier()
    return t_o


POOL_WINDOWS = (2, 4, 8, 16)
SCALE = 0.125
BIG = 1.0e9


def _bd_build(cx, es, P, name, w_d, idx0, idx1):
    f = cx.sb(es, name + "f", [128, 128], F32)
    b = cx.sb(es, name + "b", [128, 128], BF16)
    t = P.tok(name)
    P.add("dve", lambda e: e.memset(f[:], 0.0), [], [t])
    P.dma("sp", f[0:64, 0:64], w_d[idx0], [], t)
    P.dma("sp", f[64:128, 64:128], w_d[idx1], [], t)
    P.add("dve", lambda e: e.tensor_copy(out=b[:], in_=f[:]), [t], [t])
    return b, t


def emit_pool(cx, xpT_d, poolfix_d, pool_w, pool_scale, yT_d, t_yT, TC):
    P, nc = cx.P, cx.nc
    L = 16 + TC
    with ExitStack() as es:
        sb = lambda n, s, d: cx.sb(es, "pl" + n, s, d)
        xp = sb("xp", [128, 2, L], F32)
        t_xp = P.tok("xp")
        P.dma("sp", xp[:], xpT_d, [], t_xp)
        fix = sb("fix", [128, 2, 16], F32)
        t_fix = P.tok("fix")
        P.dma("sp", fix[:], poolfix_d, [], t_fix)
        psc = sb("psc", [128, 2], F32)
        t_psc = P.tok("psc")
        P.dma("sp", psc[:], pool_scale.rearrange("(c p) -> p c", p=128), [], t_psc, allow_slow_non_contiguous=True)
        wA = sb("wA", [128, L], F32)
        wB = sb("wB", [128, L], F32)
        t_wA, t_wB = P.tok("wA"), P.tok("wB")
        dfb = sb("dfb", [128, TC], BF16)
        t_dfb = P.tok("dfb")
        yo = [sb("yo", [128, 512], BF16) for _ in range(2)]
        t_yo = P.toks("yo", 2)
        for cc in range(2):
            bd, t_bd = _bd_build(cx, es, P, f"plbd{cc}", pool_w, 2 * cc, 2 * cc + 1)
            for hh in range(2):
                w = POOL_WINDOWS[2 * cc + hh]
                rows = slice(hh * 64, hh * 64 + 64)
                src, tsrc = xp[rows, cc, :], t_xp
                bufs = [(wA, t_wA), (wB, t_wB)]
                lo, sh, bi = 0, 1, 0
                while sh < w:
                    dst, tdst = bufs[bi]
                    lo2 = lo + sh
                    P.add("dve", lambda e, dst=dst, src=src, lo2=lo2, sh=sh, rows=rows: e.tensor_tensor(
                        out=dst[rows, lo2:L], in0=src[:, lo2:L], in1=src[:, lo2 - sh:L - sh], op=ALU.add),
                        [tsrc], [tdst])
                    src, tsrc = dst[rows, :], tdst
                    lo, sh, bi = lo2, sh * 2, 1 - bi
                dst, tdst = bufs[bi]
                P.add("dve", lambda e, dst=dst, src=src, rows=rows, w=w: e.tensor_scalar(
                    out=dst[rows, 16:L], in0=src[:, 16:L], scalar1=1.0 / w, scalar2=None, op0=ALU.mult),
                    [tsrc], [tdst])
                P.add("dve", lambda e, dst=dst, rows=rows, cc=cc: e.tensor_tensor(
                    out=dst[rows, 16:32], in0=dst[rows, 16:32], in1=fix[rows, cc, :], op=ALU.mult),
                    [tdst, t_fix], [tdst])
                P.add("dve", lambda e, dst=dst, rows=rows, cc=cc: e.tensor_tensor(
                    out=dfb[rows, :], in0=dst[rows, 16:L], in1=xp[rows, cc, 16:L], op=ALU.subtract),
                    [tdst, t_xp], [t_dfb])
            for ch in range(TC // 512):
                pp, tpp = cx.psb[ch % 2]
                P.add("pe", lambda e, pp=pp, ch=ch, bd=bd: e.matmul(pp[:], lhsT=bd[:], rhs=dfb[:, ch * 512:(ch + 1) * 512],
                                                                   start=True, stop=True), [t_bd, t_dfb], [tpp])
                yb, tyb = yo[ch % 2], t_yo[ch % 2]
                P.add("act", lambda e, pp=pp, yb=yb, cc=cc: e.activation(out=yb[:], in_=pp[:], func=AF.Copy,
                                                                        scale=psc[:, cc:cc + 1]), [tpp, t_psc], [tyb])
                P.dma("sp", yT_d[:, cc, ch * 512:(ch + 1) * 512], yb[:], [tyb], t_yT)
        P.barrier()


def emit_lru(cx, xlT_d, glT_d, lruflag_d, conv_w, conv_b, w_r, b_r, w_i, b_i, lam, yT_d, t_yT, S, TC):
    P, nc = cx.P, cx.nc
    NLC = S // 512
    own0 = NLC - TC // 512
    with ExitStack() as es:
        sb = lambda n, s, d: cx.sb(es, "lr" + n, s, d)
        flag = sb("flag", [128, NLC], F32)
        t_flag = P.tok("flag")
        P.dma("sp", flag[:], lruflag_d, [], t_flag)
        one = sb("one", [128, 1], F32)
        t_one = P.tok("one")
        P.add("dve", lambda e: e.memset(one[:], 1.0), [], [t_one])
        for cc in range(2):
            cw = sb("cw", [128, 4], F32)
            cst = sb("cst", [128, 8], F32)
            t_c = P.tok("lrc")
            P.dma("sp", cw[:], conv_w[:, cc * 128:(cc + 1) * 128].rearrange("k p -> p k"), [], t_c,
                  allow_slow_non_contiguous=True)
            for ci, v in enumerate((conv_b, b_r, b_i, lam)):
                P.dma("sp", cst[:, ci:ci + 1], v[cc * 128:(cc + 1) * 128].rearrange("(p o) -> p o", o=1), [], t_c,
                      allow_slow_non_contiguous=True)
            P.add("act", lambda e, cst=cst: e.activation(out=cst[:, 4:5], in_=cst[:, 3:4], func=AF.Exp, scale=-1.0),
                  [t_c], [t_c])
            P.add("dve", lambda e, cst=cst: e.tensor_scalar(out=cst[:, 4:5], in0=cst[:, 4:5], scalar1=1.0,
                                                            scalar2=None, op0=ALU.add), [t_c], [t_c])
            P.add("act", lambda e, cst=cst: e.activation(out=cst[:, 5:6], in_=cst[:, 4:5], func=AF.Ln), [t_c], [t_c])
            P.add("dve", lambda e, cst=cst: e.tensor_scalar(out=cst[:, 6:7], in0=cst[:, 5:6], scalar1=-8.0,
                                                            scalar2=None, op0=ALU.mult), [t_c], [t_c])
            P.add("dve", lambda e, cst=cst: e.tensor_scalar(out=cst[:, 7:8], in0=cst[:, 5:6], scalar1=-16.0,
                                                            scalar2=None, op0=ALU.mult), [t_c], [t_c])
            bdr, t_bdr = _bd_build(cx, es, P, f"bdr{cc}", w_r, 2 * cc, 2 * cc + 1)
            bdi, t_bdi = _bd_build(cx, es, P, f"bdi{cc}", w_i, 2 * cc, 2 * cc + 1)
            hst = sb("hst", [128, 2], F32)
            t_hst = P.tok("hst")
            P.add("dve", lambda e, hst=hst: e.memset(hst[:], 0.0), [], [t_hst])
            X = [sb("X", [128, 516], F32) for _ in range(2)]
            t_X = P.toks("X", 2)
            G = [sb("G", [128, 512], F32) for _ in range(2)]
            t_G = P.toks("G", 2)
            w = {n: sb(n, [128, 512], F32) for n in ("xc", "r", "i", "a", "a2", "m", "t", "b", "h", "g2", "u", "sg", "ge")}
            tw = {n: P.tok(n) for n in w}
            xcb = sb("xcb", [128, 512], BF16)
            t_xcb = P.tok("xcb")
            yo = [sb("yo", [128, 512], BF16) for _ in range(2)]
            t_yo = P.toks("yo", 2)
            for ch in range(NLC):
                Xb, tX = X[ch % 2], t_X[ch % 2]
                P.dma("sp", Xb[:], xlT_d[:, cc, ch * 512:ch * 512 + 516], [], tX)
                xc = w["xc"]
                P.add("dve", lambda e, Xb=Xb, cw=cw, cst=cst: e.tensor_scalar(
                    out=xc[:], in0=Xb[:, 4:516], scalar1=cw[:, 3:4], scalar2=cst[:, 0:1], op0=ALU.mult, op1=ALU.add),
                    [tX, t_c], [tw["xc"]])
                for k in range(3):
                    P.add("dve", lambda e, Xb=Xb, cw=cw, k=k: e.scalar_tensor_tensor(
                        out=xc[:], in0=Xb[:, 1 + k:513 + k], scalar=cw[:, k:k + 1], in1=xc[:], op0=ALU.mult,
                        op1=ALU.add), [tX, t_c, tw["xc"]], [tw["xc"]])
                P.add("act", lambda e: e.activation(out=xcb[:], in_=xc[:], func=AF.Copy), [tw["xc"]], [t_xcb])
                pr, tpr = cx.psb[0]
                pi_, tpi = cx.psb[1]
                P.add("pe", lambda e, bdr=bdr, pr=pr: e.matmul(pr[:], lhsT=bdr[:], rhs=xcb[:], start=True, stop=True),
                      [t_bdr, t_xcb], [tpr])
                P.add("pe", lambda e, bdi=bdi, pi_=pi_: e.matmul(pi_[:], lhsT=bdi[:], rhs=xcb[:], start=True, stop=True),
                      [t_bdi, t_xcb], [tpi])
                P.add("act", lambda e, pr=pr, cst=cst: e.activation(out=w["r"][:], in_=pr[:], func=AF.Sigmoid,
                                                                    bias=cst[:, 1:2]), [tpr, t_c], [tw["r"]])
                P.add("act", lambda e, pi_=pi_, cst=cst: e.activation(out=w["i"][:], in_=pi_[:], func=AF.Sigmoid,
                                                                      bias=cst[:, 2:3]), [tpi, t_c], [tw["i"]])
                P.add("act", lambda e, cst=cst: e.activation(out=w["a"][:], in_=w["r"][:], func=AF.Exp,
                                                             scale=cst[:, 6:7]), [tw["r"], t_c], [tw["a"]])
                P.add("act", lambda e, cst=cst: e.activation(out=w["a2"][:], in_=w["r"][:], func=AF.Exp,
                                                             scale=cst[:, 7:8]), [tw["r"], t_c], [tw["a2"]])
                P.add("act", lambda e: e.activation(out=w["m"][:], in_=w["a2"][:], func=AF.Sqrt, scale=-1.0,
                                                    bias=one[:, 0:1]), [tw["a2"], t_one], [tw["m"]])
                P.add("dve", lambda e: e.tensor_tensor(out=w["t"][:], in0=w["i"][:], in1=xc[:], op=ALU.mult),
                      [tw["i"], tw["xc"]], [tw["t"]])
                P.add("dve", lambda e, ch=ch: e.scalar_tensor_tensor(out=w["b"][:], in0=w["t"][:],
                                                                    scalar=flag[:, ch:ch + 1], in1=w["m"][:],
                                                                    op0=ALU.mult, op1=ALU.mult),
                      [tw["t"], tw["m"], t_flag], [tw["b"]])
                P.add("dve", lambda e, hst=hst: e.tensor_tensor_scan(out=w["h"][:], data0=w["a"][:], data1=w["b"][:],
                                                                    initial=hst[:, 0:1], op0=ALU.mult, op1=ALU.add),
                      [tw["a"], tw["b"], t_hst], [tw["h"]])
                P.add("dve", lambda e, hst=hst: e.tensor_copy(out=hst[:, 0:1], in_=w["h"][:, 511:512]),
                      [tw["h"]], [t_hst])
                if ch >= own0:
                    oc = ch - own0
                    Gb, tG = G[oc % 2], t_G[oc % 2]
                    P.dma("act", Gb[:], glT_d[:, cc, oc * 512:(oc + 1) * 512], [], tG)
                    P.add("dve", lambda e, Gb=Gb: e.tensor_tensor(out=w["g2"][:], in0=Gb[:], in1=Gb[:], op=ALU.mult),
                          [tG], [tw["g2"]])
                    P.add("dve", lambda e: e.tensor_scalar(out=w["u"][:], in0=w["g2"][:], scalar1=0.044715,
                                                           scalar2=1.0, op0=ALU.mult, op1=ALU.add),
                          [tw["g2"]], [tw["u"]])
                    P.add("dve", lambda e, Gb=Gb: e.tensor_tensor(out=w["g2"][:], in0=w["u"][:], in1=Gb[:],
                                                                  op=ALU.mult), [tw["u"], tG], [tw["g2"]])
                    P.add("act", lambda e: e.activation(out=w["sg"][:], in_=w["g2"][:], func=AF.Sigmoid,
                                                        scale=1.5957691216057308), [tw["g2"]], [tw["sg"]])
                    P.add("dve", lambda e, Gb=Gb: e.tensor_tensor(out=w["ge"][:], in0=w["sg"][:], in1=Gb[:],
                                                                  op=ALU.mult), [tw["sg"], tG], [tw["ge"]])
                    yb, tyb = yo[oc % 2], t_yo[oc % 2]
                    P.add("dve", lambda e, yb=yb: e.tensor_tensor(out=yb[:], in0=w["h"][:], in1=w["ge"][:],
                                                                  op=ALU.mult), [tw["h"], tw["ge"]], [tyb])
                    P.dma("sp", yT_d[:, 2 + cc, oc * 512:(oc + 1) * 512], yb[:], [tyb], t_yT)
        P.barrier()


def emit_attn(cx, a, yT_d, t_yT, S, TC):
    P, nc = cx.P, cx.nc
    TCT, NCH, NSEL, NCM = TC // 128, S // 128, S // 64, S // 16
    NCC = max(NCM // 128, 1)
    JH = min(NSEL, 128)
    NJH = max(NSEL // 128, 1)
    NB = max(NCM // 512, 1)
    NBW = min(NCM, 512)
    with ExitStack() as es:
        sb = lambda n, s, d: cx.sb(es, "at" + n, s, d)
        T = P.tok
        QT = sb("QT", [128, 4, TC], BF16); t_QT = T("QT")
        P.dma("sp", QT[:], a["QT"], [], t_QT)
        KsT = sb("KsT", [128, S], BF16); t_KsT = T("KsT")
        P.dma("act", KsT[:], a["KsT"], [], t_KsT)
        VsE = sb("VsE", [128, NCH, 2, 65], BF16); t_VsE = T("VsE")
        P.add("dve", lambda e: e.memset(VsE[:], 1.0), [], [t_VsE])
        P.dma("sp", VsE[:, :, :, 0:64], a["Vs"].rearrange("p c (h d) -> p c h d", h=2), [], t_VsE)
        KwT = sb("KwT", [128, 512 + TC], BF16); t_KwT = T("KwT")
        P.dma("act", KwT[:], a["KwT"], [], t_KwT)
        VwE = sb("VwE", [128, 4 + TCT, 2, 65], BF16); t_VwE = T("VwE")
        P.add("dve", lambda e: e.memset(VwE[:], 1.0), [], [t_VwE])
        P.dma("sp", VwE[:, :, :, 0:64], a["Vw"].rearrange("p c (h d) -> p c h d", h=2), [], t_VwE)
        Ebig = sb("Ebig", [128, 8192], BF16); t_cst = T("cst")
        P.dma("act", Ebig[:], a["Ebig"], [], t_cst)
        Akq = sb("Akq", [128, 128], F32); Ac = sb("Ac", [128, 128], F32); identf = sb("idf", [128, 128], F32)
        iota16 = sb("iota16", [128, NCM], F32)
        for dst, src in ((Akq, "Akq"), (Ac, "Ac"), (identf, "identf"), (iota16, "iota16")):
            P.dma("sp", dst[:], a[src], [], t_cst)
        gate = sb("gate", [128, TCT, 24], F32)
        P.dma("sp", gate[:], a["gate"], [], t_cst)
        NTHR = a["thr"].shape[1]
        thr = sb("thr", [128, NTHR], F32)
        P.dma("sp", thr[:], a["thr"], [], t_cst)
        thrq = sb("thrq", [128, TCT], F32)
        P.dma("sp", thrq[:], a["thrq"], [], t_cst)
        o_c, o_d, o_w = 0, TCT * NCC, TCT * NCC + TCT * 4
        KcmpT = sb("KcmpT", [128, NCM], BF16); t_Kcmp = T("Kcmp")
        VcE = sb("VcE", [128, NCC, 2, 65], BF16); t_VcE = T("VcE")
        P.add("dve", lambda e: e.memset(VcE[:], 1.0), [], [t_VcE])
        with ExitStack() as es2:
            sb2 = lambda n, s, d: cx.sb(es2, "cm" + n, s, d)
            src = sb2("src", [128, S + 16], BF16); t_src = T("src")
            wst = sb2("wst", [128, 32, 64], F32); t_wst = T("wst")
            BD = sb2("BD", [128, 32, 128], BF16); t_BD = T("BD")
            pes = sb2("pes", [128, 32], F32); PEt = sb2("PEt", [128, 32], BF16); t_pe = T("pe")
            cpe = sb2("cpe", [128, 2], F32); t_cpe = T("cpe")
            VcT = sb2("VcT", [128, NCM], BF16); t_VcT = T("VcT")
            for hh in range(2):
                P.dma("sp", pes[hh * 64:(hh + 1) * 64, :], a["cmp_pe"].rearrange("l d -> d l"), [], t_pe,
                      allow_slow_non_contiguous=True)
            P.add("dve", lambda e: e.tensor_copy(out=PEt[:], in_=pes[:]), [t_pe], [t_pe])
            for which, (srcname, wname) in enumerate((("KcT", "cmp_w_k"), ("VcT", "cmp_w_v"))):
                P.add("dve", lambda e: e.memset(src[:, S:S + 16], 0.0), [], [t_src])
                P.dma("sp", src[:, 0:S], a[srcname], [], t_src)
                for hh in range(2):
                    P.dma("act", wst[hh * 64:(hh + 1) * 64, :, :], a[wname].rearrange("l d e -> d l e"), [], t_wst)
                P.add("dve", lambda e: e.memset(BD[:], 0.0), [], [t_BD])
                for hh in range(2):
                    P.add("dve", lambda e, hh=hh: e.tensor_copy(out=BD[hh * 64:(hh + 1) * 64, :, hh * 64:(hh + 1) * 64],
                                                                in_=wst[hh * 64:(hh + 1) * 64, :, :]), [t_wst], [t_BD])
                pc, tpc = cx.psb[2]
                for l in range(32):
                    P.add("pe", lambda e, l=l, pc=pc: e.matmul(pc[:, 0:1], lhsT=BD[:, l, :], rhs=PEt[:, l:l + 1],
                                                               start=(l == 0), stop=(l == 31)), [t_BD, t_pe], [tpc])
                P.add("dve", lambda e, pc=pc, which=which: e.tensor_copy(out=cpe[:, which:which + 1], in_=pc[:, 0:1]),
                      [tpc], [t_cpe])
                sv = src[:].rearrange("p (n s) -> p n s", s=16)
                dstT = KcmpT if which == 0 else VcT
                tdst = t_Kcmp if which == 0 else t_VcT
                for nb in range(NB):
                    pk, tpk = cx.psb[nb % 2]
                    n0 = nb * NBW
                    for l in range(32):
                        rhs = sv[:, n0:n0 + NBW, l] if l < 16 else sv[:, n0 + 1:n0 + 1 + NBW, l - 16]
                        P.add("pe", lambda e, l=l, pk=pk, rhs=rhs: e.matmul(pk[:, 0:NBW], lhsT=BD[:, l, :], rhs=rhs,
                                                                          start=(l == 0), stop=(l == 31)),
                              [t_BD, t_src], [tpk])
                    P.add("act", lambda e, pk=pk, n0=n0, dstT=dstT, which=which: e.activation(
                        out=dstT[:, n0:n0 + NBW], in_=pk[:, 0:NBW], func=AF.Identity, bias=cpe[:, which:which + 1]),
                        [tpk, t_cpe], [tdst])
            for c in range(NCC):
                pv, tpv = cx.psb[3]
                P.add("pe", lambda e, c=c, pv=pv: e.matmul(pv[:, 0:128], lhsT=VcT[:, c * 128:(c + 1) * 128],
                                                           rhs=cx.ident[:], start=True, stop=True),
                      [t_VcT, cx.t_ident], [tpv])
                P.add("dve", lambda e, c=c, pv=pv: e.tensor_copy(
                    out=VcE[:, c, :, 0:64], in_=pv[:, 0:128].rearrange("p (h d) -> p h d", h=2)), [tpv], [t_VcE])
            P.barrier()
        cand = [sb("cand", [128, NSEL], F32) for _ in range(2)]; t_cand = P.toks("cand", 2)
        forc = [sb("forc", [128, NSEL], F32) for _ in range(2)]; t_forc = P.toks("forc", 2)
        Etm = sb("Etm", [128, NCM], F32); t_Etm = T("Etm")
        Pm_tm = sb("Pmtm", [128, NCM], F32); t_Pmtm = T("Pmtm")
        st = sb("st", [128, 16], F32); t_st = T("st")
        t4 = sb("t4", [128, NSEL], F32); t_t4 = T("t4")
        imp = sb("imp", [128, NSEL], F32); t_imp = T("imp")
        score = sb("score", [128, NSEL], F32); sc2 = sb("sc2", [128, NSEL], F32); t_sc = T("score")
        m8 = sb("m8", [128, 16], F32); t_m8 = T("m8")
        sel = sb("sel", [128, NSEL], BF16); t_sel = T("sel")
        selT = sb("selT", [128, NJH, 128], BF16); t_selT = T("selT")
        P.add("dve", lambda e: e.memset(selT[:], 0.0), [], [t_selT])
        E = [sb("E", [128, 4, 128], BF16) for _ in range(4)]; t_E = P.toks("E", 4)
        Pm = [sb("Pm", [128, 4, 128], BF16) for _ in range(4)]; t_Pm = P.toks("Pm", 4)
        SBK = [0, 1, 6]
        MBK = [2, 3, 7]
        LA = 2
        OTs3 = [sb("OTs3", [65, 512], F32) for _ in range(3)]; t_OTs3 = P.toks("OTs3", 3)
        OTs = sb("OTs", [65, 512], F32); t_OTs = T("OTs")
        w4 = sb("w4", [128, 8], F32); t_w4 = T("w4")
        tmpo = sb("tmpo", [128, 4, 64], F32); t_tmpo = T("tmpo")
        oacc = sb("oacc", [128, 512], F32); t_oacc = T("oacc")
        ob = sb("ob", [128, 512], BF16); t_ob = T("ob")
        oT = [sb("oT", [128, 128], BF16) for _ in range(2)]; t_oT = P.toks("oT", 2)
        step = [0]

        def run_steps(steps, qv):
            n = len(steps)
            fr = [None] * n

            def front(k):
                i = step[0]; step[0] += 1
                stp = steps[k]
                ps, tps = cx.psb[SBK[i % 3]]
                P.add("pe", lambda e, ps=ps, stp=stp: e.matmul(ps[:], lhsT=stp["lhsK"], rhs=qv, start=True, stop=True),
                      [stp["tK"], t_QT], [tps])
                aux = stp["pre"](i) if stp.get("pre") else None
                fr[k] = (i, ps, tps, aux)

            def back(k):
                stp = steps[k]
                i, ps, tps, aux = fr[k]
                Eb, tE = E[i % 3], t_E[i % 3]
                P.add("act", lambda e, ps=ps, Eb=Eb: e.activation(out=Eb[:].rearrange("p g q -> p (g q)"), in_=ps[:],
                                                                  func=AF.Exp, scale=SCALE), [tps], [tE])
                rhsP, tP = stp["mask"](Eb, tE, i, aux)
                OT, tOT = stp["OT"]
                P.add("pe", lambda e, rhsP=rhsP, stp=stp, OT=OT: e.matmul(
                    OT[0:65, :], lhsT=stp["vE"], rhs=rhsP[:].rearrange("p g q -> p (g q)"),
                    start=stp["first"], stop=stp["last"]), [stp["tV"], tP], [tOT])
                if stp["last"]:
                    br = stp["br"]
                    P.add("act", lambda e, OT=OT, br=br: e.activation(out=OTs3[br][:], in_=OT[0:65, :], func=AF.Copy),
                          [tOT], [t_OTs3[br]])

            for k in range(n + LA):
                if k < n:
                    front(k)
                if k >= LA:
                    back(k - LA)

        for m in range(TCT):
            cb, tcb = cand[m % 2], t_cand[m % 2]
            fb, tfb = forc[m % 2], t_forc[m % 2]
            P.dma("sp", cb[:], a["cand"][m], [], tcb)
            P.dma("act", fb[:], a["forced"][m], [], tfb)
            for h in range(2):
                rows = slice(h * 64, (h + 1) * 64)
                qv = QT[rows, :, m * 128:(m + 1) * 128]
                P.add("dve", lambda e: e.memset(st[:], 0.0), [], [t_st])
                for g in range(4):
                    for nb in range(NB):
                        ps, tps = cx.psb[2 + nb % 2]
                        P.add("pe", lambda e, ps=ps, g=g, nb=nb, rows=rows, m=m: e.matmul(
                            ps[:, 0:NBW], lhsT=QT[rows, g, m * 128:(m + 1) * 128], rhs=KcmpT[rows, nb * NBW:(nb + 1) * NBW],
                            start=True, stop=True), [t_QT, t_Kcmp], [tps])
                        P.add("act", lambda e, ps=ps, nb=nb: e.activation(out=Etm[:, nb * NBW:(nb + 1) * NBW],
                                                                          in_=ps[:, 0:NBW], func=AF.Exp, scale=SCALE),
                              [tps], [t_Etm])
                    P.add("dve", lambda e, g=g, m=m: e.scalar_tensor_tensor(
                        out=Pm_tm[:], in0=iota16[:], scalar=thrq[:, m:m + 1], in1=Etm[:], op0=ALU.is_le, op1=ALU.mult,
                        accum_out=st[:, g:g + 1]), [t_cst, t_Etm, t_st], [t_Pmtm, t_st])
                    P.add("dve", lambda e, g=g: e.tensor_scalar(out=st[:, 8 + g:9 + g], in0=st[:, g:g + 1],
                                                                scalar1=1e-30, scalar2=None, op0=ALU.max),
                          [t_st], [t_st])
                    P.add("dve", lambda e, g=g: e.reciprocal(out=st[:, 4 + g:5 + g], in_=st[:, 8 + g:9 + g]),
                          [t_st], [t_st])
                    pv4 = Pm_tm[:].rearrange("p (j r) -> p j r", r=4)
                    P.add("dve", lambda e, pv4=pv4: e.tensor_reduce(out=t4[:], in_=pv4, axis=AX.X, op=ALU.add),
                          [t_Pmtm], [t_t4])
                    P.add("dve", lambda e, pv4=pv4: e.tensor_tensor(out=t4[:, 1:NSEL], in0=t4[:, 1:NSEL],
                                                                    in1=pv4[:, 0:NSEL - 1, 3], op=ALU.add),
                          [t_Pmtm, t_t4], [t_t4])
                    if g == 0:
                        P.add("dve", lambda e: e.tensor_scalar(out=imp[:], in0=t4[:], scalar1=st[:, 4:5], scalar2=None,
                                                               op0=ALU.mult), [t_t4, t_st], [t_imp])
                    else:
                        P.add("dve", lambda e, g=g: e.scalar_tensor_tensor(out=imp[:], in0=t4[:], scalar=st[:, 4 + g:5 + g],
                                                                          in1=imp[:], op0=ALU.mult, op1=ALU.add),
                              [t_t4, t_st, t_imp], [t_imp])
                P.add("dve", lambda e, cb=cb: e.tensor_tensor(out=score[:], in0=imp[:], in1=cb[:], op=ALU.mult),
                      [t_imp, tcb], [t_sc])
                P.add("dve", lambda e, fb=fb: e.tensor_tensor(out=score[:], in0=score[:], in1=fb[:], op=ALU.add),
                      [t_sc, tfb], [t_sc])
                P.add("dve", lambda e: e.max(out=m8[:, 0:8], in_=score[:]), [t_sc], [t_m8])
                P.add("dve", lambda e: e.match_replace(out=sc2[:], in_to_replace=m8[:, 0:8], in_values=score[:],
                                                       imm_value=-1e30), [t_sc, t_m8], [t_sc])
                P.add("dve", lambda e: e.max(out=m8[:, 8:16], in_=sc2[:]), [t_sc], [t_m8])
                P.add("dve", lambda e: e.tensor_scalar(out=m8[:, 0:1], in0=m8[:, 15:16], scalar1=1e-30, scalar2=None,
                                                       op0=ALU.max), [t_m8], [t_m8])
                P.add("dve", lambda e: e.tensor_scalar(out=sel[:], in0=score[:], scalar1=m8[:, 0:1], scalar2=None,
                                                       op0=ALU.is_ge), [t_sc, t_m8], [t_sel])
                for jh in range(NJH):
                    pt, tpt = cx.psb[3]
                    P.add("pe", lambda e, jh=jh, pt=pt: e.matmul(pt[0:JH, 0:128], lhsT=sel[:, jh * 128:jh * 128 + JH],
                                                                 rhs=cx.ident[:], start=True, stop=True),
                          [t_sel, cx.t_ident], [tpt])
                    P.add("act", lambda e, jh=jh, pt=pt: e.activation(out=selT[0:JH, jh, :], in_=pt[0:JH, 0:128],
                                                                      func=AF.Copy), [tpt], [t_selT])
                OTb = {0: cx.psb[5], 1: cx.psb[4], 2: cx.psb[4]}
                steps = []
                kc_last = (NCH // TCT - 1) * TCT + m
                for kc in range(kc_last + 1):
                    def pre(i, kc=kc):
                        pmk, tpmk = cx.psb[MBK[i % 3]][0][:, 0:128], cx.psb[MBK[i % 3]][1]
                        jh = (2 * kc) // 128
                        off = 128 * (kc % 64)
                        P.add("pe", lambda e, pmk=pmk: e.matmul(pmk, lhsT=Ebig[0:JH, off:off + 128],
                                                                rhs=selT[0:JH, jh, :], start=True, stop=True),
                              [t_cst, t_selT], [tpmk])
                        return pmk, tpmk

                    def mf(Eb, tE, i, aux, kc=kc, m=m):
                        pmk, tpmk = aux
                        Pb, tPb = Pm[i % 3], t_Pm[i % 3]
                        P.add("dve", lambda e, pmk=pmk, Pb=Pb, Eb=Eb: e.tensor_tensor(
                            out=Pb[:], in0=Eb[:], in1=pmk.unsqueeze(1).broadcast_to([128, 4, 128]),
                            op=ALU.mult), [tE, tpmk], [tPb])
                        if kc % TCT == m:
                            ci = o_d + m * 4 + kc // TCT
                            P.add("dve", lambda e, Pb=Pb, ci=ci: e.scalar_tensor_tensor(
                                out=Pb[:], in0=Akq[:].unsqueeze(1).broadcast_to([128, 4, 128]), scalar=thr[:, ci:ci + 1],
                                in1=Pb[:], op0=ALU.is_le, op1=ALU.mult), [tPb, t_cst], [tPb])
                        return Pb, tPb
                    steps.append(dict(lhsK=KsT[rows, kc * 128:(kc + 1) * 128], tK=t_KsT, vE=VsE[:, kc, h, :], tV=t_VsE,
                                      pre=pre, mask=mf, OT=OTb[1], br=1, first=(kc == 0), last=(kc == kc_last)))
                for c in range(NCC):
                    def mf(Eb, tE, i, aux, c=c, m=m):
                        Pb, tPb = Pm[i % 3], t_Pm[i % 3]
                        ci = o_c + m * NCC + c
                        P.add("dve", lambda e, Pb=Pb, Eb=Eb, ci=ci: e.scalar_tensor_tensor(
                            out=Pb[:], in0=Ac[:].unsqueeze(1).broadcast_to([128, 4, 128]), scalar=thr[:, ci:ci + 1],
                            in1=Eb[:], op0=ALU.is_le, op1=ALU.mult), [tE, t_cst], [tPb])
                        return Pb, tPb
                    steps.append(dict(lhsK=KcmpT[rows, c * 128:(c + 1) * 128], tK=t_Kcmp, vE=VcE[:, c, h, :], tV=t_VcE,
                                      mask=mf, OT=OTb[0], br=0, first=(c == 0), last=(c == NCC - 1)))
                for c in range(5):
                    def mf(Eb, tE, i, aux, c=c, m=m):
                        if c in (0, 4) or m + c < 4:
                            Pb, tPb = Pm[i % 3], t_Pm[i % 3]
                            ci = o_w + m * 5 + c
                            op = ALU.is_gt if c == 0 else ALU.is_le
                            P.add("dve", lambda e, Pb=Pb, Eb=Eb, ci=ci, op=op: e.scalar_tensor_tensor(
                                out=Pb[:], in0=Akq[:].unsqueeze(1).broadcast_to([128, 4, 128]), scalar=thr[:, ci:ci + 1],
                                in1=Eb[:], op0=op, op1=ALU.mult), [tE, t_cst], [tPb])
                            return Pb, tPb
                        return Eb, tE
                    steps.append(dict(lhsK=KwT[rows, (m + c) * 128:(m + c + 1) * 128], tK=t_KwT,
                                      vE=VwE[:, m + c, h, :], tV=t_VwE, mask=mf, OT=OTb[2], br=2, first=(c == 0), last=(c == 4)))
                run_steps(steps, qv)
                for br in range(3):
                    pt, tpt = cx.psb[7]
                    for g in range(4):
                        P.add("pe", lambda e, g=g, pt=pt, br=br: e.matmul(pt[:, g * 65:(g + 1) * 65],
                                                                   lhsT=OTs3[br][0:65, g * 128:(g + 1) * 128],
                                                                   rhs=identf[0:65, 0:65], start=True, stop=True),
                              [t_OTs3[br], t_cst], [tpt])
                    ptv = pt[:, 0:260].rearrange("p (g e) -> p g e", e=65)
                    P.add("dve", lambda e, ptv=ptv: e.tensor_scalar(out=w4[:, 0:4], in0=ptv[:, :, 64], scalar1=1e-30,
                                                                    scalar2=None, op0=ALU.max), [tpt], [t_w4])
                    P.add("dve", lambda e: e.reciprocal(out=w4[:, 4:8], in_=w4[:, 0:4]), [t_w4], [t_w4])
                    gv = gate[:, m, h * 12:(h + 1) * 12].rearrange("p (g b) -> p g b", b=3)[:, :, br]
                    P.add("dve", lambda e, gv=gv: e.tensor_tensor(out=w4[:, 0:4], in0=w4[:, 4:8], in1=gv, op=ALU.mult),
                          [t_w4, t_cst], [t_w4])
                    oav = oacc[:, h * 256:(h + 1) * 256].rearrange("p (g d) -> p g d", d=64)
                    wb = w4[:, 0:4].unsqueeze(2).broadcast_to([128, 4, 64])
                    if br == 0:
                        P.add("dve", lambda e, ptv=ptv, oav=oav, wb=wb: e.tensor_tensor(
                            out=oav, in0=ptv[:, :, 0:64], in1=wb, op=ALU.mult), [tpt, t_w4], [t_oacc])
                    else:
                        P.add("dve", lambda e, ptv=ptv, wb=wb: e.tensor_tensor(
                            out=tmpo[:], in0=ptv[:, :, 0:64], in1=wb, op=ALU.mult), [tpt, t_w4], [t_tmpo])
                        P.add("dve", lambda e, oav=oav: e.tensor_tensor(out=oav, in0=oav, in1=tmpo[:], op=ALU.add),
                              [t_tmpo, t_oacc], [t_oacc])
            P.add("act", lambda e: e.activation(out=ob[:], in_=oacc[:], func=AF.Copy), [t_oacc], [t_ob])
            for c4 in range(4):
                pt, tpt = cx.psb[7]
                P.add("pe", lambda e, c4=c4, pt=pt: e.matmul(pt[:, 0:128], lhsT=ob[:, c4 * 128:(c4 + 1) * 128],
                                                             rhs=cx.ident[:], start=True, stop=True),
                      [t_ob, cx.t_ident], [tpt])
                otb, totb = oT[c4 % 2], t_oT[c4 % 2]
                P.add("act", lambda e, pt=pt, otb=otb: e.activation(out=otb[:], in_=pt[:, 0:128], func=AF.Copy),
                      [tpt], [totb])
                P.dma("sp", yT_d[:, 4 + c4, m * 128:(m + 1) * 128], otb[:], [totb], t_yT)
        P.barrier()


def emit_wout(cx, yT_d, t_yT, w_out, gpost, h_in, t_hin, h_out, TC):
    P, nc = cx.P, cx.nc
    t_hout = P.tok("hmid")
    with ExitStack() as es:
        sb = lambda n, s, d: cx.sb(es, "wo" + n, s, d)
        tl = Tail(cx, es, gpost, 1.0, "wo")
        wo = sb("w", [128, 8, D], BF16); t_wo = P.tok("wo")
        stg = [sb("stg", [128, D], F32) for _ in range(2)]; t_stg = P.toks("stg", 2)
        w_v = w_out.rearrange("(k p) d -> p k d", p=128)
        for k in range(8):
            load_cast(cx, wo[:, k, :], t_wo, w_v[:, k, :], stg[k % 2][:], t_stg[k % 2],
                      "sp" if k % 2 == 0 else "act", "act" if k % 2 == 0 else "dve")
        xt = [sb("xt", [128, 4, D], F32) for _ in range(2)]; t_xt = P.toks("xt", 2)
        yt = [sb("yt", [128, 8, ST], BF16) for _ in range(2)]; t_yt = P.toks("yt", 2)
        h_in_v = h_in.rearrange("(s j p) d -> s p j d", p=128, j=4)
        h_out_v = h_out.rearrange("(s j p) d -> s j p d", p=128, j=4)
        for s in range(TC // ST):
            xb, txb = xt[s % 2], t_xt[s % 2]
            yb, tyb = yt[s % 2], t_yt[s % 2]
            P.dma("sp", xb[:], h_in_v[s], [t_hin], txb)
            P.dma("act", yb[:], yT_d[:, :, s * ST:(s + 1) * ST], [t_yT], tyb)
            for j in range(4):
                tl.run(lambda k, j=j, yb=yb: yb[:, k, j * 128:(j + 1) * 128], 8,
                       lambda k, hf: wo[:, k, hf * 512:(hf + 1) * 512], [tyb], t_wo,
                       xb[:, j, :], txb, h_out_v[s, j], t_hout)
        P.barrier()
    return t_hout


import numpy as np
import ml_dtypes
BF = ml_dtypes.bfloat16
BIGV = 1.0e9

def consts(S):
    NCM = S // 16
    c = {}
    x = np.arange(8192)
    c["Ebig"] = (np.arange(128)[:, None] == (x // 64)[None, :]).astype(np.float32).astype(BF)
    k = np.arange(128, dtype=np.float32)
    c["Akq"] = (k[:, None] - k[None, :]).astype(np.float32)
    c["Ac"] = (16 * k[:, None] + 31 - k[None, :]).astype(np.float32)
    c["identf"] = np.eye(128, dtype=np.float32)
    c["iota16"] = np.tile((16.0 * np.arange(NCM, dtype=np.float32))[None, :], (128, 1))
    c["ident"] = np.eye(128, dtype=np.float32).astype(BF)
    inv = (10000.0 ** (-np.arange(0, 64, 2, dtype=np.float32) / 64)).astype(np.float32)
    c["invtab"] = np.tile(inv[None, :], (128, 1)).astype(np.float32)
    return c

def tables(qd, S, TC):
    TCT, NSEL, NCM = TC // 128, S // 64, S // 16
    NCC = max(NCM // 128, 1)
    t = {}
    q = np.arange(128)
    cand = np.zeros((TCT, 128, NSEL), np.float32)
    forced = np.zeros((TCT, 128, NSEL), np.float32)
    j = np.arange(NSEL)
    thr_c = np.zeros((TCT, NCC), np.float32); thr_d = np.zeros((TCT, 4), np.float32); thr_w = np.zeros((TCT, 5), np.float32)
    thrq = np.zeros((128, TCT), np.float32)
    for m in range(TCT):
        i = TCT * qd + m
        cur = (128 * i + q) // 64
        valid = j[None, :] <= cur[:, None]
        f = (j[None, :] == 0) | (j[None, :] == cur[:, None]) | (j[None, :] == cur[:, None] - 1)
        f = f & valid
        cand[m] = (valid & ~f).astype(np.float32)
        forced[m] = f.astype(np.float32) * BIGV
        for c in range(NCC):
            thr_c[m, c] = 128.0 * (i - 16 * c)
        for r in range(4):
            thr_d[m, r] = 0.0 if r == qd else BIGV
        for c in range(5):
            ok = (i - 4 + c) >= 0
            if c == 0:
                thr_w[m, c] = 0.0 if ok else BIGV
            elif c == 4:
                thr_w[m, c] = 0.0
            else:
                thr_w[m, c] = BIGV if ok else -BIGV
        thrq[:, m] = 128.0 * i + q - 31.0
    row = np.concatenate([thr_c.ravel(), thr_d.ravel(), thr_w.ravel()]).astype(np.float32)
    t["thr"] = np.tile(row[None, :], (128, 1))
    t["thrq"] = thrq
    t["cand"] = cand
    t["forced"] = forced
    fix = np.ones((128, 2, 16), np.float32)
    if qd == 0:
        for cc in range(2):
            for hh in range(2):
                w = (2, 4, 8, 16)[2 * cc + hh]
                fix[hh * 64:(hh + 1) * 64, cc, :] = w / np.minimum(np.arange(16) + 1, w)
    t["poolfix"] = fix
    NLC = S // 512
    flag = np.zeros((128, NLC), np.float32)
    flag[:, NLC - ((qd + 1) * TC) // 512:] = 1.0
    t["lruflag"] = flag
    return t

def chT(a):
    return np.ascontiguousarray(a.T.reshape(2, 128, -1).transpose(1, 0, 2))

def layout_B(qd, S, TC, f32p, qkv, gates):
    t0 = qd * TC
    TCT, NCH = TC // 128, S // 128
    d = {}
    xp, xl, gl = f32p[:, 0:256], f32p[:, 256:512], f32p[:, 512:768]
    d["xpT"] = chT(np.concatenate([np.zeros((16, 256), np.float32), xp])[t0:t0 + 16 + TC])
    d["xlT"] = chT(np.concatenate([np.zeros((4 + S, 256), np.float32), xl])[t0 + TC:t0 + TC + 4 + S])
    d["glT"] = chT(gl[t0:t0 + TC])
    q = qkv[:, 0:512]
    kc, vc, ks, vs, kw, vw = (qkv[:, 512 + 128 * i:640 + 128 * i] for i in range(6))
    d["QT"] = np.ascontiguousarray(q[t0:t0 + TC].reshape(TC, 2, 4, 64).transpose(1, 3, 2, 0).reshape(128, 4, TC))
    d["KsT"] = np.ascontiguousarray(ks.T)
    d["Vs"] = np.ascontiguousarray(vs.reshape(NCH, 128, 128).transpose(1, 0, 2))
    d["KcT"] = np.ascontiguousarray(kc.T)
    d["VcT"] = np.ascontiguousarray(vc.T)
    z = np.zeros((512, 128), kw.dtype)
    d["KwT"] = np.ascontiguousarray(np.concatenate([z, kw])[t0:t0 + 512 + TC].T)
    d["Vw"] = np.ascontiguousarray(np.concatenate([z, vw])[t0:t0 + 512 + TC].reshape(4 + TCT, 128, 128).transpose(1, 0, 2))
    d["gate"] = np.ascontiguousarray(gates[t0:t0 + TC].reshape(TCT, 128, 24).transpose(1, 0, 2))
    return d


from concourse.bass_utils import run_bass_kernel_spmd

_MK_S = 16384
_PROGS = {}


def _build(kind, S, TC):
    TCT, NCH, NSEL, NCM = TC // 128, S // 128, S // 64, S // 16
    NCC = max(NCM // 128, 1)
    NTHR = TCT * NCC + TCT * 4 + TCT * 5
    nc = bass.Bass("TRN2", target_bir_lowering=False)
    dt = lambda n, s, d, k="ExternalInput": nc.dram_tensor(n, s, d, kind=k).ap()
    P = Prog(nc)
    fin = []
    with ExitStack() as es:
        cx = Ctx(nc, P, es)
        ident = dt("ident", [128, 128], BF16)
        load_consts(cx, es, ident)
        has_B = kind in ("BA", "B")
        has_A = kind in ("A", "BA")
        t_h = P.tok("h0")
        if has_B:
            a = dict(QT=dt("QT", [128, 4, TC], BF16), KsT=dt("KsT", [128, S], BF16), Vs=dt("Vs", [128, NCH, 128], BF16),
                     KcT=dt("KcT", [128, S], BF16), VcT=dt("VcT", [128, S], BF16), KwT=dt("KwT", [128, 512 + TC], BF16),
                     Vw=dt("Vw", [128, 4 + TCT, 128], BF16), gate=dt("gate", [128, TCT, 24], F32),
                     Ebig=dt("Ebig", [128, 8192], BF16), Akq=dt("Akq", [128, 128], F32), Ac=dt("Ac", [128, 128], F32),
                     identf=dt("identf", [128, 128], F32), iota16=dt("iota16", [128, NCM], F32),
                     thr=dt("thr", [128, NTHR], F32), thrq=dt("thrq", [128, TCT], F32),
                     cand=dt("cand", [TCT, 128, NSEL], F32), forced=dt("forced", [TCT, 128, NSEL], F32),
                     cmp_w_k=dt("cmp_w_k", [32, 64, 64], F32), cmp_w_v=dt("cmp_w_v", [32, 64, 64], F32),
                     cmp_pe=dt("cmp_pe", [32, 64], F32))
            xpT = dt("xpT", [128, 2, 16 + TC], F32); poolfix = dt("poolfix", [128, 2, 16], F32)
            pool_w = dt("pool_w", [4, 64, 64], F32); pool_scale = dt("pool_scale", [256], F32)
            xlT = dt("xlT", [128, 2, 4 + S], F32); glT = dt("glT", [128, 2, TC], F32)
            lruflag = dt("lruflag", [128, S // 512], F32)
            conv_w = dt("conv_w", [4, 256], F32); conv_b = dt("conv_b", [256], F32)
            w_r = dt("lru_w_r", [4, 64, 64], F32); b_r = dt("lru_b_r", [256], F32)
            w_i = dt("lru_w_i", [4, 64, 64], F32); b_i = dt("lru_b_i", [256], F32); lam = dt("lru_lambda", [256], F32)
            w_out = dt("w_out", [D, D], F32); mix_post_g = dt("mix_post_g", [D], F32)
            h1 = dt("h1_in", [TC, D], F32)
            f2 = dict(wg=dt("f2_wg", [D, DFF], F32), wu=dt("f2_wu", [D, DFF], F32), wd=dt("f2_wd", [DFF, D], F32),
                      gpre=dt("f2_gpre", [D], F32), gpost=dt("f2_gpost", [D], F32))
            yT = dt("yT_s", [128, 8, TC], BF16, "Internal")
            h_mid = dt("h_mid_s", [TC, D], F32, "Internal")
            t_yT = P.tok("yT")
            emit_pool(cx, xpT, poolfix, pool_w, pool_scale, yT, t_yT, TC)
            emit_lru(cx, xlT, glT, lruflag, conv_w, conv_b, w_r, b_r, w_i, b_i, lam, yT, t_yT, S, TC)
            emit_attn(cx, a, yT, t_yT, S, TC)
            t_hm = emit_wout(cx, yT, t_yT, w_out, mix_post_g, h1, P.tok("h1in"), h_mid, TC)
            if has_A:
                h2 = dt("h2_s", [TC, D], F32, "Internal")
            else:
                h2 = dt("h_out", [TC, D], F32, "ExternalOutput")
            t_h = emit_ffn(cx, h_mid, t_hm, h2, f2["wg"], f2["wu"], f2["wd"], f2["gpre"], f2["gpost"], TC, "f2")
            h_cur = h2
            if not has_A:
                fin.append(t_h)
        else:
            h_cur = dt("h_in", [TC, D], F32)
        if has_A:
            f1 = dict(wg=dt("f1_wg", [D, DFF], F32), wu=dt("f1_wu", [D, DFF], F32), wd=dt("f1_wd", [DFF, D], F32),
                      gpre=dt("f1_gpre", [D], F32), gpost=dt("f1_gpost", [D], F32))
            w_in = dt("w_in", [D, IN_COLS], F32); mix_pre_g = dt("mix_pre_g", [D], F32)
            pos = dt("pos", [TC], I32); invtab = dt("invtab", [128, 32], F32)
            h1o = dt("h1_out", [TC, D], F32, "ExternalOutput")
            o_f32 = dt("o_f32", [TC, 768], F32, "ExternalOutput")
            o_qkv = dt("o_qkv", [TC, 1280], BF16, "ExternalOutput")
            o_gate = dt("o_gate", [TC, 24], F32, "ExternalOutput")
            t_h1 = emit_ffn(cx, h_cur, t_h, h1o, f1["wg"], f1["wu"], f1["wd"], f1["gpre"], f1["gpost"], TC, "f1")
            t_o = emit_proj(cx, h1o, t_h1, w_in, mix_pre_g, pos, invtab, o_f32, o_qkv, o_gate, TC, "pj")
            fin += [t_h1] + list(t_o)
        P.emit(fin)
    return nc


def _get_prog(kind, S, TC):
    key = (kind, S, TC)
    if key not in _PROGS:
        _PROGS[key] = _build(kind, S, TC)
    return _PROGS[key]


_RUN = [None]


def _run(nc, in_maps):
    if _RUN[0] is not None:
        return _RUN[0](nc, in_maps)
    return run_bass_kernel_spmd(nc, in_maps, core_ids=list(range(len(in_maps)))).results


def kernel(**inp):
    x = np.asarray(inp["x"])
    NBt, S, _ = x.shape
    NQ = 4
    TC = S // NQ
    ncores = NBt * NQ
    cs = consts(S)
    g = lambda k, l: np.ascontiguousarray(np.asarray(inp[k])[l])
    positions = np.asarray(inp["positions"]).astype(np.int32)

    def a_inputs(l):
        return dict(f1_wg=g("ffn1_w_gate", l), f1_wu=g("ffn1_w_up", l), f1_wd=g("ffn1_w_down", l),
                    f1_gpre=g("ffn1_pre_g", l), f1_gpost=g("ffn1_post_g", l), w_in=g("w_in", l),
                    mix_pre_g=g("mix_pre_g", l), invtab=cs["invtab"])

    def b_inputs(l):
        d = dict(f2_wg=g("ffn2_w_gate", l), f2_wu=g("ffn2_w_up", l), f2_wd=g("ffn2_w_down", l),
                 f2_gpre=g("ffn2_pre_g", l), f2_gpost=g("ffn2_post_g", l), w_out=g("w_out", l),
                 mix_post_g=g("mix_post_g", l))
        for k in ("pool_w", "pool_scale", "conv_w", "conv_b", "lru_w_r", "lru_b_r", "lru_w_i", "lru_b_i",
                  "lru_lambda", "cmp_w_k", "cmp_w_v", "cmp_pe"):
            d[k] = g(k, l)
        for k in ("Ebig", "Akq", "Ac", "identf", "iota16"):
            d[k] = cs[k]
        return d

    tabs = [tables(qd, S, TC) for qd in range(NQ)]

    def gather(res, key, width, dtype):
        out = np.zeros((NBt, S, width), dtype)
        for c in range(ncores):
            b, qd = divmod(c, NQ)
            out[b, qd * TC:(qd + 1) * TC] = np.asarray(res[c][key]).reshape(TC, width)
        return out

    base = a_inputs(0)
    in_maps = []
    for c in range(ncores):
        b, qd = divmod(c, NQ)
        m = dict(base)
        m["ident"] = cs["ident"]
        m["h_in"] = np.ascontiguousarray(x[b, qd * TC:(qd + 1) * TC])
        m["pos"] = np.ascontiguousarray(positions[b, qd * TC:(qd + 1) * TC])
        in_maps.append(m)
    res = _run(_get_prog("A", S, TC), in_maps)
    DEPTH = np.asarray(inp["w_in"]).shape[0]
    for l in range(DEPTH):
        f32p = gather(res, "o_f32", 768, np.float32)
        qkv = gather(res, "o_qkv", 1280, BF)
        gates = gather(res, "o_gate", 24, np.float32)
        last = (l == DEPTH - 1)
        base = b_inputs(l)
        if not last:
            base.update(a_inputs(l + 1))
        in_maps = []
        for c in range(ncores):
            b, qd = divmod(c, NQ)
            m = dict(base)
            m["ident"] = cs["ident"]
            m.update(layout_B(qd, S, TC, f32p[b], qkv[b], gates[b]))
            for k in ("thr", "thrq", "cand", "forced", "poolfix", "lruflag"):
                m[k] = tabs[qd][k]
            m["h1_in"] = np.asarray(res[c]["h1_out"]).reshape(TC, D)
            if not last:
                m["pos"] = np.ascontiguousarray(positions[b, qd * TC:(qd + 1) * TC])
            in_maps.append(m)
        res = _run(_get_prog("B" if last else "BA", S, TC), in_maps)
    out = gather(res, "h_out", D, np.float32)
    return out
```
